# Optimizing a Trainium2 kernel written in Bass

```python
import math
import jax, jax.numpy as jnp
from jax import lax
import numpy as np


D_MODEL = 1024
BATCH = 16
SEQ = 4096
DEPTH = 2
DEC_BATCH = 8
DEC_SEQ = 2048
PAST_LEN = 128

N_MIXERS = 2
N_ATTN_LAYERS = (DEPTH + 1) // 2
N_FNET_LAYERS = DEPTH // 2
DA_HEADS = 8
DA_HEAD_DIM = D_MODEL // DA_HEADS // 2
ROPE_THETA = 10000.0
Q_BLOCK = 128
FNET_GROUPS = 4
FNET_GROUP_DIM = D_MODEL // FNET_GROUPS
MEM_TOKENS = 256
XA_HEADS = 4
XA_HEAD_DIM = D_MODEL // XA_HEADS
D_FF = 2816
CONV_WIDTH = 3
NORM_EPS = 1e-6
SUBLN_EPS = 1e-5

kernel_name = "hybrid_diffattn_fnet_encoder"


def rmsnorm(x, g, eps=NORM_EPS):
    xf = x.astype(jnp.float32)
    y = xf * lax.rsqrt(jnp.mean(xf * xf, axis=-1, keepdims=True) + eps)
    return (y * g.astype(jnp.float32)).astype(x.dtype)


def rope_tables(S):
    half = DA_HEAD_DIM // 2
    inv_freq = ROPE_THETA ** (-jnp.arange(0, half, dtype=jnp.float32) * 2.0 / DA_HEAD_DIM)
    ang = jnp.arange(S, dtype=jnp.float32)[:, None] * inv_freq[None, :]
    return jnp.cos(ang), jnp.sin(ang)


def apply_rope(x, cos, sin):
    half = DA_HEAD_DIM // 2
    xf = x.astype(jnp.float32)
    x1, x2 = xf[..., :half], xf[..., half:]
    c = cos[:, None, None, :]
    s = sin[:, None, None, :]
    out = jnp.concatenate([x1 * c - x2 * s, x2 * c + x1 * s], axis=-1)
    return out.astype(x.dtype)


def diff_attention(xn, w_qkv, lam_q1, lam_k1, lam_q2, lam_k2, subln_g, w_o, lambda_init, cos, sin):
    B, S, _ = xn.shape
    qkv = xn @ w_qkv
    q, k, v = jnp.split(qkv, 3, axis=-1)
    q = apply_rope(q.reshape(B, S, DA_HEADS, 2, DA_HEAD_DIM), cos, sin) * (DA_HEAD_DIM ** -0.5)
    k = apply_rope(k.reshape(B, S, DA_HEADS, 2, DA_HEAD_DIM), cos, sin)
    v = v.reshape(B, S, DA_HEADS, 2 * DA_HEAD_DIM)
    f32 = jnp.float32
    lam = (jnp.exp(jnp.sum(lam_q1.astype(f32) * lam_k1.astype(f32)))
           - jnp.exp(jnp.sum(lam_q2.astype(f32) * lam_k2.astype(f32)))
           + lambda_init)
    nb = S // Q_BLOCK
    qb = q.reshape(B, nb, Q_BLOCK, DA_HEADS, 2, DA_HEAD_DIM).transpose(1, 0, 2, 3, 4, 5)

    def block(qblk):
        s = jnp.einsum("bqhcd,bkhcd->bchqk", qblk, k).astype(f32)
        p = jax.nn.softmax(s, axis=-1)
        a = p[:, 0] - lam * p[:, 1]
        return jnp.einsum("bhqk,bkhe->bqhe", a.astype(v.dtype), v)

    o = lax.map(block, qb)
    o = o.transpose(1, 0, 2, 3, 4).reshape(B, S, DA_HEADS, 2 * DA_HEAD_DIM)
    o = rmsnorm(o, subln_g, SUBLN_EPS) * (1.0 - lambda_init)
    return o.reshape(B, S, D_MODEL) @ w_o


def fourier_mix(xn, w_in, w_out):
    B, S, _ = xn.shape
    u = (xn @ w_in).reshape(B, S, FNET_GROUPS, FNET_GROUP_DIM)
    f = jnp.fft.fft2(u.astype(jnp.float32), axes=(1, 3), norm="ortho").real
    return f.astype(xn.dtype).reshape(B, S, D_MODEL) @ w_out


def cross_attention(xn, memn, w_q, w_kv, w_o):
    B, S, _ = xn.shape
    M = memn.shape[1]
    q = (xn @ w_q).reshape(B, S, XA_HEADS, XA_HEAD_DIM) * (XA_HEAD_DIM ** -0.5)
    kv = (memn @ w_kv).reshape(B, M, 2, XA_HEADS, XA_HEAD_DIM)
    k, v = kv[:, :, 0], kv[:, :, 1]
    s = jnp.einsum("bqhd,bmhd->bhqm", q, k).astype(jnp.float32)
    p = jax.nn.softmax(s, axis=-1).astype(v.dtype)
    o = jnp.einsum("bhqm,bmhd->bqhd", p, v).reshape(B, S, D_MODEL)
    return o @ w_o


def dwconv3(h, w, b):
    hp = jnp.pad(h, ((0, 0), (1, 1), (0, 0)))
    return hp[:, :-2] * w[0] + hp[:, 1:-1] * w[1] + hp[:, 2:] * w[2] + b


def conv_glu(xn, w_up, conv_w, conv_b, w_down):
    gate, val = jnp.split(xn @ w_up, 2, axis=-1)
    gate = dwconv3(gate, conv_w, conv_b)
    return (jax.nn.gelu(gate) * val) @ w_down


def trunk(x, mem, norm_mix_g, norm_xattn_g, norm_mem_g, norm_ffn_g, final_norm_g,
          attn_w_qkv, attn_lambda_q1, attn_lambda_k1, attn_lambda_q2, attn_lambda_k2,
          attn_subln_g, attn_w_o, fnet_w_in, fnet_w_out,
          xattn_w_q, xattn_w_kv, xattn_w_o,
          ffn_w_up, ffn_conv_w, ffn_conv_b, ffn_w_down):
    S = x.shape[1]
    cos, sin = rope_tables(S)
    for i in range(DEPTH):
        h = rmsnorm(x, norm_mix_g[i])
        j = i // N_MIXERS
        if i % N_MIXERS == 0:
            lambda_init = 0.8 - 0.6 * math.exp(-0.3 * i)
            x = x + diff_attention(h, attn_w_qkv[j], attn_lambda_q1[j], attn_lambda_k1[j],
                                   attn_lambda_q2[j], attn_lambda_k2[j], attn_subln_g[j],
                                   attn_w_o[j], lambda_init, cos, sin)
        else:
            x = x + fourier_mix(h, fnet_w_in[j], fnet_w_out[j])
        x = x + cross_attention(rmsnorm(x, norm_xattn_g[i]), rmsnorm(mem, norm_mem_g[i]),
                                xattn_w_q[i], xattn_w_kv[i], xattn_w_o[i])
        x = x + conv_glu(rmsnorm(x, norm_ffn_g[i]), ffn_w_up[i], ffn_conv_w[i],
                         ffn_conv_b[i], ffn_w_down[i])
    return rmsnorm(x, final_norm_g)


def setup_inputs(seed: int = 0) -> dict:
    key = jax.random.key(seed)
    ks = jax.random.split(key, 28)
    D, F = D_MODEL, D_FF

    def w(k, shape, fan_in):
        return jax.random.normal(k, shape, jnp.float32) * (fan_in ** -0.5)

    def gain(k, shape):
        return 1.0 + 0.05 * jax.random.normal(k, shape, jnp.float32)

    return {
        "x_prompt": jax.random.normal(ks[0], (BATCH, SEQ, D), jnp.float32),
        "x_sample": jax.random.normal(ks[1], (DEC_BATCH, DEC_SEQ, D), jnp.float32),
        "mem_prompt": jax.random.normal(ks[2], (BATCH, MEM_TOKENS, D), jnp.float32),
        "mem_sample": jax.random.normal(ks[3], (DEC_BATCH, MEM_TOKENS, D), jnp.float32),
        "norm_mix_g": gain(ks[4], (DEPTH, D)),
        "norm_xattn_g": gain(ks[5], (DEPTH, D)),
        "norm_mem_g": gain(ks[6], (DEPTH, D)),
        "norm_ffn_g": gain(ks[7], (DEPTH, D)),
        "final_norm_g": gain(ks[8], (D,)),
        "attn_w_qkv": w(ks[9], (N_ATTN_LAYERS, D, 3 * D), D),
        "attn_lambda_q1": 0.1 * jax.random.normal(ks[10], (N_ATTN_LAYERS, DA_HEAD_DIM), jnp.float32),
        "attn_lambda_k1": 0.1 * jax.random.normal(ks[11], (N_ATTN_LAYERS, DA_HEAD_DIM), jnp.float32),
        "attn_lambda_q2": 0.1 * jax.random.normal(ks[12], (N_ATTN_LAYERS, DA_HEAD_DIM), jnp.float32),
        "attn_lambda_k2": 0.1 * jax.random.normal(ks[13], (N_ATTN_LAYERS, DA_HEAD_DIM), jnp.float32),
        "attn_subln_g": gain(ks[14], (N_ATTN_LAYERS, 2 * DA_HEAD_DIM)),
        "attn_w_o": w(ks[15], (N_ATTN_LAYERS, D, D), D),
        "fnet_w_in": w(ks[16], (N_FNET_LAYERS, D, D), D),
        "fnet_w_out": w(ks[17], (N_FNET_LAYERS, D, D), D),
        "xattn_w_q": w(ks[18], (DEPTH, D, D), D),
        "xattn_w_kv": w(ks[19], (DEPTH, D, 2 * D), D),
        "xattn_w_o": w(ks[20], (DEPTH, D, D), D),
        "ffn_w_up": w(ks[21], (DEPTH, D, 2 * F), D),
        "ffn_conv_w": w(ks[22], (DEPTH, CONV_WIDTH, F), CONV_WIDTH),
        "ffn_conv_b": 0.02 * jax.random.normal(ks[23], (DEPTH, F), jnp.float32),
        "ffn_w_down": w(ks[24], (DEPTH, F, D), F),
    }


def reference(x_prompt, x_sample, mem_prompt, mem_sample,
              norm_mix_g, norm_xattn_g, norm_mem_g, norm_ffn_g, final_norm_g,
              attn_w_qkv, attn_lambda_q1, attn_lambda_k1, attn_lambda_q2, attn_lambda_k2,
              attn_subln_g, attn_w_o, fnet_w_in, fnet_w_out,
              xattn_w_q, xattn_w_kv, xattn_w_o,
              ffn_w_up, ffn_conv_w, ffn_conv_b, ffn_w_down):
    weights = (norm_mix_g, norm_xattn_g, norm_mem_g, norm_ffn_g, final_norm_g,
               attn_w_qkv, attn_lambda_q1, attn_lambda_k1, attn_lambda_q2, attn_lambda_k2,
               attn_subln_g, attn_w_o, fnet_w_in, fnet_w_out,
               xattn_w_q, xattn_w_kv, xattn_w_o,
               ffn_w_up, ffn_conv_w, ffn_conv_b, ffn_w_down)
    y_prompt = trunk(x_prompt, mem_prompt, *weights)
    y_sample = trunk(x_sample, mem_sample, *weights)
    return (y_prompt, y_sample)
```

```python
import math
import os
from contextlib import ExitStack
SKIP = os.environ.get('KSKIP', '').split(',')

import numpy as np
import ml_dtypes

import concourse.bass as bass
import concourse.mybir as mybir
from concourse.bass_utils import run_bass_kernel_spmd

F32 = mybir.dt.float32
BF16 = mybir.dt.bfloat16
U8 = mybir.dt.uint8
AF = mybir.ActivationFunctionType
ALU = mybir.AluOpType
AX = mybir.AxisListType
NPBF = ml_dtypes.bfloat16

ENGS = ("pe", "act", "dve", "pool", "sp")
D = 1024
FF = 2816
NFC = 22
NH = 8
MEMT = 256
EPS = 1e-6
SUBEPS = 1e-5
LAMBDA_INIT = 0.8 - 0.6 * math.exp(-0.3 * 0)
ARENA = 196608 - 2048


def _size(dt):
    return 4 if dt == F32 else (2 if dt == BF16 else 1)


class Ring:
    def __init__(self, items):
        self.items = items
        self.i = 0

    def next(self):
        it = self.items[self.i % len(self.items)]
        self.i += 1
        return it


class Prog:
    def __init__(self, nc):
        self.nc = nc
        self.es = ExitStack()
        self.streams = {e: [] for e in ENGS}
        self.waited = {e: {} for e in ENGS}
        self.res = {}
        self.dma_cnt = {}
        self.dma_sems = {}
        self.multi = set()
        self.eng_sems = {}
        self.arena = self.es.enter_context(nc.sbuf_tensor("arena", [128, ARENA], U8))
        self.ptr = 0
        self.mark = 0
        self.banks = [self.es.enter_context(nc.psum_tensor("bank%d" % i, [128, 512], F32)) for i in range(8)]

    def alloc(self, name, shape, dt):
        n = 1
        for s in shape:
            n *= s
        nb = n * _size(dt)
        off = (self.ptr + 63) // 64 * 64
        assert off + nb <= ARENA, ("SBUF arena overflow", name, off + nb)
        self.ptr = off + nb
        ap = self.arena[:, off:off + nb]
        if dt != U8:
            ap = ap.bitcast(dt)
        if len(shape) == 2:
            ap = ap.rearrange("p (a b) -> p a b", b=shape[1])
        elif len(shape) == 3:
            ap = ap.rearrange("p (a b c) -> p a b c", b=shape[1], c=shape[2])
        return ap

    def set_mark(self):
        self.mark = self.ptr

    def reset(self):
        self.ptr = self.mark

    def _deps(self, eng, is_dma, reads, writes):
        toks = {}

        def need(d):
            for sk, v in d.items():
                if (not is_dma) and sk == ("E", eng) and eng == "pe":
                    continue
                if toks.get(sk, -1) < v:
                    toks[sk] = v

        for k in reads:
            st = self.res.get(k)
            if st:
                need(st["w"])
                if isinstance(k, tuple) and k[0] == "b":
                    for sk, v in st["r"].items():
                        if sk != ("E", eng) and toks.get(sk, -1) < v:
                            toks[sk] = v
        for k in writes:
            st = self.res.get(k)
            if st:
                need(st["r"])
                if k not in self.multi:
                    need(st["w"])
        out = []
        wd = self.waited[eng]
        for sk, v in toks.items():
            if wd.get(sk, -1) >= v:
                continue
            wd[sk] = v
            out.append((sk, v))
        return out

    def _commit(self, tok, reads, writes):
        sk, v = tok
        for k in reads:
            st = self.res.setdefault(k, {"w": {}, "r": {}})
            if st["r"].get(sk, -1) < v:
                st["r"][sk] = v
        for k in writes:
            st = self.res.setdefault(k, {"w": {}, "r": {}})
            if k in self.multi and not st["r"]:
                if st["w"].get(sk, -1) < v:
                    st["w"][sk] = v
            else:
                st["w"] = {sk: v}
                st["r"] = {}

    def _flag(self, waits):
        for sk, v in waits:
            if sk[0] == "E":
                self.streams[sk[1]][v]["flag"] = True

    def op(self, eng, fn, r=(), w=()):
        waits = self._deps(eng, False, r, w)
        idx = len(self.streams[eng])
        self.streams[eng].append({"fn": fn, "waits": waits, "flag": False, "dma": None})
        self._commit((("E", eng), idx), r, w)
        self._flag(waits)

    def mm(self, out, lhsT, rhs, start, stop, r, w):
        self.op("pe", lambda e: e.matmul(out, lhsT, rhs, start=start, stop=stop), r=r, w=[w])

    def dma(self, q, out, in_, sem, r=(), w=(), slow=False):
        waits = self._deps(q, True, r, w)
        c = self.dma_cnt.get(sem, 0) + 1
        self.dma_cnt[sem] = c
        self.streams[q].append({"fn": (out, in_, slow), "waits": waits, "flag": False, "dma": sem})
        self._commit((("D", sem), c * 16), r, w)
        self._flag(waits)

    def barrier(self):
        toks = []
        for e in ENGS:
            last = None
            for i in range(len(self.streams[e]) - 1, -1, -1):
                o = self.streams[e][i]
                if o["dma"] is None and o["fn"] is not None:
                    last = i
                    break
            if last is not None:
                toks.append((("E", e), last))
        for sem, c in self.dma_cnt.items():
            toks.append((("D", sem), c * 16))
        for e in ENGS:
            waits = []
            wd = self.waited[e]
            for sk, v in toks:
                if sk == ("E", e):
                    continue
                if wd.get(sk, -1) >= v:
                    continue
                wd[sk] = v
                waits.append((sk, v))
            self.streams[e].append({"fn": None, "waits": waits, "flag": False, "dma": None})
            self._flag(waits)

    def final_wait(self, q):
        waits = [(("D", sem), c * 16) for sem, c in self.dma_cnt.items()]
        self.streams[q].append({"fn": None, "waits": waits, "flag": False, "dma": None})

    def emit(self):
        nc = self.nc
        es = self.es
        for e in ENGS:
            self.eng_sems[e] = es.enter_context(nc.semaphore("sem_" + e))
        for s in self.dma_cnt:
            self.dma_sems[s] = es.enter_context(nc.semaphore("dsem_" + s))
        vals = {}
        for e in ENGS:
            c = 0
            v = []
            for o in self.streams[e]:
                if o["flag"]:
                    c += 1
                v.append(c)
            vals[e] = v
        self.stats = {e: (len(self.streams[e]), vals[e][-1] if vals[e] else 0) for e in ENGS}
        block = es.enter_context(nc.Block())

        def run(engname, eobj):
            sem_me = self.eng_sems[engname]
            for o in self.streams[engname]:
                for sk, v in o["waits"]:
                    if sk[0] == "E":
                        eobj.wait_ge(self.eng_sems[sk[1]], vals[sk[1]][v])
                    else:
                        eobj.wait_ge(self.dma_sems[sk[1]], v)
                if o["fn"] is None:
                    continue
                if o["dma"] is not None:
                    out, in_, slow = o["fn"]
                    if slow:
                        ins = eobj.dma_start(out=out, in_=in_, allow_slow_non_contiguous=True)
                    else:
                        ins = eobj.dma_start(out=out, in_=in_)
                    ins.then_inc(self.dma_sems[o["dma"]], 16)
                else:
                    ins = o["fn"](eobj)
                    if o["flag"]:
                        ins.then_inc(sem_me, 1)

        @block.tensor
        def _(e):
            run("pe", e)

        @block.scalar
        def _(e):
            run("act", e)

        @block.vector
        def _(e):
            run("dve", e)

        @block.gpsimd
        def _(e):
            run("pool", e)

        @block.sync
        def _(e):
            run("sp", e)


WNAMES = [
    ("norm_mix_g", (2, D)), ("norm_xattn_g", (2, D)), ("norm_mem_g", (2, D)), ("norm_ffn_g", (2, D)),
    ("final_norm_g", (1, D)),
    ("attn_w_qkv", (1, D, 3 * D)), ("attn_lambda_q1", (1, 64)), ("attn_lambda_k1", (1, 64)),
    ("attn_lambda_q2", (1, 64)), ("attn_lambda_k2", (1, 64)), ("attn_subln_g", (1, 128)),
    ("attn_w_o", (1, D, D)), ("fnet_w_in", (1, D, D)), ("fnet_w_out", (1, D, D)),
    ("xattn_w_q", (2, D, D)), ("xattn_w_kv", (2, D, 2 * D)), ("xattn_w_o", (2, D, D)),
    ("ffn_w_up", (2, D, 2 * FF)), ("ffn_conv_w", (2, 3, FF)), ("ffn_conv_b", (2, FF)),
    ("ffn_w_down", (2, FF, D)),
]


def build(seqs, TAB, dbg=False, stop_after=None):
    nseq = len(seqs)
    NT = sum(seqs)
    offs = [sum(seqs[:i]) for i in range(nseq)]
    Smax = max(seqs)
    svals = sorted(set(seqs))
    nc = bass.Bass("TRN2", target_bir_lowering=False)

    def din(name, shape, dt=F32):
        return nc.dram_tensor(name, list(shape), dt, kind="ExternalInput").ap()

    def dscr(name, shape, dt):
        if dbg:
            return nc.dram_tensor(name, list(shape), dt, kind="ExternalOutput").ap()
        return nc.dram_tensor(name, list(shape), dt).ap()

    x = din("x", [NT, D])
    mem = din("mem", [nseq * MEMT, D])
    W = {n: din(n, s) for n, s in WNAMES}
    c_ident = din("c_ident", [128, 128], BF16)
    c_ones = din("c_ones", [128, 128], BF16)
    c_rot = din("c_rot", [128, 128], BF16)
    c_cos = din("c_cos", [128, TAB])
    c_sin = din("c_sin", [128, TAB])
    c_cc = {S: din("c_cc%d" % S, [256, 256], BF16) for S in svals}
    c_sc = {S: din("c_sc%d" % S, [256, 256], BF16) for S in svals}
    c_CS = din("c_CS", [TAB, TAB], BF16)
    c_NS = din("c_NS", [TAB, TAB], BF16)
    y = nc.dram_tensor("y", [NT, D], F32, kind="ExternalOutput").ap()

    wqkv_b = dscr("wqkv_b", [128, 8, 3 * D], BF16)
    wo_b = dscr("wo_b", [128, 8, D], BF16)
    fin_b = dscr("fin_b", [128, 8, D], BF16)
    fout_b = dscr("fout_b", [128, 8, D], BF16)
    xq_b = [dscr("xq_b%d" % i, [128, 8, D], BF16) for i in range(2)]
    xo_b = [dscr("xo_b%d" % i, [128, 8, D], BF16) for i in range(2)]
    xkv_b = [dscr("xkv_b%d" % i, [128, 8, 2 * D], BF16) for i in range(2)]
    wup_b = [dscr("wup_b%d" % i, [22, 128, 8, 256], BF16) for i in range(2)]
    wdn_b = [dscr("wdn_b%d" % i, [128, NFC, D], BF16) for i in range(2)]
    xa = dscr("xa", [Smax, D], F32)
    xb = dscr("xb", [Smax, D], F32)
    hTa = dscr("hTa", [D, Smax + 2], BF16)
    hTb = dscr("hTb", [D, Smax + 2], BF16)
    qT = dscr("qT", [D, Smax], BF16)
    kT = dscr("kT", [D, Smax], BF16)
    v2 = dscr("v2", [NH, 128, Smax // 128, 128], BF16)
    Ad = dscr("Ad", [Smax, D], BF16)
    Bd = dscr("Bd", [Smax, D], BF16)
    fTd = dscr("fTd", [D, Smax], BF16)
    memK = dscr("memK", [2 * nseq, 128, 8, MEMT], BF16)
    memV = dscr("memV", [2 * nseq, 128, 2, D], BF16)

    P = Prog(nc)
    for k in ("xa", "xb", "hTa", "hTb", "qT", "kT", "v2", "Ad", "Bd", "fTd", "memK", "memV", "y", "wscr"):
        P.multi.add(k)
    bank = [b[:] for b in P.banks]

    def bk(i):
        return ("b", i)

    ident = P.alloc("ident", [128], BF16)
    ones = P.alloc("ones", [128], BF16)
    rot = P.alloc("rot", [128], BF16)
    epsD = P.alloc("epsD", [1], F32)
    epsS = P.alloc("epsS", [1], F32)
    nlam = P.alloc("nlam", [1], F32)
    sg = P.alloc("sg", [1], F32)
    zcol = P.alloc("zcol", [8, 1], BF16)
    ones32 = P.alloc("ones32", [128], F32)
    cq = P.alloc("cq", [1], F32)
    ck = P.alloc("ck", [1], F32)
    P.dma("sp", ident, c_ident, "c0", w=["ident"])
    P.dma("sp", ones, c_ones, "c1", w=["ones"])
    P.dma("sp", rot, c_rot, "c2", w=["rot"])
    P.op("pool", lambda e: e.memset(epsD, EPS), w=["epsD"])
    P.op("pool", lambda e: e.memset(epsS, SUBEPS), w=["epsS"])
    P.op("pool", lambda e: e.memset(zcol, 0.0), w=["zcol"])
    P.op("pool", lambda e: e.memset(cq, 0.125), w=["cq"])
    P.op("pool", lambda e: e.memset(ones32, 1.0), w=["ones32"])
    P.op("pool", lambda e: e.memset(ck, 1.0), w=["ck"])
    P.set_mark()

    def phase_lambda():
        P.reset()
        lt = [P.alloc("lt%d" % i, [64], F32) for i in range(4)]
        pr = [P.alloc("pr%d" % i, [64], F32) for i in range(2)]
        sm = [P.alloc("sm%d" % i, [1], F32) for i in range(2)]
        names = ["attn_lambda_q1", "attn_lambda_k1", "attn_lambda_q2", "attn_lambda_k2"]
        for i in range(4):
            P.dma("sp", lt[i], W[names[i]][0:1, :].partition_broadcast(128), "lam%d" % i, w=["lt%d" % i])
        for i in range(2):
            P.op("dve", lambda e, i=i: e.tensor_tensor(out=pr[i], in0=lt[2 * i], in1=lt[2 * i + 1], op=ALU.mult),
                 r=["lt%d" % (2 * i), "lt%d" % (2 * i + 1)], w=["pr%d" % i])
            P.op("dve", lambda e, i=i: e.reduce_sum(sm[i], pr[i], axis=AX.X), r=["pr%d" % i], w=["sm%d" % i])
            P.op("act", lambda e, i=i: e.activation(out=sm[i], in_=sm[i], func=AF.Exp), r=["sm%d" % i], w=["sm%d" % i])
        P.op("dve", lambda e: e.tensor_tensor(out=nlam, in0=sm[1], in1=sm[0], op=ALU.subtract), r=["sm0", "sm1"], w=["nlam"])
        P.op("dve", lambda e: e.tensor_scalar(nlam, nlam, -LAMBDA_INIT, None, op0=ALU.add), r=["nlam"], w=["nlam"])
        P.dma("sp", sg, W["attn_subln_g"][0, :].rearrange("(p k) -> p k", k=1), "lam4", w=["sg"], slow=True)
        P.op("dve", lambda e: e.tensor_scalar(sg, sg, 1.0 - LAMBDA_INIT, None, op0=ALU.mult), r=["sg"], w=["sg"])
        P.barrier()

    def phase_weights():
        P.reset()
        FW = 3072
        NS_ = 3
        s32 = [P.alloc("w32_%d" % i, [FW], F32) for i in range(NS_)]
        s16 = [P.alloc("w16_%d" % i, [FW], BF16) for i in range(NS_)]
        cnt = [0]
        cengs = ["dve", "pool", "act"]

        def piece(src, dst, fw_):
            i = cnt[0] % NS_
            ce = cengs[cnt[0] % 3]
            cnt[0] += 1
            a32 = s32[i][:, 0:fw_]
            a16 = s16[i][:, 0:fw_]
            P.dma("sp", a32, src, "w32_%d" % i, w=["w32_%d" % i])
            if ce == "act":
                P.op("act", lambda e: e.activation(out=a16, in_=a32, func=AF.Copy), r=["w32_%d" % i], w=["w16_%d" % i])
            else:
                P.op(ce, lambda e: e.tensor_copy(a16, a32), r=["w32_%d" % i], w=["w16_%d" % i])
            if len(dst.shape) == 3:
                a16v = a16.rearrange("p (g k) -> p g k", k=dst.shape[2])
            else:
                a16v = a16
            P.dma("pool", dst, a16v, "w16_%d" % i, r=["w16_%d" % i], w=["wscr"])

        def std(src2d, dst, C, Fo):
            for c in range(C):
                for f0 in range(0, Fo, FW):
                    fw_ = min(FW, Fo - f0)
                    piece(src2d[c * 128:(c + 1) * 128, f0:f0 + fw_], dst[:, c, f0:f0 + fw_], fw_)

        std(W["attn_w_qkv"][0], wqkv_b, 8, 3 * D)
        std(W["attn_w_o"][0], wo_b, 8, D)
        for i in range(2):
            std(W["xattn_w_kv"][i], xkv_b[i], 8, 2 * D)
        for i in range(2):
            std(W["xattn_w_q"][i], xq_b[i], 8, D)
            std(W["xattn_w_o"][i], xo_b[i], 8, D)
        for i in range(2):
            for c in range(8):
                for half in range(2):
                    src = W["ffn_w_up"][i][c * 128:(c + 1) * 128, half * FF:(half + 1) * FF]
                    dst = wup_b[i][half * 11:(half + 1) * 11, :, c, :].rearrange("g p k -> p g k")
                    piece(src, dst, FF)
            std(W["ffn_w_down"][i], wdn_b[i], NFC, D)
        std(W["fnet_w_in"][0], fin_b, 8, D)
        std(W["fnet_w_out"][0], fout_b, 8, D)
        P.barrier()

    def load_gB(gB, g_ap_row, sem):
        P.dma("sp", gB, g_ap_row.partition_broadcast(128), sem, w=["gB"])

    def rstd_ops(ss, rstd, n, kss, krs, dim, eps_ap, keps):
        P.op("act", lambda e: e.activation(out=rstd[:, 0:n], in_=ss[:, 0:n], func=AF.Ln, bias=eps_ap, scale=1.0 / dim),
             r=[kss, keps], w=[krs])
        P.op("act", lambda e: e.activation(out=rstd[:, 0:n], in_=rstd[:, 0:n], func=AF.Exp, scale=-0.5), r=[krs], w=[krs])

    def norm_transpose(xs, kxs, rs, krs, gB, h0r, psT_i, dst, kdst, mul_eng="dve"):
        h0, kh0 = h0r.next()
        P.op(mul_eng, lambda e: e.scalar_tensor_tensor(out=h0, in0=xs, scalar=rs, in1=gB, op0=ALU.mult, op1=ALU.mult),
             r=[kxs, krs, "gB"], w=[kh0])
        psT = bank[psT_i].bitcast(BF16)
        for c in range(8):
            P.op("pe", lambda e, c=c: e.transpose(psT[:, c * 128:(c + 1) * 128], h0[:, c * 128:(c + 1) * 128], ident),
                 r=[kh0, "ident"], w=[bk(psT_i)])
        P.op("act", lambda e: e.activation(out=dst, in_=psT.rearrange("p (c t) -> p c t", t=128), func=AF.Copy),
             r=[bk(psT_i)], w=[kdst])

    class EpiBufs:
        def __init__(self, nx=2, nh=2):
            self.xr = Ring([(P.alloc("xt%d" % i, [4, D], F32), "xt%d" % i) for i in range(nx)])
            self.hr = Ring([(P.alloc("hTs%d" % i, [8, 512], BF16), "hTs%d" % i) for i in range(nh)])
            self.h0r = Ring([(P.alloc("h0_%d" % i, [D], BF16), "h0_%d" % i) for i in range(2)])
            self.ss = P.alloc("ss", [4], F32)
            self.rstd = P.alloc("rstd", [4], F32)
            self.ssr = Ring([(P.alloc("ssr%d" % i, [4], F32), "ssr%d" % i) for i in range(2)])
            self.rsr = Ring([(P.alloc("rsr%d" % i, [4], F32), "rsr%d" % i) for i in range(2)])
            self.junk = P.alloc("junk", [D], BF16)
            self.gB = P.alloc("gB", [D], F32)

    def epilogue(E, t0, ps_fn, x_in, x_out, kxo, hT_out, kho, y_out, pairs, psT_i):
        psT_list = list(psT_i) if isinstance(psT_i, (list, tuple)) else [psT_i]
        xt, kx = E.xr.next()
        kxs = [(kx, s) for s in range(4)]
        P.dma("sp", xt, x_in[t0:t0 + 512, :].rearrange("(s p) d -> p s d", p=128), kx, w=kxs)
        ss, kss = E.ssr.next()
        rstd, krs = E.rsr.next()
        P.op("pool", lambda e: e.memset(ss, 0.0), w=[kss])
        for st in range(4):
            bA, bB = ps_fn(st, pairs[st % 2])
            P.op("dve", lambda e, st=st, bA=bA: e.tensor_tensor(out=xt[:, st, 0:512], in0=bank[bA], in1=xt[:, st, 0:512], op=ALU.add),
                 r=[bk(bA), kxs[st]], w=[kxs[st]])
            P.op("dve", lambda e, st=st, bB=bB: e.tensor_tensor(out=xt[:, st, 512:1024], in0=bank[bB], in1=xt[:, st, 512:1024], op=ALU.add),
                 r=[bk(bB), kxs[st]], w=[kxs[st]])
            P.op("act", lambda e, st=st: e.activation(out=E.junk, in_=xt[:, st, :], func=AF.Square, accum_out=ss[:, st:st + 1]),
                 r=[kxs[st]], w=["junk", kss])
        rstd_ops(ss, rstd, 4, kss, krs, D, epsD, "epsD")
        if x_out is not None:
            P.dma("pool", x_out[t0:t0 + 512, :].rearrange("(s p) d -> p s d", p=128), xt, kx + "s", r=kxs, w=[kxo])

        def part2():
            if hT_out is not None:
                hTs, khs = E.hr.next()
                for st in range(4):
                    norm_transpose(xt[:, st, :], kxs[st], rstd[:, st:st + 1], krs, E.gB, E.h0r, psT_list[st % len(psT_list)],
                                   hTs[:, :, st * 128:(st + 1) * 128], khs)
                P.dma("pool", hT_out.rearrange("(c p) s -> p c s", p=128)[:, :, 1 + t0:1 + t0 + 512], hTs, khs + "s",
                      r=[khs], w=[kho])
            else:
                for st in range(4):
                    P.op("dve", lambda e, st=st: e.scalar_tensor_tensor(out=xt[:, st, :], in0=xt[:, st, :], scalar=rstd[:, st:st + 1],
                                                                        in1=E.gB, op0=ALU.mult, op1=ALU.mult),
                         r=[kxs[st], krs, "gB"], w=[kxs[st]])
                P.dma("pool", y_out[t0:t0 + 512, :].rearrange("(s p) d -> p s d", p=128), xt, kx + "s", r=kxs, w=["y"])
        return part2

    def proj_psfn(actT, kact, wres, kw):
        def f(st, pair):
            for half in range(2):
                b = pair[half]
                for c in range(8):
                    P.mm(bank[b], actT[:, c, st * 128:(st + 1) * 128], wres[:, c, half * 512:(half + 1) * 512],
                         c == 0, c == 7, [kact, kw], bk(b))
            return pair
        return f

    def hT_window(hT_src, t0):
        return hT_src.rearrange("(c p) s -> p c s", p=128)[:, :, 1 + t0:1 + t0 + 512]

    def phase_mem():
        P.reset()
        wkv = P.alloc("wkv", [8, 2 * D], BF16)
        gB = P.alloc("gB", [D], F32)
        mt = Ring([(P.alloc("mt%d" % i, [2, D], F32), "mt%d" % i) for i in range(2)])
        memT = Ring([(P.alloc("memT%d" % i, [8, MEMT], BF16), "memT%d" % i) for i in range(2)])
        kxs = Ring([(P.alloc("kxs%d" % i, [8, MEMT], BF16), "kxs%d" % i) for i in range(2)])
        vxs = Ring([(P.alloc("vxs%d" % i, [2, D], BF16), "vxs%d" % i) for i in range(2)])
        h0r = Ring([(P.alloc("h0_%d" % i, [D], BF16), "h0_%d" % i) for i in range(2)])
        ss = P.alloc("ss", [4], F32)
        rstd = P.alloc("rstd", [4], F32)
        junk = P.alloc("junk", [D], BF16)
        bi = [0]
        for li in range(2):
            P.dma("sp", wkv, xkv_b[li], "wkv", r=["wscr"], w=["wkv"])
            load_gB(gB, W["norm_mem_g"][li:li + 1, :], "gB")
            for j in range(nseq):
                m, km = mt.next()
                P.dma("sp", m, mem[j * MEMT:(j + 1) * MEMT, :].rearrange("(s p) d -> p s d", p=128), km, w=[km])
                P.op("pool", lambda e: e.memset(ss, 0.0), w=["ss"])
                for st in range(2):
                    P.op("act", lambda e, st=st, m=m: e.activation(out=junk, in_=m[:, st, :], func=AF.Square, accum_out=ss[:, st:st + 1]),
                         r=[km], w=["junk", "ss"])
                rstd_ops(ss, rstd, 2, "ss", "rstd", D, epsD, "epsD")
                mT, kmT = memT.next()
                for st in range(2):
                    norm_transpose(m[:, st, :], km, rstd[:, st:st + 1], "rstd", gB, h0r, 7, mT[:, :, st * 128:(st + 1) * 128], kmT)
                kx, kkx = kxs.next()
                for fo in range(8):
                    b = bi[0] % 4
                    bi[0] += 1
                    for c in range(8):
                        P.mm(bank[b][:, 0:MEMT], wkv[:, c, fo * 128:(fo + 1) * 128], mT[:, c, :], c == 0, c == 7, ["wkv", kmT], bk(b))
                    P.op("act" if fo % 2 == 0 else "dve",
                         (lambda e, b=b, fo=fo, kx=kx: e.activation(out=kx[:, fo, :], in_=bank[b][:, 0:MEMT], func=AF.Copy)) if fo % 2 == 0 else
                         (lambda e, b=b, fo=fo, kx=kx: e.tensor_copy(kx[:, fo, :], bank[b][:, 0:MEMT])),
                         r=[bk(b)], w=[kkx])
                P.dma("pool", memK[li * nseq + j], kx, kkx + "s", r=[kkx], w=["memK"])
                vx, kvx = vxs.next()
                for st in range(2):
                    for half in range(2):
                        b = bi[0] % 4
                        bi[0] += 1
                        for c in range(8):
                            P.mm(bank[b], mT[:, c, st * 128:(st + 1) * 128], wkv[:, c, D + half * 512:D + (half + 1) * 512],
                                 c == 0, c == 7, ["wkv", kmT], bk(b))
                        if half == 0:
                            P.op("act", lambda e, b=b, st=st, vx=vx: e.activation(out=vx[:, st, 0:512], in_=bank[b], func=AF.Copy),
                                 r=[bk(b)], w=[kvx])
                        else:
                            P.op("dve", lambda e, b=b, st=st, vx=vx: e.tensor_copy(vx[:, st, 512:1024], bank[b]), r=[bk(b)], w=[kvx])
                P.dma("pool", memV[li * nseq + j], vx, kvx + "s", r=[kvx], w=["memV"])
        P.barrier()

    def phase_A(xin, S, hT_out, kho):
        P.reset()
        E = EpiBufs()
        load_gB(E.gB, W["norm_mix_g"][0:1, :], "gB")
        for t0 in range(0, S, 512):
            xt, kx = E.xr.next()
            P.dma("sp", xt, xin[t0:t0 + 512, :].rearrange("(s p) d -> p s d", p=128), kx, w=[kx])
            P.op("pool", lambda e: e.memset(E.ss, 0.0), w=["ss"])
            for st in range(4):
                P.op("act", lambda e, st=st, xt=xt: e.activation(out=E.junk, in_=xt[:, st, :], func=AF.Square, accum_out=E.ss[:, st:st + 1]),
                     r=[kx], w=["junk", "ss"])
            rstd_ops(E.ss, E.rstd, 4, "ss", "rstd", D, epsD, "epsD")
            hTs, khs = E.hr.next()
            for st in range(4):
                norm_transpose(xt[:, st, :], kx, E.rstd[:, st:st + 1], "rstd", E.gB, E.h0r, 4 + (st % 2),
                               hTs[:, :, st * 128:(st + 1) * 128], khs)
            P.dma("pool", hT_window(hT_out, t0), hTs, khs + "s", r=[khs], w=[kho])
        P.barrier()

    def phase_B(S, hT_in, khi):
        P.reset()
        wqkv = P.alloc("wqkv", [8, 3 * D], BF16)
        cosT = P.alloc("cosT", [S], F32)
        sinT = P.alloc("sinT", [S], F32)
        hw = Ring([(P.alloc("hTw%d" % i, [8, 512], BF16), "hTw%d" % i) for i in range(2)])
        qs = Ring([(P.alloc("qTs%d" % i, [8, 512], BF16), "qTs%d" % i) for i in range(2)])
        ks = Ring([(P.alloc("kTs%d" % i, [8, 512], BF16), "kTs%d" % i) for i in range(2)])
        vs = Ring([(P.alloc("vs%d" % i, [4, D], BF16), "vs%d" % i) for i in range(2)])
        qb = Ring([(P.alloc("qb%d" % i, [512], BF16), "qb%d" % i) for i in range(3)])
        t1r = Ring([(P.alloc("t1_%d" % i, [512], F32), "t1_%d" % i) for i in range(3)])
        t2r = Ring([(P.alloc("t2_%d" % i, [512], F32), "t2_%d" % i) for i in range(3)])
        P.dma("sp", wqkv, wqkv_b, "wqkv", r=["wscr"], w=["wqkv"])
        P.dma("sp", cosT, c_cos[:, 0:S], "cosT", w=["cosT"])
        P.dma("sp", sinT, c_sin[:, 0:S], "sinT", w=["sinT"])
        bi = [0]
        for t0 in range(0, S, 512):
            hTw, khw = hw.next()
            P.dma("sp", hTw, hT_window(hT_in, t0), khw, r=[khi], w=[khw])
            qTs, kqs = qs.next()
            kTs, kks = ks.next()
            for fo in range(16):
                if "qk" in SKIP:
                    break
                isq = fo < 8
                b = bi[0] % 3
                br = 3 + bi[0] % 3
                bi[0] += 1
                for c in range(8):
                    P.mm(bank[b], wqkv[:, c, fo * 128:(fo + 1) * 128], hTw[:, c, :], c == 0, c == 7, ["wqkv", khw], bk(b))
                if "qk_act" in SKIP:
                    continue
                q16, kq16 = qb.next()
                P.op("act", lambda e, b=b, q16=q16: e.activation(out=q16, in_=bank[b], func=AF.Copy), r=[bk(b)], w=[kq16])
                if "rot" not in SKIP:
                    P.mm(bank[br], rot, q16, True, True, ["rot", kq16], bk(br))
                if "qk_dve" in SKIP:
                    continue
                t1, kt1 = t1r.next()
                t2, kt2 = t2r.next()
                sc = cq if isq else ck
                P.op("dve", lambda e, b=b, t1=t1, t0=t0: e.tensor_tensor(out=t1, in0=bank[b], in1=cosT[:, t0:t0 + 512], op=ALU.mult),
                     r=[bk(b), "cosT"], w=[kt1])
                if "qk_t2" in SKIP:
                    continue
                P.op("dve", lambda e, br=br, t2=t2, t0=t0: e.tensor_tensor(out=t2, in0=bank[br], in1=sinT[:, t0:t0 + 512], op=ALU.mult),
                     r=[bk(br), "sinT"], w=[kt2])
                if "qk_add" in SKIP:
                    continue
                dst, kd = (qTs[:, fo, :], kqs) if isq else (kTs[:, fo - 8, :], kks)
                P.op("dve" if "pooladd" in SKIP else "pool", lambda e, t1=t1, t2=t2, dst=dst: e.tensor_tensor(out=dst, in0=t1, in1=t2, op=ALU.add), r=[kt1, kt2], w=[kd])
            if "qkstore" not in SKIP:
                P.dma("pool", qT.rearrange("(h p) s -> p h s", p=128)[:, :, t0:t0 + 512], qTs, kqs + "s", r=[kqs], w=["qT"])
                P.dma("pool", kT.rearrange("(h p) s -> p h s", p=128)[:, :, t0:t0 + 512], kTs, kks + "s", r=[kks], w=["kT"])
            vt, kv = vs.next()
            for st in range(4):
                if "v" in SKIP:
                    break
                for half in range(2):
                    b = 6 + half
                    for c in range(8):
                        P.mm(bank[b], hTw[:, c, st * 128:(st + 1) * 128], wqkv[:, c, 2 * D + half * 512:2 * D + (half + 1) * 512],
                             c == 0, c == 7, ["wqkv", khw], bk(b))
                    if half == 0:
                        P.op("act", lambda e, b=b, st=st, vt=vt: e.activation(out=vt[:, st, 0:512], in_=bank[b], func=AF.Copy), r=[bk(b)], w=[kv])
                    else:
                        P.op("dve", lambda e, b=b, st=st, vt=vt: e.tensor_copy(vt[:, st, 512:1024], bank[b]), r=[bk(b)], w=[kv])
            kt0 = t0 // 128
            for h in range(NH):
                if "vstore" in SKIP:
                    break
                P.dma("pool", v2[h, :, kt0:kt0 + 4, :], vt[:, :, h * 128:(h + 1) * 128], kv + "s", r=[kv], w=["v2"])
        P.barrier()

    def phase_C(S, xin, x_out, kxo, hT_out, kho):
        P.reset()
        NKT = S // 128
        E = EpiBufs(nx=1, nh=1)
        load_gB(E.gB, W["norm_xattn_g"][0:1, :], "gB")
        wo = P.alloc("wo", [8, D], BF16)
        P.dma("sp", wo, wo_b, "wo", r=["wscr"], w=["wo"])
        qw = Ring([(P.alloc("qTw%d" % i, [8, 512], BF16), "qTw%d" % i) for i in range(2)])
        kvr = Ring([((P.alloc("KT%d" % i, [S], BF16), P.alloc("Vh%d" % i, [NKT, 128], BF16)), "KV%d" % i) for i in range(2)])
        pr = [Ring([(P.alloc("pT%d_%d" % (c, i), [512], BF16), "pT%d_%d" % (c, i)) for i in range(4)]) for c in range(2)]
        oallr = Ring([(P.alloc("oall%d" % i, [8, 512], F32), "oall%d" % i) for i in range(2)])
        oT = P.alloc("oT", [8, 512], BF16)
        oc0 = P.alloc("oc0", [512], F32)
        oc1 = P.alloc("oc1", [512], F32)
        sc0 = P.alloc("sc0", [512], F32)
        sc1 = P.alloc("sc1", [512], F32)
        acc0 = P.alloc("acc0", [512], F32)
        acc1 = P.alloc("acc1", [512], F32)
        acc1p = P.alloc("acc1p", [512], F32)
        pend1 = [None]
        pend2 = [None]
        r0 = P.alloc("r0", [512], F32)
        r1 = P.alloc("r1", [512], F32)
        ta = P.alloc("ta", [512], F32)
        tb = P.alloc("tb", [512], F32)
        sqr = Ring([(P.alloc("sq%d" % i, [512], BF16), "sq%d" % i) for i in range(2)])
        rsr = Ring([(P.alloc("rsn%d" % i, [512], F32), "rsn%d" % i) for i in range(2)])
        SB = [(0, 1), (2, 3)]
        O0, O1, S0, S1 = 4, 5, 6, 7
        for t0 in range(0, S, 512):
            qTw, kqw = qw.next()
            P.dma("sp", qTw, qT.rearrange("(h p) s -> p h s", p=128)[:, :, t0:t0 + 512], kqw, r=["qT"], w=[kqw])
            oall, koall = oallr.next()
            for h in range(NH):
                (KT, Vh), kkv = kvr.next()
                P.dma("sp", KT, kT[h * 128:(h + 1) * 128, 0:S], kkv, r=["kT"], w=[kkv])
                P.dma("sp", Vh, v2[h, :, 0:NKT, :], kkv, r=["v2"], w=[kkv])

                def scores(kt, par, KT=KT, qTw=qTw, kkv=kkv, kqw=kqw, h=h):
                    b0, b1 = SB[par]
                    P.mm(bank[b0], KT[0:64, kt * 128:(kt + 1) * 128], qTw[0:64, h, :], True, True, [kkv, kqw], bk(b0))
                    P.mm(bank[b1], KT[64:128, kt * 128:(kt + 1) * 128], qTw[64:128, h, :], True, True, [kkv, kqw], bk(b1))

                scores(0, 0)
                for kt in range(NKT):
                    par = kt % 2
                    if kt + 1 < NKT:
                        scores(kt + 1, 1 - par)
                    b0, b1 = SB[par]
                    p0, kp0 = pr[0].next()
                    p1, kp1 = pr[1].next()
                    P.op("act", lambda e, b0=b0, p0=p0: e.activation(out=p0, in_=bank[b0], func=AF.Exp, scale=0.125), r=[bk(b0)], w=[kp0])
                    P.op("act", lambda e, b1=b1, p1=p1: e.activation(out=p1, in_=bank[b1], func=AF.Exp, scale=0.125), r=[bk(b1)], w=[kp1])
                    st_, sp_ = (kt == 0), (kt == NKT - 1)
                    P.mm(bank[O0], Vh[:, kt, :], p0, st_, sp_, [kkv, kp0], bk(O0))
                    P.mm(bank[O1], Vh[:, kt, :], p1, st_, sp_, [kkv, kp1], bk(O1))
                    if kt == 0:
                        P.op("dve", lambda e, p0=p0: e.tensor_copy(acc0, p0), r=[kp0], w=["acc0"])
                        P.op("dve", lambda e, p1=p1: e.tensor_copy(acc1, p1), r=[kp1], w=["acc1"])
                    else:
                        P.op("dve", lambda e, p0=p0: e.tensor_tensor(out=acc0, in0=acc0, in1=p0, op=ALU.add), r=[kp0, "acc0"], w=["acc0"])
                        if kt == 1:
                            P.op("pool", lambda e, p1=p1: e.tensor_copy(acc1p, p1), r=[kp1], w=["acc1p"])
                        elif kt % 2 == 1:
                            P.op("pool", lambda e, p1=p1: e.tensor_tensor(out=acc1p, in0=acc1p, in1=p1, op=ALU.add), r=[kp1, "acc1p"], w=["acc1p"])
                        else:
                            P.op("dve", lambda e, p1=p1: e.tensor_tensor(out=acc1, in0=acc1, in1=p1, op=ALU.add), r=[kp1, "acc1"], w=["acc1"])
                P.op("dve", lambda e: e.tensor_tensor(out=acc1, in0=acc1, in1=acc1p, op=ALU.add), r=["acc1", "acc1p"], w=["acc1"])
                P.mm(bank[S0], ones32, acc0, True, True, ["ones32", "acc0"], bk(S0))
                P.mm(bank[S1], ones32, acc1, True, True, ["ones32", "acc1"], bk(S1))
                P.op("dve", lambda e: e.tensor_copy(oc0, bank[O0]), r=[bk(O0)], w=["oc0"])
                P.op("act", lambda e: e.activation(out=sc0, in_=bank[S0], func=AF.Copy), r=[bk(S0)], w=["sc0"])
                P.op("dve", lambda e: e.tensor_copy(oc1, bank[O1]), r=[bk(O1)], w=["oc1"])
                P.op("act", lambda e: e.activation(out=sc1, in_=bank[S1], func=AF.Copy), r=[bk(S1)], w=["sc1"])
                P.op("dve", lambda e: e.reciprocal(r0, sc0), r=["sc0"], w=["r0"])
                P.op("dve", lambda e: e.reciprocal(r1, sc1), r=["sc1"], w=["r1"])
                P.op("dve", lambda e: e.tensor_tensor(out=ta, in0=oc0, in1=r0, op=ALU.mult), r=["oc0", "r0"], w=["ta"])
                P.op("dve", lambda e: e.tensor_tensor(out=tb, in0=oc1, in1=r1, op=ALU.mult), r=["oc1", "r1"], w=["tb"])
                P.op("dve", lambda e, h=h, oall=oall: e.scalar_tensor_tensor(out=oall[:, h, :], in0=tb, scalar=nlam, in1=ta, op0=ALU.mult, op1=ALU.add),
                     r=["ta", "tb", "nlam"], w=[(koall, h)])
                if h == 0 and pend1[0] is not None:
                    pend2[0] = pend1[0]()
                    pend1[0] = None
                elif h == 1 and pend2[0] is not None:
                    pend2[0]()
                    pend2[0] = None

            def post_heads(t0=t0, oall=oall, koall=koall):
                for h in range(NH):
                    sq, ksq = sqr.next()
                    rsn, krsn = rsr.next()
                    b = h % 4
                    P.op("pool", lambda e, h=h, sq=sq: e.tensor_tensor(out=sq, in0=oall[:, h, :], in1=oall[:, h, :], op=ALU.mult),
                         r=[(koall, h)], w=[ksq])
                    P.mm(bank[b], ones, sq, True, True, ["ones", ksq], bk(b))
                    P.op("act", lambda e, b=b, rsn=rsn: e.activation(out=rsn, in_=bank[b], func=AF.Ln, bias=epsS, scale=1.0 / 128),
                         r=[bk(b), "epsS"], w=[krsn])
                    P.op("act", lambda e, rsn=rsn: e.activation(out=rsn, in_=rsn, func=AF.Exp, scale=-0.5), r=[krsn], w=[krsn])
                    P.op("dve", lambda e, h=h, rsn=rsn: e.scalar_tensor_tensor(out=oT[:, h, :], in0=oall[:, h, :], scalar=sg, in1=rsn,
                                                                              op0=ALU.mult, op1=ALU.mult),
                         r=[(koall, h), "sg", krsn], w=["oT"])
                return epilogue(E, t0, proj_psfn(oT, "oT", wo, "wo"), xin, x_out, kxo, hT_out, kho, None, [(0, 1), (2, 3)], [0, 1, 2, 3])

            pend1[0] = post_heads
        if pend1[0] is not None:
            pend2[0] = pend1[0]()
        if pend2[0] is not None:
            pend2[0]()
        P.barrier()

    def phase_D(S, li, sj, hT_in, khi, xin, kxi, x_out, kxo, hT_out, kho):
        P.reset()
        E = EpiBufs()
        load_gB(E.gB, W["norm_ffn_g"][li:li + 1, :], "gB")
        wq = P.alloc("wq", [8, D], BF16)
        wo = P.alloc("wo", [8, D], BF16)
        Kx = P.alloc("Kx", [8, MEMT], BF16)
        Vx = P.alloc("Vx", [2, D], BF16)
        P.dma("sp", wq, xq_b[li], "wq", r=["wscr"], w=["wq"])
        P.dma("sp", wo, xo_b[li], "wo", r=["wscr"], w=["wo"])
        P.dma("sp", Kx, memK[li * nseq + sj], "Kx", r=["memK"], w=["Kx"])
        P.dma("sp", Vx, memV[li * nseq + sj], "Vx", r=["memV"], w=["Vx"])
        hw = Ring([(P.alloc("hTw%d" % i, [8, 512], BF16), "hTw%d" % i) for i in range(2)])
        qxr = Ring([(P.alloc("qx%d" % i, [8, 512], BF16), "qx%d" % i) for i in range(2)])
        oXr = Ring([(P.alloc("oX%d" % i, [8, 512], BF16), "oX%d" % i) for i in range(2)])
        pmr = Ring([(P.alloc("pm%d" % i, [512], BF16), "pm%d" % i) for i in range(4)])
        rr = Ring([(P.alloc("rx%d" % i, [512], F32), "rx%d" % i) for i in range(2)])
        smr = Ring([(P.alloc("smc%d" % i, [512], F32), "smc%d" % i) for i in range(2)])
        pend = [None]
        for t0 in range(0, S, 512):
            hTw, khw = hw.next()
            P.dma("sp", hTw, hT_window(hT_in, t0), khw, r=[khi], w=[khw])
            qx, kqx = qxr.next()
            for fo in range(8):
                b = fo % 2
                for c in range(8):
                    P.mm(bank[b], wq[:, c, fo * 128:(fo + 1) * 128], hTw[:, c, :], c == 0, c == 7, ["wq", khw], bk(b))
                if fo % 2 == 0:
                    P.op("act", lambda e, b=b, fo=fo, qx=qx: e.activation(out=qx[:, fo, :], in_=bank[b], func=AF.Copy, scale=1.0 / 16),
                         r=[bk(b)], w=[(kqx, fo)])
                else:
                    P.op("dve", lambda e, b=b, fo=fo, qx=qx: e.tensor_scalar(qx[:, fo, :], bank[b], 1.0 / 16, None, op0=ALU.mult),
                         r=[bk(b)], w=[(kqx, fo)])
            if pend[0] is not None:
                pend[0]()
                pend[0] = None
            oX, koX = oXr.next()

            def xscores(h, par, qx=qx, kqx=kqx):
                for mt_ in range(2):
                    b = 2 * par + mt_
                    for j in range(2):
                        P.mm(bank[b], Kx[:, 2 * h + j, mt_ * 128:(mt_ + 1) * 128], qx[:, 2 * h + j, :], j == 0, j == 1,
                             ["Kx", (kqx, 2 * h + j)], bk(b))

            xscores(0, 0)
            for h in range(4):
                par = h % 2
                if h + 1 < 4:
                    xscores(h + 1, 1 - par)
                pms = []
                for mt_ in range(2):
                    b = 2 * par + mt_
                    pm, kpm = pmr.next()
                    P.op("act", lambda e, b=b, pm=pm: e.activation(out=pm, in_=bank[b], func=AF.Exp), r=[bk(b)], w=[kpm])
                    pms.append((pm, kpm))
                for j in range(2):
                    b = 4 + j
                    for mt_ in range(2):
                        P.mm(bank[b], Vx[:, mt_, (2 * h + j) * 128:(2 * h + j + 1) * 128], pms[mt_][0], mt_ == 0, mt_ == 1,
                             ["Vx", pms[mt_][1]], bk(b))
                for mt_ in range(2):
                    P.mm(bank[6], ones, pms[mt_][0], mt_ == 0, mt_ == 1, ["ones", pms[mt_][1]], bk(6))
                rx, krx = rr.next()
                smc, ksmc = smr.next()
                P.op("act", lambda e, smc=smc: e.activation(out=smc, in_=bank[6], func=AF.Copy), r=[bk(6)], w=[ksmc])
                P.op("dve", lambda e, rx=rx, smc=smc: e.reciprocal(rx, smc), r=[ksmc], w=[krx])
                for j in range(2):
                    P.op("dve", lambda e, j=j, h=h, rx=rx, oX=oX: e.tensor_tensor(out=oX[:, 2 * h + j, :], in0=bank[4 + j], in1=rx, op=ALU.mult),
                         r=[bk(4 + j), krx], w=[koX])
            pend[0] = epilogue(E, t0, proj_psfn(oX, koX, wo, "wo"), xin, x_out, kxo, hT_out, kho, None, [(0, 1), (2, 3)], 7)
        if pend[0] is not None:
            pend[0]()
        P.barrier()

    def phase_E(S, li, hT_in, khi, xin, kxi, x_out, kxo, hT_out, kho, y_out, g_row):
        P.reset()
        E = EpiBufs(nx=1, nh=1)
        load_gB(E.gB, g_row, "gB")
        wdn = P.alloc("wdn", [NFC, D], BF16)
        P.dma("sp", wdn, wdn_b[li], "wdn", r=["wscr"], w=["wdn"])
        cw = P.alloc("cw", [3, NFC], F32)
        cb = P.alloc("cb", [NFC], F32)
        for j in range(3):
            P.dma("sp", cw[:, j, :], W["ffn_conv_w"][li, j, :].rearrange("(f p) -> p f", p=128), "cw", w=["cw"], slow=True)
        P.dma("sp", cb, W["ffn_conv_b"][li, :].rearrange("(f p) -> p f", p=128), "cw", w=["cw"], slow=True)
        wur = Ring([(P.alloc("wu%d" % i, [2, 8, 256], BF16), "wu%d" % i) for i in range(3)])
        hw = Ring([(P.alloc("hTh%d" % i, [8, 514], BF16), "hTh%d" % i) for i in range(2)])
        u = P.alloc("u", [NFC, 512], BF16)
        gbr = Ring([(P.alloc("gb%d" % i, [514], F32), "gb%d" % i) for i in range(2)])
        c1r = Ring([(P.alloc("c1_%d" % i, [512], F32), "c1_%d" % i) for i in range(2)])
        c2r = Ring([(P.alloc("c2_%d" % i, [512], F32), "c2_%d" % i) for i in range(2)])
        c3r = Ring([(P.alloc("c3_%d" % i, [512], F32), "c3_%d" % i) for i in range(2)])
        ger = Ring([(P.alloc("ge%d" % i, [512], F32), "ge%d" % i) for i in range(2)])
        bi = [0]
        pend = [None]
        for t0 in range(0, S, 512):
            hTh, khh = hw.next()
            P.dma("sp", hTh, hT_in.rearrange("(c p) s -> p c s", p=128)[:, :, t0:t0 + 514], khh, r=[khi], w=[khh])
            for g in range(11):
                if g == 2 and pend[0] is not None:
                    pend[0]()
                    pend[0] = None
                wu, kwu = wur.next()
                P.dma("sp", wu[:, 0], wup_b[li][g], kwu, r=["wscr"], w=[kwu])
                P.dma("sp", wu[:, 1], wup_b[li][11 + g], kwu, r=["wscr"], w=[kwu])
                for j in range(2):
                    fc = 2 * g + j
                    par = bi[0] % 2
                    bi[0] += 1
                    bG, bV, bH = 2 * par, 2 * par + 1, 4 + par
                    for c in range(8):
                        P.mm(bank[bG], wu[:, 0, c, j * 128:(j + 1) * 128], hTh[:, c, 1:513], c == 0, c == 7, [kwu, khh], bk(bG))
                    for c in range(8):
                        P.mm(bank[bH][:, 0:2], wu[:, 0, c, j * 128:(j + 1) * 128], hTh[:, c, 0:514:513], c == 0, c == 7, [kwu, khh], bk(bH))
                    for c in range(8):
                        P.mm(bank[bV], wu[:, 1, c, j * 128:(j + 1) * 128], hTh[:, c, 1:513], c == 0, c == 7, [kwu, khh], bk(bV))
                    gb, kgb = gbr.next()
                    P.op("act", lambda e, gb=gb, bG=bG: e.activation(out=gb[:, 1:513], in_=bank[bG], func=AF.Copy), r=[bk(bG)], w=[(kgb, 0)])
                    P.op("dve", lambda e, gb=gb, bH=bH: e.tensor_copy(gb[:, 0:514:513], bank[bH][:, 0:2]), r=[bk(bH)], w=[(kgb, 1)])
                    c1, kc1 = c1r.next()
                    c2, kc2 = c2r.next()
                    c3, kc3 = c3r.next()
                    ge, kge = ger.next()
                    P.op("dve", lambda e, gb=gb, c1=c1, fc=fc: e.tensor_scalar(c1, gb[:, 0:512], cw[:, 0, fc:fc + 1], cb[:, fc:fc + 1],
                                                                               op0=ALU.mult, op1=ALU.add),
                         r=[(kgb, 0), (kgb, 1), "cw"], w=[kc1])
                    P.op("dve", lambda e, gb=gb, c1=c1, c2=c2, fc=fc: e.scalar_tensor_tensor(out=c2, in0=gb[:, 1:513], scalar=cw[:, 1, fc:fc + 1], in1=c1,
                                                                                              op0=ALU.mult, op1=ALU.add),
                         r=[(kgb, 0), (kgb, 1), "cw", kc1], w=[kc2])
                    P.op("dve", lambda e, gb=gb, c2=c2, c3=c3, fc=fc: e.scalar_tensor_tensor(out=c3, in0=gb[:, 2:514], scalar=cw[:, 2, fc:fc + 1], in1=c2,
                                                                                              op0=ALU.mult, op1=ALU.add),
                         r=[(kgb, 0), (kgb, 1), "cw", kc2], w=[kc3])
                    P.op("act", lambda e, c3=c3, ge=ge: e.activation(out=ge, in_=c3, func=AF.Gelu), r=[kc3], w=[kge])
                    P.op("dve", lambda e, ge=ge, bV=bV, fc=fc: e.tensor_tensor(out=u[:, fc, :], in0=bank[bV], in1=ge, op=ALU.mult),
                         r=[bk(bV), kge], w=[("u", fc)])

            def ps_fn(st, pair):
                for half in range(2):
                    b = pair[half]
                    for kc in range(NFC):
                        P.mm(bank[b], u[:, kc, st * 128:(st + 1) * 128], wdn[:, kc, half * 512:(half + 1) * 512],
                             kc == 0, kc == NFC - 1, [("u", kc), "wdn"], bk(b))
                return pair

            pend[0] = epilogue(E, t0, ps_fn, xin, x_out, kxo, hT_out, kho, y_out, [(0, 1), (2, 3)], [6, 7])
        if pend[0] is not None:
            pend[0]()
        P.barrier()

    def phase_F1(S, hT_in, khi):
        P.reset()
        win = P.alloc("win", [8, D], BF16)
        cc = P.alloc("cc", [2, 256], BF16)
        sc = P.alloc("sc", [2, 256], BF16)
        P.dma("sp", win, fin_b, "win", r=["wscr"], w=["win"])
        P.dma("sp", cc, c_cc[S].rearrange("(j p) n -> p j n", p=128), "cc", w=["cc"])
        P.dma("sp", sc, c_sc[S].rearrange("(j p) n -> p j n", p=128), "sc", w=["sc"])
        hw = Ring([(P.alloc("hTw%d" % i, [8, 512], BF16), "hTw%d" % i) for i in range(2)])
        uTr = Ring([(P.alloc("uT%d" % i, [8, 512], BF16), "uT%d" % i) for i in range(2)])
        Asr = Ring([(P.alloc("As%d" % i, [4, D], BF16), "As%d" % i) for i in range(2)])
        Bsr = Ring([(P.alloc("Bs%d" % i, [4, D], BF16), "Bs%d" % i) for i in range(2)])
        for t0 in range(0, S, 512):
            hTw, khw = hw.next()
            P.dma("sp", hTw, hT_window(hT_in, t0), khw, r=[khi], w=[khw])
            uT, kuT = uTr.next()
            for fo in range(8):
                b = fo % 2
                for c in range(8):
                    P.mm(bank[b], win[:, c, fo * 128:(fo + 1) * 128], hTw[:, c, :], c == 0, c == 7, ["win", khw], bk(b))
                if fo % 2 == 0:
                    P.op("act", lambda e, b=b, fo=fo, uT=uT: e.activation(out=uT[:, fo, :], in_=bank[b], func=AF.Copy), r=[bk(b)], w=[(kuT, fo)])
                else:
                    P.op("dve", lambda e, b=b, fo=fo, uT=uT: e.tensor_copy(uT[:, fo, :], bank[b]), r=[bk(b)], w=[(kuT, fo)])
            As, kAs = Asr.next()
            Bs, kBs = Bsr.next()
            for st in range(4):
                for (tab, ktab, dst, kd, b0) in ((cc, "cc", As, kAs, 2), (sc, "sc", Bs, kBs, 4)):
                    for g in range(4):
                        b = b0 + g // 2
                        o = (g % 2) * 256
                        for j in range(2):
                            P.mm(bank[b][:, o:o + 256], uT[:, 2 * g + j, st * 128:(st + 1) * 128], tab[:, j, :], j == 0, j == 1,
                                 [(kuT, 2 * g + j), ktab], bk(b))
                    P.op("act", lambda e, b0=b0, dst=dst, st=st: e.activation(out=dst[:, st, 0:512], in_=bank[b0], func=AF.Copy), r=[bk(b0)], w=[kd])
                    P.op("dve", lambda e, b0=b0, dst=dst, st=st: e.tensor_copy(dst[:, st, 512:1024], bank[b0 + 1]), r=[bk(b0 + 1)], w=[kd])
            P.dma("pool", Ad[t0:t0 + 512, :].rearrange("(s p) d -> p s d", p=128), As, kAs + "s", r=[kAs], w=["Ad"])
            P.dma("pool", Bd[t0:t0 + 512, :].rearrange("(s p) d -> p s d", p=128), Bs, kBs + "s", r=[kBs], w=["Bd"])
        P.barrier()

    def phase_F2(S):
        P.reset()
        NKT = S // 128
        rs_ = TAB // S
        G = 4
        Ah = P.alloc("Ah", [NKT, 512], BF16)
        Bh = P.alloc("Bh", [NKT, 512], BF16)
        tr = Ring([((P.alloc("tC%d" % i, [G, 512], BF16), P.alloc("tN%d" % i, [G, 512], BF16)), "tab%d" % i) for i in range(4)])
        fr = Ring([(P.alloc("fs%d" % i, [4, 512], BF16), "fs%d" % i) for i in range(2)])
        CSv = c_CS.rearrange("(k p r) n -> p k r n", p=128, r=rs_)
        NSv = c_NS.rearrange("(k p r) n -> p k r n", p=128, r=rs_)
        pb = [0]
        for half in range(2):
            P.dma("sp", Ah, Ad[0:S, half * 512:(half + 1) * 512].rearrange("(k p) d -> p k d", p=128), "Ah", r=["Ad"], w=["Ah"])
            P.dma("sp", Bh, Bd[0:S, half * 512:(half + 1) * 512].rearrange("(k p) d -> p k d", p=128), "Bh", r=["Bd"], w=["Bh"])
            for s0 in range(0, S, 512):
                base = 4 * (pb[0] % 2)
                pb[0] += 1
                for kg in range(0, NKT, G):
                    (tC, tN), ktab = tr.next()
                    P.dma("sp", tC, CSv[:, kg:kg + G, 0, s0:s0 + 512], ktab, w=[ktab])
                    P.dma("sp", tN, NSv[:, kg:kg + G, 0, s0:s0 + 512], ktab, w=[ktab])
                    for cc_ in range(4):
                        b = base + cc_
                        for k in range(G):
                            kt = kg + k
                            P.mm(bank[b], Ah[:, kt, cc_ * 128:(cc_ + 1) * 128], tC[:, k, :], kt == 0, False, ["Ah", ktab], bk(b))
                            P.mm(bank[b], Bh[:, kt, cc_ * 128:(cc_ + 1) * 128], tN[:, k, :], False, kt == NKT - 1, ["Bh", ktab], bk(b))
                fs, kfs = fr.next()
                for cc_ in range(4):
                    b = base + cc_
                    if cc_ % 2 == 0:
                        P.op("act", lambda e, b=b, cc_=cc_, fs=fs: e.activation(out=fs[:, cc_, :], in_=bank[b], func=AF.Copy), r=[bk(b)], w=[kfs])
                    else:
                        P.op("dve", lambda e, b=b, cc_=cc_, fs=fs: e.tensor_copy(fs[:, cc_, :], bank[b]), r=[bk(b)], w=[kfs])
                P.dma("pool", fTd.rearrange("(c p) s -> p c s", p=128)[:, half * 4:(half + 1) * 4, s0:s0 + 512], fs, kfs + "s",
                      r=[kfs], w=["fTd"])
        P.barrier()

    def phase_F3(S, xin, kxi, x_out, kxo, hT_out, kho):
        P.reset()
        E = EpiBufs()
        load_gB(E.gB, W["norm_xattn_g"][1:2, :], "gB")
        wout = P.alloc("wout", [8, D], BF16)
        P.dma("sp", wout, fout_b, "wout", r=["wscr"], w=["wout"])
        fw = Ring([(P.alloc("fTw%d" % i, [8, 512], BF16), "fTw%d" % i) for i in range(2)])
        pend = [None]
        for t0 in range(0, S, 512):
            fTw, kfw = fw.next()
            P.dma("sp", fTw, fTd.rearrange("(c p) s -> p c s", p=128)[:, :, t0:t0 + 512], kfw, r=["fTd"], w=[kfw])
            p2 = epilogue(E, t0, proj_psfn(fTw, kfw, wout, "wout"), xin, x_out, kxo, hT_out, kho, None, [(0, 1), (2, 3)], [4, 5, 6, 7])
            if pend[0] is not None:
                pend[0]()
            pend[0] = p2
        if pend[0] is not None:
            pend[0]()
        P.barrier()

    phase_lambda()
    if stop_after != "L":
        phase_weights()
    if stop_after not in ("L", "W"):
        phase_mem()
    for sj, S in enumerate(seqs):
        if stop_after in ("L", "W", "M"):
            break
        xin = x[offs[sj]:offs[sj] + S, :]
        yout = y[offs[sj]:offs[sj] + S, :]
        for hTx, kh in ((hTa, "hTa"), (hTb, "hTb")):
            v = hTx.rearrange("(c p) s -> p c s", p=128)
            P.dma("pool", v[:, :, 0:1], zcol, "zc", r=["zcol"], w=[kh], slow=True)
            P.dma("pool", v[:, :, S + 1:S + 2], zcol, "zc", r=["zcol"], w=[kh], slow=True)
        P.barrier()
        phase_A(xin, S, hTa, "hTa")
        if stop_after == "A":
            break
        phase_B(S, hTa, "hTa")
        if stop_after == "B":
            break
        phase_C(S, xin, xa, "xa", hTb, "hTb")
        if stop_after == "C":
            break
        phase_D(S, 0, sj, hTb, "hTb", xa, "xa", xb, "xb", hTa, "hTa")
        if stop_after == "D0":
            break
        phase_E(S, 0, hTa, "hTa", xb, "xb", xa, "xa", hTb, "hTb", None, W["norm_mix_g"][1:2, :])
        if stop_after == "E0":
            break
        phase_F1(S, hTb, "hTb")
        phase_F2(S)
        if stop_after == "F2":
            break
        phase_F3(S, xa, "xa", xb, "xb", hTa, "hTa")
        if stop_after == "F3":
            break
        phase_D(S, 1, sj, hTa, "hTa", xb, "xb", xa, "xa", hTb, "hTb")
        if stop_after == "D1":
            break
        phase_E(S, 1, hTb, "hTb", xa, "xa", None, None, None, None, yout, W["final_norm_g"][0:1, :])
    P.final_wait("pool")
    P.final_wait("sp")
    P.emit()
    stats = P.stats
    P.es.close()
    return nc, stats


def make_consts(seqs, TAB):
    c = {}
    c["c_ident"] = np.eye(128, dtype=np.float32).astype(NPBF)
    c["c_ones"] = np.ones((128, 128), dtype=np.float32).astype(NPBF)
    rot = np.zeros((128, 128), dtype=np.float32)
    for p in range(128):
        blk = (p % 64) // 32
        if blk == 0:
            rot[p + 32, p] = -1.0
        else:
            rot[p - 32, p] = 1.0
    c["c_rot"] = rot.astype(NPBF)
    half = 32
    inv_freq = (10000.0 ** (-(np.arange(0, half, dtype=np.float32)) * 2.0 / 64)).astype(np.float32)
    ang = np.arange(TAB, dtype=np.float32)[:, None] * inv_freq[None, :]
    cos = np.cos(ang).astype(np.float32).T
    sin = np.sin(ang).astype(np.float32).T
    c["c_cos"] = np.ascontiguousarray(np.tile(cos, (4, 1)))
    c["c_sin"] = np.ascontiguousarray(np.tile(sin, (4, 1)))
    k = np.arange(256, dtype=np.int64)
    a256 = 2.0 * np.pi * ((k[:, None] * k[None, :]) % 256) / 256.0
    for S in sorted(set(seqs)):
        scl = 1.0 / math.sqrt(256.0 * S)
        c["c_cc%d" % S] = (np.cos(a256) * scl).astype(np.float32).astype(NPBF)
        c["c_sc%d" % S] = (np.sin(a256) * scl).astype(np.float32).astype(NPBF)
    s = np.arange(TAB, dtype=np.int64)
    aS = (2.0 * np.pi / TAB) * ((s[:, None] * s[None, :]) % TAB).astype(np.float64)
    c["c_CS"] = np.cos(aS).astype(np.float32).astype(NPBF)
    c["c_NS"] = (-np.sin(aS)).astype(np.float32).astype(NPBF)
    return c


_CACHE = {}


def run_cores(xs, mems, weights, seqs, TAB, dbg=False, stop_after=None, ncores=None):
    key = (tuple(seqs), TAB, dbg, stop_after)
    if key not in _CACHE:
        _CACHE[key] = (build(seqs, TAB, dbg=dbg, stop_after=stop_after), make_consts(seqs, TAB))
    (nc, stats), consts = _CACHE[key]
    n = len(xs)
    wd = {}
    for nme, shp in WNAMES:
        wd[nme] = np.ascontiguousarray(np.asarray(weights[nme], dtype=np.float32).reshape(shp))
    in_maps = []
    for i in range(n):
        m = {"x": np.ascontiguousarray(xs[i], dtype=np.float32), "mem": np.ascontiguousarray(mems[i], dtype=np.float32)}
        m.update(wd)
        m.update(consts)
        in_maps.append(m)
    res = run_bass_kernel_spmd(nc, in_maps, core_ids=list(range(n)))
    return res.results


def kernel(x_prompt, x_sample, mem_prompt, mem_sample, **weights):
    x_prompt = np.asarray(x_prompt, dtype=np.float32)
    x_sample = np.asarray(x_sample, dtype=np.float32)
    mem_prompt = np.asarray(mem_prompt, dtype=np.float32)
    mem_sample = np.asarray(mem_sample, dtype=np.float32)
    n = 8
    SP, SS = x_prompt.shape[1], x_sample.shape[1]
    seqs = [SP, SP, SS]
    xs, mems = [], []
    for i in range(n):
        xs.append(np.concatenate([x_prompt[2 * i], x_prompt[2 * i + 1], x_sample[i]], axis=0))
        mems.append(np.concatenate([mem_prompt[2 * i], mem_prompt[2 * i + 1], mem_sample[i]], axis=0))
    res = run_cores(xs, mems, weights, seqs, max(seqs))
    y_prompt = np.empty_like(x_prompt)
    y_sample = np.empty_like(x_sample)
    for i in range(n):
        yy = res[i]["y"]
        y_prompt[2 * i] = yy[0:SP]
        y_prompt[2 * i + 1] = yy[SP:2 * SP]
        y_sample[i] = yy[2 * SP:2 * SP + SS]
    return (y_prompt, y_sample)
```

```python
import math
import os
from contextlib import ExitStack
SKIP = os.environ.get('KSKIP', '').split(',')

import numpy as np
import ml_dtypes

import concourse.bass as bass
import concourse.mybir as mybir
from concourse.bass_utils import run_bass_kernel_spmd

F32 = mybir.dt.float32
BF16 = mybir.dt.bfloat16
U8 = mybir.dt.uint8
AF = mybir.ActivationFunctionType
ALU = mybir.AluOpType
AX = mybir.AxisListType
NPBF = ml_dtypes.bfloat16

ENGS = ("pe", "act", "dve", "pool", "sp")
D = 1024
FF = 2816
NFC = 22
NH = 8
MEMT = 256
EPS = 1e-6
SUBEPS = 1e-5
LAMBDA_INIT = 0.8 - 0.6 * math.exp(-0.3 * 0)
ARENA = 196608 - 2048


def _size(dt):
    return 4 if dt == F32 else (2 if dt == BF16 else 1)


class Ring:
    def __init__(self, items):
        self.items = items
        self.i = 0

    def next(self):
        it = self.items[self.i % len(self.items)]
        self.i += 1
        return it


class Prog:
    def __init__(self, nc):
        self.nc = nc
        self.es = ExitStack()
        self.streams = {e: [] for e in ENGS}
        self.waited = {e: {} for e in ENGS}
        self.res = {}
        self.dma_cnt = {}
        self.dma_sems = {}
        self.multi = set()
        self.eng_sems = {}
        self.arena = self.es.enter_context(nc.sbuf_tensor("arena", [128, ARENA], U8))
        self.ptr = 0
        self.mark = 0
        self.banks = [self.es.enter_context(nc.psum_tensor("bank%d" % i, [128, 512], F32)) for i in range(8)]

    def alloc(self, name, shape, dt):
        n = 1
        for s in shape:
            n *= s
        nb = n * _size(dt)
        off = (self.ptr + 63) // 64 * 64
        assert off + nb <= ARENA, ("SBUF arena overflow", name, off + nb)
        self.ptr = off + nb
        ap = self.arena[:, off:off + nb]
        if dt != U8:
            ap = ap.bitcast(dt)
        if len(shape) == 2:
            ap = ap.rearrange("p (a b) -> p a b", b=shape[1])
        elif len(shape) == 3:
            ap = ap.rearrange("p (a b c) -> p a b c", b=shape[1], c=shape[2])
        return ap

    def set_mark(self):
        self.mark = self.ptr

    def reset(self):
        self.ptr = self.mark

    def _deps(self, eng, is_dma, reads, writes):
        toks = {}

        def need(d):
            for sk, v in d.items():
                if (not is_dma) and sk == ("E", eng) and eng == "pe":
                    continue
                if toks.get(sk, -1) < v:
                    toks[sk] = v

        for k in reads:
            st = self.res.get(k)
            if st:
                need(st["w"])
                if isinstance(k, tuple) and k[0] == "b":
                    for sk, v in st["r"].items():
                        if sk != ("E", eng) and toks.get(sk, -1) < v:
                            toks[sk] = v
        for k in writes:
            st = self.res.get(k)
            if st:
                need(st["r"])
                if k not in self.multi:
                    need(st["w"])
        out = []
        wd = self.waited[eng]
        for sk, v in toks.items():
            if wd.get(sk, -1) >= v:
                continue
            wd[sk] = v
            out.append((sk, v))
        return out

    def _commit(self, tok, reads, writes):
        sk, v = tok
        for k in reads:
            st = self.res.setdefault(k, {"w": {}, "r": {}})
            if st["r"].get(sk, -1) < v:
                st["r"][sk] = v
        for k in writes:
            st = self.res.setdefault(k, {"w": {}, "r": {}})
            if k in self.multi and not st["r"]:
                if st["w"].get(sk, -1) < v:
                    st["w"][sk] = v
            else:
                st["w"] = {sk: v}
                st["r"] = {}

    def _flag(self, waits):
        for sk, v in waits:
            if sk[0] == "E":
                self.streams[sk[1]][v]["flag"] = True

    def op(self, eng, fn, r=(), w=()):
        waits = self._deps(eng, False, r, w)
        idx = len(self.streams[eng])
        self.streams[eng].append({"fn": fn, "waits": waits, "flag": False, "dma": None})
        self._commit((("E", eng), idx), r, w)
        self._flag(waits)

    def mm(self, out, lhsT, rhs, start, stop, r, w):
        self.op("pe", lambda e: e.matmul(out, lhsT, rhs, start=start, stop=stop), r=r, w=[w])

    def dma(self, q, out, in_, sem, r=(), w=(), slow=False):
        waits = self._deps(q, True, r, w)
        c = self.dma_cnt.get(sem, 0) + 1
        self.dma_cnt[sem] = c
        self.streams[q].append({"fn": (out, in_, slow), "waits": waits, "flag": False, "dma": sem})
        self._commit((("D", sem), c * 16), r, w)
        self._flag(waits)

    def barrier(self):
        toks = []
        for e in ENGS:
            last = None
            for i in range(len(self.streams[e]) - 1, -1, -1):
                o = self.streams[e][i]
                if o["dma"] is None and o["fn"] is not None:
                    last = i
                    break
            if last is not None:
                toks.append((("E", e), last))
        for sem, c in self.dma_cnt.items():
            toks.append((("D", sem), c * 16))
        for e in ENGS:
            waits = []
            wd = self.waited[e]
            for sk, v in toks:
                if sk == ("E", e):
                    continue
                if wd.get(sk, -1) >= v:
                    continue
                wd[sk] = v
                waits.append((sk, v))
            self.streams[e].append({"fn": None, "waits": waits, "flag": False, "dma": None})
            self._flag(waits)

    def final_wait(self, q):
        waits = [(("D", sem), c * 16) for sem, c in self.dma_cnt.items()]
        self.streams[q].append({"fn": None, "waits": waits, "flag": False, "dma": None})

    def emit(self):
        nc = self.nc
        es = self.es
        for e in ENGS:
            self.eng_sems[e] = es.enter_context(nc.semaphore("sem_" + e))
        for s in self.dma_cnt:
            self.dma_sems[s] = es.enter_context(nc.semaphore("dsem_" + s))
        vals = {}
        for e in ENGS:
            c = 0
            v = []
            for o in self.streams[e]:
                if o["flag"]:
                    c += 1
                v.append(c)
            vals[e] = v
        self.stats = {e: (len(self.streams[e]), vals[e][-1] if vals[e] else 0) for e in ENGS}
        block = es.enter_context(nc.Block())

        def run(engname, eobj):
            sem_me = self.eng_sems[engname]
            for o in self.streams[engname]:
                for sk, v in o["waits"]:
                    if sk[0] == "E":
                        eobj.wait_ge(self.eng_sems[sk[1]], vals[sk[1]][v])
                    else:
                        eobj.wait_ge(self.dma_sems[sk[1]], v)
                if o["fn"] is None:
                    continue
                if o["dma"] is not None:
                    out, in_, slow = o["fn"]
                    if slow:
                        ins = eobj.dma_start(out=out, in_=in_, allow_slow_non_contiguous=True)
                    else:
                        ins = eobj.dma_start(out=out, in_=in_)
                    ins.then_inc(self.dma_sems[o["dma"]], 16)
                else:
                    ins = o["fn"](eobj)
                    if o["flag"]:
                        ins.then_inc(sem_me, 1)

        @block.tensor
        def _(e):
            run("pe", e)

        @block.scalar
        def _(e):
            run("act", e)

        @block.vector
        def _(e):
            run("dve", e)

        @block.gpsimd
        def _(e):
            run("pool", e)

        @block.sync
        def _(e):
            run("sp", e)


WNAMES = [
    ("norm_mix_g", (2, D)), ("norm_xattn_g", (2, D)), ("norm_mem_g", (2, D)), ("norm_ffn_g", (2, D)),
    ("final_norm_g", (1, D)),
    ("attn_w_qkv", (1, D, 3 * D)), ("attn_lambda_q1", (1, 64)), ("attn_lambda_k1", (1, 64)),
    ("attn_lambda_q2", (1, 64)), ("attn_lambda_k2", (1, 64)), ("attn_subln_g", (1, 128)),
    ("attn_w_o", (1, D, D)), ("fnet_w_in", (1, D, D)), ("fnet_w_out", (1, D, D)),
    ("xattn_w_q", (2, D, D)), ("xattn_w_kv", (2, D, 2 * D)), ("xattn_w_o", (2, D, D)),
    ("ffn_w_up", (2, D, 2 * FF)), ("ffn_conv_w", (2, 3, FF)), ("ffn_conv_b", (2, FF)),
    ("ffn_w_down", (2, FF, D)),
]


def build(seqs, TAB, dbg=False, stop_after=None):
    nseq = len(seqs)
    NT = sum(seqs)
    offs = [sum(seqs[:i]) for i in range(nseq)]
    Smax = max(seqs)
    svals = sorted(set(seqs))
    nc = bass.Bass("TRN2", target_bir_lowering=False)

    def din(name, shape, dt=F32):
        return nc.dram_tensor(name, list(shape), dt, kind="ExternalInput").ap()

    def dscr(name, shape, dt):
        if dbg:
            return nc.dram_tensor(name, list(shape), dt, kind="ExternalOutput").ap()
        return nc.dram_tensor(name, list(shape), dt).ap()

    x = din("x", [NT, D])
    mem = din("mem", [nseq * MEMT, D])
    W = {n: din(n, s) for n, s in WNAMES}
    c_ident = din("c_ident", [128, 128], BF16)
    c_ones = din("c_ones", [128, 128], BF16)
    c_rot = din("c_rot", [128, 128], BF16)
    c_cos = din("c_cos", [128, TAB])
    c_sin = din("c_sin", [128, TAB])
    c_cc = {S: din("c_cc%d" % S, [256, 256], BF16) for S in svals}
    c_sc = {S: din("c_sc%d" % S, [256, 256], BF16) for S in svals}
    c_CS = din("c_CS", [TAB, TAB], BF16)
    c_NS = din("c_NS", [TAB, TAB], BF16)
    y = nc.dram_tensor("y", [NT, D], F32, kind="ExternalOutput").ap()

    wqkv_b = dscr("wqkv_b", [128, 8, 3 * D], BF16)
    wo_b = dscr("wo_b", [128, 8, D], BF16)
    fin_b = dscr("fin_b", [128, 8, D], BF16)
    fout_b = dscr("fout_b", [128, 8, D], BF16)
    xq_b = [dscr("xq_b%d" % i, [128, 8, D], BF16) for i in range(2)]
    xo_b = [dscr("xo_b%d" % i, [128, 8, D], BF16) for i in range(2)]
    xkv_b = [dscr("xkv_b%d" % i, [128, 8, 2 * D], BF16) for i in range(2)]
    wup_b = [dscr("wup_b%d" % i, [22, 128, 8, 256], BF16) for i in range(2)]
    wdn_b = [dscr("wdn_b%d" % i, [128, NFC, D], BF16) for i in range(2)]
    xa = dscr("xa", [Smax, D], F32)
    xb = dscr("xb", [Smax, D], F32)
    hTa = dscr("hTa", [D, Smax + 2], BF16)
    hTb = dscr("hTb", [D, Smax + 2], BF16)
    qT = dscr("qT", [D, Smax], BF16)
    kT = dscr("kT", [D, Smax], BF16)
    v2 = dscr("v2", [NH, 128, Smax // 128, 128], BF16)
    Ad = dscr("Ad", [Smax, D], BF16)
    Bd = dscr("Bd", [Smax, D], BF16)
    fTd = dscr("fTd", [D, Smax], BF16)
    memK = dscr("memK", [2 * nseq, 128, 8, MEMT], BF16)
    memV = dscr("memV", [2 * nseq, 128, 2, D], BF16)

    P = Prog(nc)
    for k in ("xa", "xb", "hTa", "hTb", "qT", "kT", "v2", "Ad", "Bd", "fTd", "memK", "memV", "y", "wscr"):
        P.multi.add(k)
    bank = [b[:] for b in P.banks]

    def bk(i):
        return ("b", i)

    ident = P.alloc("ident", [128], BF16)
    ones = P.alloc("ones", [128], BF16)
    rot = P.alloc("rot", [128], BF16)
    epsD = P.alloc("epsD", [1], F32)
    epsS = P.alloc("epsS", [1], F32)
    nlam = P.alloc("nlam", [1], F32)
    sg = P.alloc("sg", [1], F32)
    zcol = P.alloc("zcol", [8, 1], BF16)
    ones32 = P.alloc("ones32", [128], F32)
    cq = P.alloc("cq", [1], F32)
    ck = P.alloc("ck", [1], F32)
    P.dma("sp", ident, c_ident, "c0", w=["ident"])
    P.dma("sp", ones, c_ones, "c1", w=["ones"])
    P.dma("sp", rot, c_rot, "c2", w=["rot"])
    P.op("pool", lambda e: e.memset(epsD, EPS), w=["epsD"])
    P.op("pool", lambda e: e.memset(epsS, SUBEPS), w=["epsS"])
    P.op("pool", lambda e: e.memset(zcol, 0.0), w=["zcol"])
    P.op("pool", lambda e: e.memset(cq, 0.125), w=["cq"])
    P.op("pool", lambda e: e.memset(ones32, 1.0), w=["ones32"])
    P.op("pool", lambda e: e.memset(ck, 1.0), w=["ck"])
    P.set_mark()

    def phase_lambda():
        P.reset()
        lt = [P.alloc("lt%d" % i, [64], F32) for i in range(4)]
        pr = [P.alloc("pr%d" % i, [64], F32) for i in range(2)]
        sm = [P.alloc("sm%d" % i, [1], F32) for i in range(2)]
        names = ["attn_lambda_q1", "attn_lambda_k1", "attn_lambda_q2", "attn_lambda_k2"]
        for i in range(4):
            P.dma("sp", lt[i], W[names[i]][0:1, :].partition_broadcast(128), "lam%d" % i, w=["lt%d" % i])
        for i in range(2):
            P.op("dve", lambda e, i=i: e.tensor_tensor(out=pr[i], in0=lt[2 * i], in1=lt[2 * i + 1], op=ALU.mult),
                 r=["lt%d" % (2 * i), "lt%d" % (2 * i + 1)], w=["pr%d" % i])
            P.op("dve", lambda e, i=i: e.reduce_sum(sm[i], pr[i], axis=AX.X), r=["pr%d" % i], w=["sm%d" % i])
            P.op("act", lambda e, i=i: e.activation(out=sm[i], in_=sm[i], func=AF.Exp), r=["sm%d" % i], w=["sm%d" % i])
        P.op("dve", lambda e: e.tensor_tensor(out=nlam, in0=sm[1], in1=sm[0], op=ALU.subtract), r=["sm0", "sm1"], w=["nlam"])
        P.op("dve", lambda e: e.tensor_scalar(nlam, nlam, -LAMBDA_INIT, None, op0=ALU.add), r=["nlam"], w=["nlam"])
        P.dma("sp", sg, W["attn_subln_g"][0, :].rearrange("(p k) -> p k", k=1), "lam4", w=["sg"], slow=True)
        P.op("dve", lambda e: e.tensor_scalar(sg, sg, 1.0 - LAMBDA_INIT, None, op0=ALU.mult), r=["sg"], w=["sg"])
        P.barrier()

    def phase_weights():
        P.reset()
        FW = 3072
        NS_ = 3
        s32 = [P.alloc("w32_%d" % i, [FW], F32) for i in range(NS_)]
        s16 = [P.alloc("w16_%d" % i, [FW], BF16) for i in range(NS_)]
        cnt = [0]
        cengs = ["dve", "pool", "act"]

        def piece(src, dst, fw_):
            i = cnt[0] % NS_
            ce = cengs[cnt[0] % 3]
            cnt[0] += 1
            a32 = s32[i][:, 0:fw_]
            a16 = s16[i][:, 0:fw_]
            P.dma("sp", a32, src, "w32_%d" % i, w=["w32_%d" % i])
            if ce == "act":
                P.op("act", lambda e: e.activation(out=a16, in_=a32, func=AF.Copy), r=["w32_%d" % i], w=["w16_%d" % i])
            else:
                P.op(ce, lambda e: e.tensor_copy(a16, a32), r=["w32_%d" % i], w=["w16_%d" % i])
            if len(dst.shape) == 3:
                a16v = a16.rearrange("p (g k) -> p g k", k=dst.shape[2])
            else:
                a16v = a16
            P.dma("pool", dst, a16v, "w16_%d" % i, r=["w16_%d" % i], w=["wscr"])

        def std(src2d, dst, C, Fo):
            for c in range(C):
                for f0 in range(0, Fo, FW):
                    fw_ = min(FW, Fo - f0)
                    piece(src2d[c * 128:(c + 1) * 128, f0:f0 + fw_], dst[:, c, f0:f0 + fw_], fw_)

        std(W["attn_w_qkv"][0], wqkv_b, 8, 3 * D)
        std(W["attn_w_o"][0], wo_b, 8, D)
        for i in range(2):
            std(W["xattn_w_kv"][i], xkv_b[i], 8, 2 * D)
        for i in range(2):
            std(W["xattn_w_q"][i], xq_b[i], 8, D)
            std(W["xattn_w_o"][i], xo_b[i], 8, D)
        for i in range(2):
            for c in range(8):
                for half in range(2):
                    src = W["ffn_w_up"][i][c * 128:(c + 1) * 128, half * FF:(half + 1) * FF]
                    dst = wup_b[i][half * 11:(half + 1) * 11, :, c, :].rearrange("g p k -> p g k")
                    piece(src, dst, FF)
            std(W["ffn_w_down"][i], wdn_b[i], NFC, D)
        std(W["fnet_w_in"][0], fin_b, 8, D)
        std(W["fnet_w_out"][0], fout_b, 8, D)
        P.barrier()

    def load_gB(gB, g_ap_row, sem):
        P.dma("sp", gB, g_ap_row.partition_broadcast(128), sem, w=["gB"])

    def rstd_ops(ss, rstd, n, kss, krs, dim, eps_ap, keps):
        P.op("act", lambda e: e.activation(out=rstd[:, 0:n], in_=ss[:, 0:n], func=AF.Ln, bias=eps_ap, scale=1.0 / dim),
             r=[kss, keps], w=[krs])
        P.op("act", lambda e: e.activation(out=rstd[:, 0:n], in_=rstd[:, 0:n], func=AF.Exp, scale=-0.5), r=[krs], w=[krs])

    def norm_transpose(xs, kxs, rs, krs, gB, h0r, psT_i, dst, kdst, mul_eng="dve"):
        h0, kh0 = h0r.next()
        P.op(mul_eng, lambda e: e.scalar_tensor_tensor(out=h0, in0=xs, scalar=rs, in1=gB, op0=ALU.mult, op1=ALU.mult),
             r=[kxs, krs, "gB"], w=[kh0])
        psT = bank[psT_i].bitcast(BF16)
        for c in range(8):
            P.op("pe", lambda e, c=c: e.transpose(psT[:, c * 128:(c + 1) * 128], h0[:, c * 128:(c + 1) * 128], ident),
                 r=[kh0, "ident"], w=[bk(psT_i)])
        P.op("act", lambda e: e.activation(out=dst, in_=psT.rearrange("p (c t) -> p c t", t=128), func=AF.Copy),
             r=[bk(psT_i)], w=[kdst])

    class EpiBufs:
        def __init__(self, nx=2, nh=2):
            self.xr = Ring([(P.alloc("xt%d" % i, [4, D], F32), "xt%d" % i) for i in range(nx)])
            self.hr = Ring([(P.alloc("hTs%d" % i, [8, 512], BF16), "hTs%d" % i) for i in range(nh)])
            self.h0r = Ring([(P.alloc("h0_%d" % i, [D], BF16), "h0_%d" % i) for i in range(2)])
            self.ss = P.alloc("ss", [4], F32)
            self.rstd = P.alloc("rstd", [4], F32)
            self.ssr = Ring([(P.alloc("ssr%d" % i, [4], F32), "ssr%d" % i) for i in range(2)])
            self.rsr = Ring([(P.alloc("rsr%d" % i, [4], F32), "rsr%d" % i) for i in range(2)])
            self.junk = P.alloc("junk", [D], BF16)
            self.gB = P.alloc("gB", [D], F32)

    def epilogue(E, t0, ps_fn, x_in, x_out, kxo, hT_out, kho, y_out, pairs, psT_i):
        psT_list = list(psT_i) if isinstance(psT_i, (list, tuple)) else [psT_i]
        xt, kx = E.xr.next()
        kxs = [(kx, s) for s in range(4)]
        P.dma("sp", xt, x_in[t0:t0 + 512, :].rearrange("(s p) d -> p s d", p=128), kx, w=kxs)
        ss, kss = E.ssr.next()
        rstd, krs = E.rsr.next()
        P.op("pool", lambda e: e.memset(ss, 0.0), w=[kss])
        for st in range(4):
            bA, bB = ps_fn(st, pairs[st % 2])
            P.op("dve", lambda e, st=st, bA=bA: e.tensor_tensor(out=xt[:, st, 0:512], in0=bank[bA], in1=xt[:, st, 0:512], op=ALU.add),
                 r=[bk(bA), kxs[st]], w=[kxs[st]])
            P.op("dve", lambda e, st=st, bB=bB: e.tensor_tensor(out=xt[:, st, 512:1024], in0=bank[bB], in1=xt[:, st, 512:1024], op=ALU.add),
                 r=[bk(bB), kxs[st]], w=[kxs[st]])
            P.op("act", lambda e, st=st: e.activation(out=E.junk, in_=xt[:, st, :], func=AF.Square, accum_out=ss[:, st:st + 1]),
                 r=[kxs[st]], w=["junk", kss])
        rstd_ops(ss, rstd, 4, kss, krs, D, epsD, "epsD")
        if x_out is not None:
            P.dma("pool", x_out[t0:t0 + 512, :].rearrange("(s p) d -> p s d", p=128), xt, kx + "s", r=kxs, w=[kxo])

        def part2():
            if hT_out is not None:
                hTs, khs = E.hr.next()
                for st in range(4):
                    norm_transpose(xt[:, st, :], kxs[st], rstd[:, st:st + 1], krs, E.gB, E.h0r, psT_list[st % len(psT_list)],
                                   hTs[:, :, st * 128:(st + 1) * 128], khs)
                P.dma("pool", hT_out.rearrange("(c p) s -> p c s", p=128)[:, :, 1 + t0:1 + t0 + 512], hTs, khs + "s",
                      r=[khs], w=[kho])
            else:
                for st in range(4):
                    P.op("dve", lambda e, st=st: e.scalar_tensor_tensor(out=xt[:, st, :], in0=xt[:, st, :], scalar=rstd[:, st:st + 1],
                                                                        in1=E.gB, op0=ALU.mult, op1=ALU.mult),
                         r=[kxs[st], krs, "gB"], w=[kxs[st]])
                P.dma("pool", y_out[t0:t0 + 512, :].rearrange("(s p) d -> p s d", p=128), xt, kx + "s", r=kxs, w=["y"])
        return part2

    def proj_psfn(actT, kact, wres, kw):
        def f(st, pair):
            for half in range(2):
                b = pair[half]
                for c in range(8):
                    P.mm(bank[b], actT[:, c, st * 128:(st + 1) * 128], wres[:, c, half * 512:(half + 1) * 512],
                         c == 0, c == 7, [kact, kw], bk(b))
            return pair
        return f

    def hT_window(hT_src, t0):
        return hT_src.rearrange("(c p) s -> p c s", p=128)[:, :, 1 + t0:1 + t0 + 512]

    def phase_mem():
        P.reset()
        wkv = P.alloc("wkv", [8, 2 * D], BF16)
        gB = P.alloc("gB", [D], F32)
        mt = Ring([(P.alloc("mt%d" % i, [2, D], F32), "mt%d" % i) for i in range(2)])
        memT = Ring([(P.alloc("memT%d" % i, [8, MEMT], BF16), "memT%d" % i) for i in range(2)])
        kxs = Ring([(P.alloc("kxs%d" % i, [8, MEMT], BF16), "kxs%d" % i) for i in range(2)])
        vxs = Ring([(P.alloc("vxs%d" % i, [2, D], BF16), "vxs%d" % i) for i in range(2)])
        h0r = Ring([(P.alloc("h0_%d" % i, [D], BF16), "h0_%d" % i) for i in range(2)])
        ss = P.alloc("ss", [4], F32)
        rstd = P.alloc("rstd", [4], F32)
        junk = P.alloc("junk", [D], BF16)
        bi = [0]
        for li in range(2):
            P.dma("sp", wkv, xkv_b[li], "wkv", r=["wscr"], w=["wkv"])
            load_gB(gB, W["norm_mem_g"][li:li + 1, :], "gB")
            for j in range(nseq):
                m, km = mt.next()
                P.dma("sp", m, mem[j * MEMT:(j + 1) * MEMT, :].rearrange("(s p) d -> p s d", p=128), km, w=[km])
                P.op("pool", lambda e: e.memset(ss, 0.0), w=["ss"])
                for st in range(2):
                    P.op("act", lambda e, st=st, m=m: e.activation(out=junk, in_=m[:, st, :], func=AF.Square, accum_out=ss[:, st:st + 1]),
                         r=[km], w=["junk", "ss"])
                rstd_ops(ss, rstd, 2, "ss", "rstd", D, epsD, "epsD")
                mT, kmT = memT.next()
                for st in range(2):
                    norm_transpose(m[:, st, :], km, rstd[:, st:st + 1], "rstd", gB, h0r, 7, mT[:, :, st * 128:(st + 1) * 128], kmT)
                kx, kkx = kxs.next()
                for fo in range(8):
                    b = bi[0] % 4
                    bi[0] += 1
                    for c in range(8):
                        P.mm(bank[b][:, 0:MEMT], wkv[:, c, fo * 128:(fo + 1) * 128], mT[:, c, :], c == 0, c == 7, ["wkv", kmT], bk(b))
                    P.op("act" if fo % 2 == 0 else "dve",
                         (lambda e, b=b, fo=fo, kx=kx: e.activation(out=kx[:, fo, :], in_=bank[b][:, 0:MEMT], func=AF.Copy)) if fo % 2 == 0 else
                         (lambda e, b=b, fo=fo, kx=kx: e.tensor_copy(kx[:, fo, :], bank[b][:, 0:MEMT])),
                         r=[bk(b)], w=[kkx])
                P.dma("pool", memK[li * nseq + j], kx, kkx + "s", r=[kkx], w=["memK"])
                vx, kvx = vxs.next()
                for st in range(2):
                    for half in range(2):
                        b = bi[0] % 4
                        bi[0] += 1
                        for c in range(8):
                            P.mm(bank[b], mT[:, c, st * 128:(st + 1) * 128], wkv[:, c, D + half * 512:D + (half + 1) * 512],
                                 c == 0, c == 7, ["wkv", kmT], bk(b))
                        if half == 0:
                            P.op("act", lambda e, b=b, st=st, vx=vx: e.activation(out=vx[:, st, 0:512], in_=bank[b], func=AF.Copy),
                                 r=[bk(b)], w=[kvx])
                        else:
                            P.op("dve", lambda e, b=b, st=st, vx=vx: e.tensor_copy(vx[:, st, 512:1024], bank[b]), r=[bk(b)], w=[kvx])
                P.dma("pool", memV[li * nseq + j], vx, kvx + "s", r=[kvx], w=["memV"])
        P.barrier()

    def phase_A(xin, S, hT_out, kho):
        P.reset()
        E = EpiBufs()
        load_gB(E.gB, W["norm_mix_g"][0:1, :], "gB")
        for t0 in range(0, S, 512):
            xt, kx = E.xr.next()
            P.dma("sp", xt, xin[t0:t0 + 512, :].rearrange("(s p) d -> p s d", p=128), kx, w=[kx])
            P.op("pool", lambda e: e.memset(E.ss, 0.0), w=["ss"])
            for st in range(4):
                P.op("act", lambda e, st=st, xt=xt: e.activation(out=E.junk, in_=xt[:, st, :], func=AF.Square, accum_out=E.ss[:, st:st + 1]),
                     r=[kx], w=["junk", "ss"])
            rstd_ops(E.ss, E.rstd, 4, "ss", "rstd", D, epsD, "epsD")
            hTs, khs = E.hr.next()
            for st in range(4):
                norm_transpose(xt[:, st, :], kx, E.rstd[:, st:st + 1], "rstd", E.gB, E.h0r, 4 + (st % 2),
                               hTs[:, :, st * 128:(st + 1) * 128], khs)
            P.dma("pool", hT_window(hT_out, t0), hTs, khs + "s", r=[khs], w=[kho])
        P.barrier()

    def phase_B(S, hT_in, khi):
        P.reset()
        wqkv = P.alloc("wqkv", [8, 3 * D], BF16)
        cosT = P.alloc("cosT", [S], F32)
        sinT = P.alloc("sinT", [S], F32)
        hw = Ring([(P.alloc("hTw%d" % i, [8, 512], BF16), "hTw%d" % i) for i in range(2)])
        qs = Ring([(P.alloc("qTs%d" % i, [8, 512], BF16), "qTs%d" % i) for i in range(2)])
        ks = Ring([(P.alloc("kTs%d" % i, [8, 512], BF16), "kTs%d" % i) for i in range(2)])
        vs = Ring([(P.alloc("vs%d" % i, [4, D], BF16), "vs%d" % i) for i in range(2)])
        qb = Ring([(P.alloc("qb%d" % i, [512], BF16), "qb%d" % i) for i in range(3)])
        t1r = Ring([(P.alloc("t1_%d" % i, [512], F32), "t1_%d" % i) for i in range(3)])
        t2r = Ring([(P.alloc("t2_%d" % i, [512], F32), "t2_%d" % i) for i in range(3)])
        P.dma("sp", wqkv, wqkv_b, "wqkv", r=["wscr"], w=["wqkv"])
        P.dma("sp", cosT, c_cos[:, 0:S], "cosT", w=["cosT"])
        P.dma("sp", sinT, c_sin[:, 0:S], "sinT", w=["sinT"])
        bi = [0]
        for t0 in range(0, S, 512):
            hTw, khw = hw.next()
            P.dma("sp", hTw, hT_window(hT_in, t0), khw, r=[khi], w=[khw])
            qTs, kqs = qs.next()
            kTs, kks = ks.next()
            for fo in range(16):
                if "qk" in SKIP:
                    break
                isq = fo < 8
                b = bi[0] % 3
                br = 3 + bi[0] % 3
                bi[0] += 1
                for c in range(8):
                    P.mm(bank[b], wqkv[:, c, fo * 128:(fo + 1) * 128], hTw[:, c, :], c == 0, c == 7, ["wqkv", khw], bk(b))
                if "qk_act" in SKIP:
                    continue
                q16, kq16 = qb.next()
                P.op("act", lambda e, b=b, q16=q16: e.activation(out=q16, in_=bank[b], func=AF.Copy), r=[bk(b)], w=[kq16])
                if "rot" not in SKIP:
                    P.mm(bank[br], rot, q16, True, True, ["rot", kq16], bk(br))
                if "qk_dve" in SKIP:
                    continue
                t1, kt1 = t1r.next()
                t2, kt2 = t2r.next()
                sc = cq if isq else ck
                P.op("dve", lambda e, b=b, t1=t1, t0=t0: e.tensor_tensor(out=t1, in0=bank[b], in1=cosT[:, t0:t0 + 512], op=ALU.mult),
                     r=[bk(b), "cosT"], w=[kt1])
                if "qk_t2" in SKIP:
                    continue
                P.op("dve", lambda e, br=br, t2=t2, t0=t0: e.tensor_tensor(out=t2, in0=bank[br], in1=sinT[:, t0:t0 + 512], op=ALU.mult),
                     r=[bk(br), "sinT"], w=[kt2])
                if "qk_add" in SKIP:
                    continue
                dst, kd = (qTs[:, fo, :], kqs) if isq else (kTs[:, fo - 8, :], kks)
                P.op("dve" if "pooladd" in SKIP else "pool", lambda e, t1=t1, t2=t2, dst=dst: e.tensor_tensor(out=dst, in0=t1, in1=t2, op=ALU.add), r=[kt1, kt2], w=[kd])
            if "qkstore" not in SKIP:
                P.dma("pool", qT.rearrange("(h p) s -> p h s", p=128)[:, :, t0:t0 + 512], qTs, kqs + "s", r=[kqs], w=["qT"])
                P.dma("pool", kT.rearrange("(h p) s -> p h s", p=128)[:, :, t0:t0 + 512], kTs, kks + "s", r=[kks], w=["kT"])
            vt, kv = vs.next()
            for st in range(4):
                if "v" in SKIP:
                    break
                for half in range(2):
                    b = 6 + half
                    for c in range(8):
                        P.mm(bank[b], hTw[:, c, st * 128:(st + 1) * 128], wqkv[:, c, 2 * D + half * 512:2 * D + (half + 1) * 512],
                             c == 0, c == 7, ["wqkv", khw], bk(b))
                    if half == 0:
                        P.op("act", lambda e, b=b, st=st, vt=vt: e.activation(out=vt[:, st, 0:512], in_=bank[b], func=AF.Copy), r=[bk(b)], w=[kv])
                    else:
                        P.op("dve", lambda e, b=b, st=st, vt=vt: e.tensor_copy(vt[:, st, 512:1024], bank[b]), r=[bk(b)], w=[kv])
            kt0 = t0 // 128
            for h in range(NH):
                if "vstore" in SKIP:
                    break
                P.dma("pool", v2[h, :, kt0:kt0 + 4, :], vt[:, :, h * 128:(h + 1) * 128], kv + "s", r=[kv], w=["v2"])
        P.barrier()

    def phase_C(S, xin, x_out, kxo, hT_out, kho):
        P.reset()
        NKT = S // 128
        E = EpiBufs(nx=1, nh=1)
        load_gB(E.gB, W["norm_xattn_g"][0:1, :], "gB")
        wo = P.alloc("wo", [8, D], BF16)
        P.dma("sp", wo, wo_b, "wo", r=["wscr"], w=["wo"])
        qw = Ring([(P.alloc("qTw%d" % i, [8, 512], BF16), "qTw%d" % i) for i in range(2)])
        kvr = Ring([((P.alloc("KT%d" % i, [S], BF16), P.alloc("Vh%d" % i, [NKT, 128], BF16)), "KV%d" % i) for i in range(2)])
        pr = [Ring([(P.alloc("pT%d_%d" % (c, i), [512], BF16), "pT%d_%d" % (c, i)) for i in range(4)]) for c in range(2)]
        oallr = Ring([(P.alloc("oall%d" % i, [8, 512], F32), "oall%d" % i) for i in range(2)])
        oT = P.alloc("oT", [8, 512], BF16)
        oc0 = P.alloc("oc0", [512], F32)
        oc1 = P.alloc("oc1", [512], F32)
        sc0 = P.alloc("sc0", [512], F32)
        sc1 = P.alloc("sc1", [512], F32)
        acc0 = P.alloc("acc0", [512], F32)
        acc1 = P.alloc("acc1", [512], F32)
        acc1p = P.alloc("acc1p", [512], F32)
        pend1 = [None]
        pend2 = [None]
        r0 = P.alloc("r0", [512], F32)
        r1 = P.alloc("r1", [512], F32)
        ta = P.alloc("ta", [512], F32)
        tb = P.alloc("tb", [512], F32)
        sqr = Ring([(P.alloc("sq%d" % i, [512], BF16), "sq%d" % i) for i in range(2)])
        rsr = Ring([(P.alloc("rsn%d" % i, [512], F32), "rsn%d" % i) for i in range(2)])
        SB = [(0, 1), (2, 3)]
        O0, O1, S0, S1 = 4, 5, 6, 7
        for t0 in range(0, S, 512):
            qTw, kqw = qw.next()
            P.dma("sp", qTw, qT.rearrange("(h p) s -> p h s", p=128)[:, :, t0:t0 + 512], kqw, r=["qT"], w=[kqw])
            oall, koall = oallr.next()
            for h in range(NH):
                (KT, Vh), kkv = kvr.next()
                P.dma("sp", KT, kT[h * 128:(h + 1) * 128, 0:S], kkv, r=["kT"], w=[kkv])
                P.dma("sp", Vh, v2[h, :, 0:NKT, :], kkv, r=["v2"], w=[kkv])

                def scores(kt, par, KT=KT, qTw=qTw, kkv=kkv, kqw=kqw, h=h):
                    b0, b1 = SB[par]
                    P.mm(bank[b0], KT[0:64, kt * 128:(kt + 1) * 128], qTw[0:64, h, :], True, True, [kkv, kqw], bk(b0))
                    P.mm(bank[b1], KT[64:128, kt * 128:(kt + 1) * 128], qTw[64:128, h, :], True, True, [kkv, kqw], bk(b1))

                scores(0, 0)
                for kt in range(NKT):
                    par = kt % 2
                    if kt + 1 < NKT:
                        scores(kt + 1, 1 - par)
                    b0, b1 = SB[par]
                    p0, kp0 = pr[0].next()
                    p1, kp1 = pr[1].next()
                    P.op("act", lambda e, b0=b0, p0=p0: e.activation(out=p0, in_=bank[b0], func=AF.Exp, scale=0.125), r=[bk(b0)], w=[kp0])
                    P.op("act", lambda e, b1=b1, p1=p1: e.activation(out=p1, in_=bank[b1], func=AF.Exp, scale=0.125), r=[bk(b1)], w=[kp1])
                    st_, sp_ = (kt == 0), (kt == NKT - 1)
                    P.mm(bank[O0], Vh[:, kt, :], p0, st_, sp_, [kkv, kp0], bk(O0))
                    P.mm(bank[O1], Vh[:, kt, :], p1, st_, sp_, [kkv, kp1], bk(O1))
                    if kt % 2 == 0:
                        P.mm(bank[S0], ones, p0, kt == 0, False, ["ones", kp0], bk(S0))
                    elif kt == 1:
                        P.op("dve", lambda e, p0=p0: e.tensor_copy(acc0, p0), r=[kp0], w=["acc0"])
                    else:
                        P.op("dve", lambda e, p0=p0: e.tensor_tensor(out=acc0, in0=acc0, in1=p0, op=ALU.add), r=[kp0, "acc0"], w=["acc0"])
                    if kt == 0:
                        P.op("dve", lambda e, p1=p1: e.tensor_copy(acc1, p1), r=[kp1], w=["acc1"])
                    else:
                        P.op("dve", lambda e, p1=p1: e.tensor_tensor(out=acc1, in0=acc1, in1=p1, op=ALU.add), r=[kp1, "acc1"], w=["acc1"])
                P.mm(bank[S0], ones32, acc0, False, True, ["ones32", "acc0"], bk(S0))
                P.mm(bank[S1], ones32, acc1, True, True, ["ones32", "acc1"], bk(S1))
                P.op("dve", lambda e: e.tensor_copy(oc0, bank[O0]), r=[bk(O0)], w=["oc0"])
                P.op("act", lambda e: e.activation(out=sc0, in_=bank[S0], func=AF.Copy), r=[bk(S0)], w=["sc0"])
                P.op("dve", lambda e: e.tensor_copy(oc1, bank[O1]), r=[bk(O1)], w=["oc1"])
                P.op("act", lambda e: e.activation(out=sc1, in_=bank[S1], func=AF.Copy), r=[bk(S1)], w=["sc1"])
                P.op("dve", lambda e: e.reciprocal(r0, sc0), r=["sc0"], w=["r0"])
                P.op("dve", lambda e: e.reciprocal(r1, sc1), r=["sc1"], w=["r1"])
                P.op("dve", lambda e: e.tensor_tensor(out=ta, in0=oc0, in1=r0, op=ALU.mult), r=["oc0", "r0"], w=["ta"])
                P.op("dve", lambda e: e.tensor_tensor(out=tb, in0=oc1, in1=r1, op=ALU.mult), r=["oc1", "r1"], w=["tb"])
                P.op("dve", lambda e, h=h, oall=oall: e.scalar_tensor_tensor(out=oall[:, h, :], in0=tb, scalar=nlam, in1=ta, op0=ALU.mult, op1=ALU.add),
                     r=["ta", "tb", "nlam"], w=[(koall, h)])
                if h == 0 and pend1[0] is not None:
                    pend2[0] = pend1[0]()
                    pend1[0] = None
                elif h == 1 and pend2[0] is not None:
                    pend2[0]()
                    pend2[0] = None

            def post_heads(t0=t0, oall=oall, koall=koall):
                for h in range(NH):
                    sq, ksq = sqr.next()
                    rsn, krsn = rsr.next()
                    b = h % 4
                    P.op("pool", lambda e, h=h, sq=sq: e.tensor_tensor(out=sq, in0=oall[:, h, :], in1=oall[:, h, :], op=ALU.mult),
                         r=[(koall, h)], w=[ksq])
                    P.mm(bank[b], ones, sq, True, True, ["ones", ksq], bk(b))
                    P.op("act", lambda e, b=b, rsn=rsn: e.activation(out=rsn, in_=bank[b], func=AF.Ln, bias=epsS, scale=1.0 / 128),
                         r=[bk(b), "epsS"], w=[krsn])
                    P.op("act", lambda e, rsn=rsn: e.activation(out=rsn, in_=rsn, func=AF.Exp, scale=-0.5), r=[krsn], w=[krsn])
                    P.op("dve", lambda e, h=h, rsn=rsn: e.scalar_tensor_tensor(out=oT[:, h, :], in0=oall[:, h, :], scalar=sg, in1=rsn,
                                                                              op0=ALU.mult, op1=ALU.mult),
                         r=[(koall, h), "sg", krsn], w=["oT"])
                return epilogue(E, t0, proj_psfn(oT, "oT", wo, "wo"), xin, x_out, kxo, hT_out, kho, None, [(0, 1), (2, 3)], [0, 1, 2, 3])

            pend1[0] = post_heads
        if pend1[0] is not None:
            pend2[0] = pend1[0]()
        if pend2[0] is not None:
            pend2[0]()
        P.barrier()

    def phase_D(S, li, sj, hT_in, khi, xin, kxi, x_out, kxo, hT_out, kho):
        P.reset()
        E = EpiBufs()
        load_gB(E.gB, W["norm_ffn_g"][li:li + 1, :], "gB")
        wq = P.alloc("wq", [8, D], BF16)
        wo = P.alloc("wo", [8, D], BF16)
        Kx = P.alloc("Kx", [8, MEMT], BF16)
        Vx = P.alloc("Vx", [2, D], BF16)
        P.dma("sp", wq, xq_b[li], "wq", r=["wscr"], w=["wq"])
        P.dma("sp", wo, xo_b[li], "wo", r=["wscr"], w=["wo"])
        P.dma("sp", Kx, memK[li * nseq + sj], "Kx", r=["memK"], w=["Kx"])
        P.dma("sp", Vx, memV[li * nseq + sj], "Vx", r=["memV"], w=["Vx"])
        hw = Ring([(P.alloc("hTw%d" % i, [8, 512], BF16), "hTw%d" % i) for i in range(2)])
        qxr = Ring([(P.alloc("qx%d" % i, [8, 512], BF16), "qx%d" % i) for i in range(2)])
        oXr = Ring([(P.alloc("oX%d" % i, [8, 512], BF16), "oX%d" % i) for i in range(2)])
        pmr = Ring([(P.alloc("pm%d" % i, [512], BF16), "pm%d" % i) for i in range(4)])
        rr = Ring([(P.alloc("rx%d" % i, [512], F32), "rx%d" % i) for i in range(2)])
        smr = Ring([(P.alloc("smc%d" % i, [512], F32), "smc%d" % i) for i in range(2)])
        pend = [None]
        for t0 in range(0, S, 512):
            hTw, khw = hw.next()
            P.dma("sp", hTw, hT_window(hT_in, t0), khw, r=[khi], w=[khw])
            qx, kqx = qxr.next()
            for fo in range(8):
                b = fo % 2
                for c in range(8):
                    P.mm(bank[b], wq[:, c, fo * 128:(fo + 1) * 128], hTw[:, c, :], c == 0, c == 7, ["wq", khw], bk(b))
                if fo % 2 == 0:
                    P.op("act", lambda e, b=b, fo=fo, qx=qx: e.activation(out=qx[:, fo, :], in_=bank[b], func=AF.Copy, scale=1.0 / 16),
                         r=[bk(b)], w=[(kqx, fo)])
                else:
                    P.op("dve", lambda e, b=b, fo=fo, qx=qx: e.tensor_scalar(qx[:, fo, :], bank[b], 1.0 / 16, None, op0=ALU.mult),
                         r=[bk(b)], w=[(kqx, fo)])
            if pend[0] is not None:
                pend[0]()
                pend[0] = None
            oX, koX = oXr.next()

            def xscores(h, par, qx=qx, kqx=kqx):
                for mt_ in range(2):
                    b = 2 * par + mt_
                    for j in range(2):
                        P.mm(bank[b], Kx[:, 2 * h + j, mt_ * 128:(mt_ + 1) * 128], qx[:, 2 * h + j, :], j == 0, j == 1,
                             ["Kx", (kqx, 2 * h + j)], bk(b))

            xscores(0, 0)
            for h in range(4):
                par = h % 2
                if h + 1 < 4:
                    xscores(h + 1, 1 - par)
                pms = []
                for mt_ in range(2):
                    b = 2 * par + mt_
                    pm, kpm = pmr.next()
                    P.op("act", lambda e, b=b, pm=pm: e.activation(out=pm, in_=bank[b], func=AF.Exp), r=[bk(b)], w=[kpm])
                    pms.append((pm, kpm))
                for j in range(2):
                    b = 4 + j
                    for mt_ in range(2):
                        P.mm(bank[b], Vx[:, mt_, (2 * h + j) * 128:(2 * h + j + 1) * 128], pms[mt_][0], mt_ == 0, mt_ == 1,
                             ["Vx", pms[mt_][1]], bk(b))
                for mt_ in range(2):
                    P.mm(bank[6], ones, pms[mt_][0], mt_ == 0, mt_ == 1, ["ones", pms[mt_][1]], bk(6))
                rx, krx = rr.next()
                smc, ksmc = smr.next()
                P.op("act", lambda e, smc=smc: e.activation(out=smc, in_=bank[6], func=AF.Copy), r=[bk(6)], w=[ksmc])
                P.op("dve", lambda e, rx=rx, smc=smc: e.reciprocal(rx, smc), r=[ksmc], w=[krx])
                for j in range(2):
                    P.op("dve", lambda e, j=j, h=h, rx=rx, oX=oX: e.tensor_tensor(out=oX[:, 2 * h + j, :], in0=bank[4 + j], in1=rx, op=ALU.mult),
                         r=[bk(4 + j), krx], w=[koX])
            pend[0] = epilogue(E, t0, proj_psfn(oX, koX, wo, "wo"), xin, x_out, kxo, hT_out, kho, None, [(0, 1), (2, 3)], 7)
        if pend[0] is not None:
            pend[0]()
        P.barrier()

    def phase_E(S, li, hT_in, khi, xin, kxi, x_out, kxo, hT_out, kho, y_out, g_row):
        P.reset()
        E = EpiBufs(nx=1, nh=1)
        load_gB(E.gB, g_row, "gB")
        wdn = P.alloc("wdn", [NFC, D], BF16)
        P.dma("sp", wdn, wdn_b[li], "wdn", r=["wscr"], w=["wdn"])
        cw = P.alloc("cw", [3, NFC], F32)
        cb = P.alloc("cb", [NFC], F32)
        for j in range(3):
            P.dma("sp", cw[:, j, :], W["ffn_conv_w"][li, j, :].rearrange("(f p) -> p f", p=128), "cw", w=["cw"], slow=True)
        P.dma("sp", cb, W["ffn_conv_b"][li, :].rearrange("(f p) -> p f", p=128), "cw", w=["cw"], slow=True)
        wur = Ring([(P.alloc("wu%d" % i, [2, 8, 256], BF16), "wu%d" % i) for i in range(3)])
        hw = Ring([(P.alloc("hTh%d" % i, [8, 514], BF16), "hTh%d" % i) for i in range(2)])
        u = P.alloc("u", [NFC, 512], BF16)
        gbr = Ring([(P.alloc("gb%d" % i, [514], F32), "gb%d" % i) for i in range(2)])
        c1r = Ring([(P.alloc("c1_%d" % i, [512], F32), "c1_%d" % i) for i in range(2)])
        c2r = Ring([(P.alloc("c2_%d" % i, [512], F32), "c2_%d" % i) for i in range(2)])
        c3r = Ring([(P.alloc("c3_%d" % i, [512], F32), "c3_%d" % i) for i in range(2)])
        ger = Ring([(P.alloc("ge%d" % i, [512], F32), "ge%d" % i) for i in range(2)])
        bi = [0]
        pend = [None]
        for t0 in range(0, S, 512):
            hTh, khh = hw.next()
            P.dma("sp", hTh, hT_in.rearrange("(c p) s -> p c s", p=128)[:, :, t0:t0 + 514], khh, r=[khi], w=[khh])
            for g in range(11):
                if g == 2 and pend[0] is not None:
                    pend[0]()
                    pend[0] = None
                wu, kwu = wur.next()
                P.dma("sp", wu[:, 0], wup_b[li][g], kwu, r=["wscr"], w=[kwu])
                P.dma("sp", wu[:, 1], wup_b[li][11 + g], kwu, r=["wscr"], w=[kwu])
                for j in range(2):
                    fc = 2 * g + j
                    par = bi[0] % 2
                    bi[0] += 1
                    bG, bV, bH = 2 * par, 2 * par + 1, 4 + par
                    for c in range(8):
                        P.mm(bank[bG], wu[:, 0, c, j * 128:(j + 1) * 128], hTh[:, c, 1:513], c == 0, c == 7, [kwu, khh], bk(bG))
                    for c in range(8):
                        P.mm(bank[bH][:, 0:2], wu[:, 0, c, j * 128:(j + 1) * 128], hTh[:, c, 0:514:513], c == 0, c == 7, [kwu, khh], bk(bH))
                    for c in range(8):
                        P.mm(bank[bV], wu[:, 1, c, j * 128:(j + 1) * 128], hTh[:, c, 1:513], c == 0, c == 7, [kwu, khh], bk(bV))
                    gb, kgb = gbr.next()
                    P.op("act", lambda e, gb=gb, bG=bG: e.activation(out=gb[:, 1:513], in_=bank[bG], func=AF.Copy), r=[bk(bG)], w=[(kgb, 0)])
                    P.op("dve", lambda e, gb=gb, bH=bH: e.tensor_copy(gb[:, 0:514:513], bank[bH][:, 0:2]), r=[bk(bH)], w=[(kgb, 1)])
                    c1, kc1 = c1r.next()
                    c2, kc2 = c2r.next()
                    c3, kc3 = c3r.next()
                    ge, kge = ger.next()
                    P.op("dve", lambda e, gb=gb, c1=c1, fc=fc: e.tensor_scalar(c1, gb[:, 0:512], cw[:, 0, fc:fc + 1], cb[:, fc:fc + 1],
                                                                               op0=ALU.mult, op1=ALU.add),
                         r=[(kgb, 0), (kgb, 1), "cw"], w=[kc1])
                    P.op("dve", lambda e, gb=gb, c1=c1, c2=c2, fc=fc: e.scalar_tensor_tensor(out=c2, in0=gb[:, 1:513], scalar=cw[:, 1, fc:fc + 1], in1=c1,
                                                                                              op0=ALU.mult, op1=ALU.add),
                         r=[(kgb, 0), (kgb, 1), "cw", kc1], w=[kc2])
                    P.op("dve", lambda e, gb=gb, c2=c2, c3=c3, fc=fc: e.scalar_tensor_tensor(out=c3, in0=gb[:, 2:514], scalar=cw[:, 2, fc:fc + 1], in1=c2,
                                                                                              op0=ALU.mult, op1=ALU.add),
                         r=[(kgb, 0), (kgb, 1), "cw", kc2], w=[kc3])
                    P.op("act", lambda e, c3=c3, ge=ge: e.activation(out=ge, in_=c3, func=AF.Gelu), r=[kc3], w=[kge])
                    P.op("dve", lambda e, ge=ge, bV=bV, fc=fc: e.tensor_tensor(out=u[:, fc, :], in0=bank[bV], in1=ge, op=ALU.mult),
                         r=[bk(bV), kge], w=[("u", fc)])

            def ps_fn(st, pair):
                for half in range(2):
                    b = pair[half]
                    for kc in range(NFC):
                        P.mm(bank[b], u[:, kc, st * 128:(st + 1) * 128], wdn[:, kc, half * 512:(half + 1) * 512],
                             kc == 0, kc == NFC - 1, [("u", kc), "wdn"], bk(b))
                return pair

            pend[0] = epilogue(E, t0, ps_fn, xin, x_out, kxo, hT_out, kho, y_out, [(0, 1), (2, 3)], [6, 7])
        if pend[0] is not None:
            pend[0]()
        P.barrier()

    def phase_F1(S, hT_in, khi):
        P.reset()
        win = P.alloc("win", [8, D], BF16)
        cc = P.alloc("cc", [2, 256], BF16)
        sc = P.alloc("sc", [2, 256], BF16)
        P.dma("sp", win, fin_b, "win", r=["wscr"], w=["win"])
        P.dma("sp", cc, c_cc[S].rearrange("(j p) n -> p j n", p=128), "cc", w=["cc"])
        P.dma("sp", sc, c_sc[S].rearrange("(j p) n -> p j n", p=128), "sc", w=["sc"])
        hw = Ring([(P.alloc("hTw%d" % i, [8, 512], BF16), "hTw%d" % i) for i in range(2)])
        uTr = Ring([(P.alloc("uT%d" % i, [8, 512], BF16), "uT%d" % i) for i in range(2)])
        Asr = Ring([(P.alloc("As%d" % i, [4, D], BF16), "As%d" % i) for i in range(2)])
        Bsr = Ring([(P.alloc("Bs%d" % i, [4, D], BF16), "Bs%d" % i) for i in range(2)])
        for t0 in range(0, S, 512):
            hTw, khw = hw.next()
            P.dma("sp", hTw, hT_window(hT_in, t0), khw, r=[khi], w=[khw])
            uT, kuT = uTr.next()
            for fo in range(8):
                b = fo % 2
                for c in range(8):
                    P.mm(bank[b], win[:, c, fo * 128:(fo + 1) * 128], hTw[:, c, :], c == 0, c == 7, ["win", khw], bk(b))
                if fo % 2 == 0:
                    P.op("act", lambda e, b=b, fo=fo, uT=uT: e.activation(out=uT[:, fo, :], in_=bank[b], func=AF.Copy), r=[bk(b)], w=[(kuT, fo)])
                else:
                    P.op("dve", lambda e, b=b, fo=fo, uT=uT: e.tensor_copy(uT[:, fo, :], bank[b]), r=[bk(b)], w=[(kuT, fo)])
            As, kAs = Asr.next()
            Bs, kBs = Bsr.next()
            for st in range(4):
                for (tab, ktab, dst, kd, b0) in ((cc, "cc", As, kAs, 2), (sc, "sc", Bs, kBs, 4)):
                    for g in range(4):
                        b = b0 + g // 2
                        o = (g % 2) * 256
                        for j in range(2):
                            P.mm(bank[b][:, o:o + 256], uT[:, 2 * g + j, st * 128:(st + 1) * 128], tab[:, j, :], j == 0, j == 1,
                                 [(kuT, 2 * g + j), ktab], bk(b))
                    P.op("act", lambda e, b0=b0, dst=dst, st=st: e.activation(out=dst[:, st, 0:512], in_=bank[b0], func=AF.Copy), r=[bk(b0)], w=[kd])
                    P.op("dve", lambda e, b0=b0, dst=dst, st=st: e.tensor_copy(dst[:, st, 512:1024], bank[b0 + 1]), r=[bk(b0 + 1)], w=[kd])
            P.dma("pool", Ad[t0:t0 + 512, :].rearrange("(s p) d -> p s d", p=128), As, kAs + "s", r=[kAs], w=["Ad"])
            P.dma("pool", Bd[t0:t0 + 512, :].rearrange("(s p) d -> p s d", p=128), Bs, kBs + "s", r=[kBs], w=["Bd"])
        P.barrier()

    def phase_F2(S):
        P.reset()
        NKT = S // 128
        rs_ = TAB // S
        G = 4
        Ah = P.alloc("Ah", [NKT, 512], BF16)
        Bh = P.alloc("Bh", [NKT, 512], BF16)
        tr = Ring([((P.alloc("tC%d" % i, [G, 512], BF16), P.alloc("tN%d" % i, [G, 512], BF16)), "tab%d" % i) for i in range(4)])
        fr = Ring([(P.alloc("fs%d" % i, [4, 512], BF16), "fs%d" % i) for i in range(2)])
        CSv = c_CS.rearrange("(k p r) n -> p k r n", p=128, r=rs_)
        NSv = c_NS.rearrange("(k p r) n -> p k r n", p=128, r=rs_)
        pb = [0]
        for half in range(2):
            P.dma("sp", Ah, Ad[0:S, half * 512:(half + 1) * 512].rearrange("(k p) d -> p k d", p=128), "Ah", r=["Ad"], w=["Ah"])
            P.dma("sp", Bh, Bd[0:S, half * 512:(half + 1) * 512].rearrange("(k p) d -> p k d", p=128), "Bh", r=["Bd"], w=["Bh"])
            for s0 in range(0, S, 512):
                base = 4 * (pb[0] % 2)
                pb[0] += 1
                for kg in range(0, NKT, G):
                    (tC, tN), ktab = tr.next()
                    P.dma("sp", tC, CSv[:, kg:kg + G, 0, s0:s0 + 512], ktab, w=[ktab])
                    P.dma("sp", tN, NSv[:, kg:kg + G, 0, s0:s0 + 512], ktab, w=[ktab])
                    for cc_ in range(4):
                        b = base + cc_
                        for k in range(G):
                            kt = kg + k
                            P.mm(bank[b], Ah[:, kt, cc_ * 128:(cc_ + 1) * 128], tC[:, k, :], kt == 0, False, ["Ah", ktab], bk(b))
                            P.mm(bank[b], Bh[:, kt, cc_ * 128:(cc_ + 1) * 128], tN[:, k, :], False, kt == NKT - 1, ["Bh", ktab], bk(b))
                fs, kfs = fr.next()
                for cc_ in range(4):
                    b = base + cc_
                    if cc_ % 2 == 0:
                        P.op("act", lambda e, b=b, cc_=cc_, fs=fs: e.activation(out=fs[:, cc_, :], in_=bank[b], func=AF.Copy), r=[bk(b)], w=[kfs])
                    else:
                        P.op("dve", lambda e, b=b, cc_=cc_, fs=fs: e.tensor_copy(fs[:, cc_, :], bank[b]), r=[bk(b)], w=[kfs])
                P.dma("pool", fTd.rearrange("(c p) s -> p c s", p=128)[:, half * 4:(half + 1) * 4, s0:s0 + 512], fs, kfs + "s",
                      r=[kfs], w=["fTd"])
        P.barrier()

    def phase_F3(S, xin, kxi, x_out, kxo, hT_out, kho):
        P.reset()
        E = EpiBufs()
        load_gB(E.gB, W["norm_xattn_g"][1:2, :], "gB")
        wout = P.alloc("wout", [8, D], BF16)
        P.dma("sp", wout, fout_b, "wout", r=["wscr"], w=["wout"])
        fw = Ring([(P.alloc("fTw%d" % i, [8, 512], BF16), "fTw%d" % i) for i in range(2)])
        pend = [None]
        for t0 in range(0, S, 512):
            fTw, kfw = fw.next()
            P.dma("sp", fTw, fTd.rearrange("(c p) s -> p c s", p=128)[:, :, t0:t0 + 512], kfw, r=["fTd"], w=[kfw])
            p2 = epilogue(E, t0, proj_psfn(fTw, kfw, wout, "wout"), xin, x_out, kxo, hT_out, kho, None, [(0, 1), (2, 3)], [4, 5, 6, 7])
            if pend[0] is not None:
                pend[0]()
            pend[0] = p2
        if pend[0] is not None:
            pend[0]()
        P.barrier()

    phase_lambda()
    if stop_after != "L":
        phase_weights()
    if stop_after not in ("L", "W"):
        phase_mem()
    for sj, S in enumerate(seqs):
        if stop_after in ("L", "W", "M"):
            break
        xin = x[offs[sj]:offs[sj] + S, :]
        yout = y[offs[sj]:offs[sj] + S, :]
        for hTx, kh in ((hTa, "hTa"), (hTb, "hTb")):
            v = hTx.rearrange("(c p) s -> p c s", p=128)
            P.dma("pool", v[:, :, 0:1], zcol, "zc", r=["zcol"], w=[kh], slow=True)
            P.dma("pool", v[:, :, S + 1:S + 2], zcol, "zc", r=["zcol"], w=[kh], slow=True)
        P.barrier()
        phase_A(xin, S, hTa, "hTa")
        if stop_after == "A":
            break
        phase_B(S, hTa, "hTa")
        if stop_after == "B":
            break
        phase_C(S, xin, xa, "xa", hTb, "hTb")
        if stop_after == "C":
            break
        phase_D(S, 0, sj, hTb, "hTb", xa, "xa", xb, "xb", hTa, "hTa")
        if stop_after == "D0":
            break
        phase_E(S, 0, hTa, "hTa", xb, "xb", xa, "xa", hTb, "hTb", None, W["norm_mix_g"][1:2, :])
        if stop_after == "E0":
            break
        phase_F1(S, hTb, "hTb")
        phase_F2(S)
        if stop_after == "F2":
            break
        phase_F3(S, xa, "xa", xb, "xb", hTa, "hTa")
        if stop_after == "F3":
            break
        phase_D(S, 1, sj, hTa, "hTa", xb, "xb", xa, "xa", hTb, "hTb")
        if stop_after == "D1":
            break
        phase_E(S, 1, hTb, "hTb", xa, "xa", None, None, None, None, yout, W["final_norm_g"][0:1, :])
    P.final_wait("pool")
    P.final_wait("sp")
    P.emit()
    stats = P.stats
    P.es.close()
    return nc, stats


def make_consts(seqs, TAB):
    c = {}
    c["c_ident"] = np.eye(128, dtype=np.float32).astype(NPBF)
    c["c_ones"] = np.ones((128, 128), dtype=np.float32).astype(NPBF)
    rot = np.zeros((128, 128), dtype=np.float32)
    for p in range(128):
        blk = (p % 64) // 32
        if blk == 0:
            rot[p + 32, p] = -1.0
        else:
            rot[p - 32, p] = 1.0
    c["c_rot"] = rot.astype(NPBF)
    half = 32
    inv_freq = (10000.0 ** (-(np.arange(0, half, dtype=np.float32)) * 2.0 / 64)).astype(np.float32)
    ang = np.arange(TAB, dtype=np.float32)[:, None] * inv_freq[None, :]
    cos = np.cos(ang).astype(np.float32).T
    sin = np.sin(ang).astype(np.float32).T
    c["c_cos"] = np.ascontiguousarray(np.tile(cos, (4, 1)))
    c["c_sin"] = np.ascontiguousarray(np.tile(sin, (4, 1)))
    k = np.arange(256, dtype=np.int64)
    a256 = 2.0 * np.pi * ((k[:, None] * k[None, :]) % 256) / 256.0
    for S in sorted(set(seqs)):
        scl = 1.0 / math.sqrt(256.0 * S)
        c["c_cc%d" % S] = (np.cos(a256) * scl).astype(np.float32).astype(NPBF)
        c["c_sc%d" % S] = (np.sin(a256) * scl).astype(np.float32).astype(NPBF)
    s = np.arange(TAB, dtype=np.int64)
    aS = (2.0 * np.pi / TAB) * ((s[:, None] * s[None, :]) % TAB).astype(np.float64)
    c["c_CS"] = np.cos(aS).astype(np.float32).astype(NPBF)
    c["c_NS"] = (-np.sin(aS)).astype(np.float32).astype(NPBF)
    return c


_CACHE = {}


def run_cores(xs, mems, weights, seqs, TAB, dbg=False, stop_after=None, ncores=None):
    key = (tuple(seqs), TAB, dbg, stop_after)
    if key not in _CACHE:
        _CACHE[key] = (build(seqs, TAB, dbg=dbg, stop_after=stop_after), make_consts(seqs, TAB))
    (nc, stats), consts = _CACHE[key]
    n = len(xs)
    wd = {}
    for nme, shp in WNAMES:
        wd[nme] = np.ascontiguousarray(np.asarray(weights[nme], dtype=np.float32).reshape(shp))
    in_maps = []
    for i in range(n):
        m = {"x": np.ascontiguousarray(xs[i], dtype=np.float32), "mem": np.ascontiguousarray(mems[i], dtype=np.float32)}
        m.update(wd)
        m.update(consts)
        in_maps.append(m)
    res = run_bass_kernel_spmd(nc, in_maps, core_ids=list(range(n)))
    return res.results


def kernel(x_prompt, x_sample, mem_prompt, mem_sample, **weights):
    x_prompt = np.asarray(x_prompt, dtype=np.float32)
    x_sample = np.asarray(x_sample, dtype=np.float32)
    mem_prompt = np.asarray(mem_prompt, dtype=np.float32)
    mem_sample = np.asarray(mem_sample, dtype=np.float32)
    n = 8
    SP, SS = x_prompt.shape[1], x_sample.shape[1]
    seqs = [SP, SP, SS]
    xs, mems = [], []
    for i in range(n):
        xs.append(np.concatenate([x_prompt[2 * i], x_prompt[2 * i + 1], x_sample[i]], axis=0))
        mems.append(np.concatenate([mem_prompt[2 * i], mem_prompt[2 * i + 1], mem_sample[i]], axis=0))
    res = run_cores(xs, mems, weights, seqs, max(seqs))
    y_prompt = np.empty_like(x_prompt)
    y_sample = np.empty_like(x_sample)
    for i in range(n):
        yy = res[i]["y"]
        y_prompt[2 * i] = yy[0:SP]
        y_prompt[2 * i + 1] = yy[SP:2 * SP]
        y_sample[i] = yy[2 * SP:2 * SP + SS]
    return (y_prompt, y_sample)
```

```python
import math
import os
from contextlib import ExitStack
SKIP = os.environ.get('KSKIP', '').split(',')

import numpy as np
import ml_dtypes

import concourse.bass as bass
import concourse.mybir as mybir
from concourse.bass_utils import run_bass_kernel_spmd

F32 = mybir.dt.float32
BF16 = mybir.dt.bfloat16
U8 = mybir.dt.uint8
AF = mybir.ActivationFunctionType
ALU = mybir.AluOpType
AX = mybir.AxisListType
NPBF = ml_dtypes.bfloat16

ENGS = ("pe", "act", "dve", "pool", "sp")
D = 1024
FF = 2816
NFC = 22
NH = 8
MEMT = 256
EPS = 1e-6
SUBEPS = 1e-5
LAMBDA_INIT = 0.8 - 0.6 * math.exp(-0.3 * 0)
ARENA = 196608 - 2048


def _size(dt):
    return 4 if dt == F32 else (2 if dt == BF16 else 1)


class Ring:
    def __init__(self, items):
        self.items = items
        self.i = 0

    def next(self):
        it = self.items[self.i % len(self.items)]
        self.i += 1
        return it


class Prog:
    def __init__(self, nc):
        self.nc = nc
        self.es = ExitStack()
        self.streams = {e: [] for e in ENGS}
        self.waited = {e: {} for e in ENGS}
        self.res = {}
        self.dma_cnt = {}
        self.dma_sems = {}
        self.multi = set()
        self.bg_sems = set()
        self.eng_sems = {}
        self.arena = self.es.enter_context(nc.sbuf_tensor("arena", [128, ARENA], U8))
        self.ptr = 0
        self.mark = 0
        self.banks = [self.es.enter_context(nc.psum_tensor("bank%d" % i, [128, 512], F32)) for i in range(8)]

    def alloc(self, name, shape, dt):
        n = 1
        for s in shape:
            n *= s
        nb = n * _size(dt)
        off = (self.ptr + 63) // 64 * 64
        assert off + nb <= ARENA, ("SBUF arena overflow", name, off + nb)
        self.ptr = off + nb
        ap = self.arena[:, off:off + nb]
        if dt != U8:
            ap = ap.bitcast(dt)
        if len(shape) == 2:
            ap = ap.rearrange("p (a b) -> p a b", b=shape[1])
        elif len(shape) == 3:
            ap = ap.rearrange("p (a b c) -> p a b c", b=shape[1], c=shape[2])
        return ap

    def set_mark(self):
        self.mark = self.ptr

    def reset(self):
        self.ptr = self.mark

    def _deps(self, eng, is_dma, reads, writes):
        toks = {}

        def need(d):
            for sk, v in d.items():
                if (not is_dma) and sk == ("E", eng) and eng == "pe":
                    continue
                if toks.get(sk, -1) < v:
                    toks[sk] = v

        for k in reads:
            st = self.res.get(k)
            if st:
                need(st["w"])
                if isinstance(k, tuple) and k[0] == "b":
                    for sk, v in st["r"].items():
                        if sk != ("E", eng) and toks.get(sk, -1) < v:
                            toks[sk] = v
        for k in writes:
            st = self.res.get(k)
            if st:
                need(st["r"])
                if k not in self.multi:
                    need(st["w"])
        out = []
        wd = self.waited[eng]
        for sk, v in toks.items():
            if wd.get(sk, -1) >= v:
                continue
            wd[sk] = v
            out.append((sk, v))
        return out

    def _commit(self, tok, reads, writes):
        sk, v = tok
        for k in reads:
            st = self.res.setdefault(k, {"w": {}, "r": {}})
            if st["r"].get(sk, -1) < v:
                st["r"][sk] = v
        for k in writes:
            st = self.res.setdefault(k, {"w": {}, "r": {}})
            if k in self.multi and not st["r"]:
                if st["w"].get(sk, -1) < v:
                    st["w"][sk] = v
            else:
                st["w"] = {sk: v}
                st["r"] = {}

    def _flag(self, waits):
        for sk, v in waits:
            if sk[0] == "E":
                self.streams[sk[1]][v]["flag"] = True

    def op(self, eng, fn, r=(), w=()):
        waits = self._deps(eng, False, r, w)
        idx = len(self.streams[eng])
        self.streams[eng].append({"fn": fn, "waits": waits, "flag": False, "dma": None})
        self._commit((("E", eng), idx), r, w)
        self._flag(waits)

    def mm(self, out, lhsT, rhs, start, stop, r, w):
        self.op("pe", lambda e: e.matmul(out, lhsT, rhs, start=start, stop=stop), r=r, w=[w])

    def dma(self, q, out, in_, sem, r=(), w=(), slow=False):
        waits = self._deps(q, True, r, w)
        c = self.dma_cnt.get(sem, 0) + 1
        self.dma_cnt[sem] = c
        self.streams[q].append({"fn": (out, in_, slow), "waits": waits, "flag": False, "dma": sem})
        self._commit((("D", sem), c * 16), r, w)
        self._flag(waits)

    def barrier(self):
        toks = []
        for e in ENGS:
            last = None
            for i in range(len(self.streams[e]) - 1, -1, -1):
                o = self.streams[e][i]
                if o["dma"] is None and o["fn"] is not None:
                    last = i
                    break
            if last is not None:
                toks.append((("E", e), last))
        for sem, c in self.dma_cnt.items():
            if sem in self.bg_sems:
                continue
            toks.append((("D", sem), c * 16))
        for e in ENGS:
            waits = []
            wd = self.waited[e]
            for sk, v in toks:
                if sk == ("E", e):
                    continue
                if wd.get(sk, -1) >= v:
                    continue
                wd[sk] = v
                waits.append((sk, v))
            self.streams[e].append({"fn": None, "waits": waits, "flag": False, "dma": None})
            self._flag(waits)

    def final_wait(self, q):
        waits = [(("D", sem), c * 16) for sem, c in self.dma_cnt.items()]
        self.streams[q].append({"fn": None, "waits": waits, "flag": False, "dma": None})

    def emit(self):
        nc = self.nc
        es = self.es
        for e in ENGS:
            self.eng_sems[e] = es.enter_context(nc.semaphore("sem_" + e))
        for s in self.dma_cnt:
            self.dma_sems[s] = es.enter_context(nc.semaphore("dsem_" + s))
        vals = {}
        for e in ENGS:
            c = 0
            v = []
            for o in self.streams[e]:
                if o["flag"]:
                    c += 1
                v.append(c)
            vals[e] = v
        self.stats = {e: (len(self.streams[e]), vals[e][-1] if vals[e] else 0) for e in ENGS}
        block = es.enter_context(nc.Block())

        def run(engname, eobj):
            sem_me = self.eng_sems[engname]
            for o in self.streams[engname]:
                for sk, v in o["waits"]:
                    if sk[0] == "E":
                        eobj.wait_ge(self.eng_sems[sk[1]], vals[sk[1]][v])
                    else:
                        eobj.wait_ge(self.dma_sems[sk[1]], v)
                if o["fn"] is None:
                    continue
                if o["dma"] is not None:
                    out, in_, slow = o["fn"]
                    if slow:
                        ins = eobj.dma_start(out=out, in_=in_, allow_slow_non_contiguous=True)
                    else:
                        ins = eobj.dma_start(out=out, in_=in_)
                    ins.then_inc(self.dma_sems[o["dma"]], 16)
                else:
                    ins = o["fn"](eobj)
                    if o["flag"]:
                        ins.then_inc(sem_me, 1)

        @block.tensor
        def _(e):
            run("pe", e)

        @block.scalar
        def _(e):
            run("act", e)

        @block.vector
        def _(e):
            run("dve", e)

        @block.gpsimd
        def _(e):
            run("pool", e)

        @block.sync
        def _(e):
            run("sp", e)


WNAMES = [
    ("norm_mix_g", (2, D)), ("norm_xattn_g", (2, D)), ("norm_mem_g", (2, D)), ("norm_ffn_g", (2, D)),
    ("final_norm_g", (1, D)),
    ("attn_w_qkv", (1, D, 3 * D)), ("attn_lambda_q1", (1, 64)), ("attn_lambda_k1", (1, 64)),
    ("attn_lambda_q2", (1, 64)), ("attn_lambda_k2", (1, 64)), ("attn_subln_g", (1, 128)),
    ("attn_w_o", (1, D, D)), ("fnet_w_in", (1, D, D)), ("fnet_w_out", (1, D, D)),
    ("xattn_w_q", (2, D, D)), ("xattn_w_kv", (2, D, 2 * D)), ("xattn_w_o", (2, D, D)),
    ("ffn_w_up", (2, D, 2 * FF)), ("ffn_conv_w", (2, 3, FF)), ("ffn_conv_b", (2, FF)),
    ("ffn_w_down", (2, FF, D)),
]


def build(seqs, TAB, dbg=False, stop_after=None):
    nseq = len(seqs)
    NT = sum(seqs)
    offs = [sum(seqs[:i]) for i in range(nseq)]
    Smax = max(seqs)
    svals = sorted(set(seqs))
    nc = bass.Bass("TRN2", target_bir_lowering=False)

    def din(name, shape, dt=F32):
        return nc.dram_tensor(name, list(shape), dt, kind="ExternalInput").ap()

    def dscr(name, shape, dt):
        if dbg:
            return nc.dram_tensor(name, list(shape), dt, kind="ExternalOutput").ap()
        return nc.dram_tensor(name, list(shape), dt).ap()

    x = din("x", [NT, D])
    mem = din("mem", [nseq * MEMT, D])
    W = {n: din(n, s) for n, s in WNAMES}
    c_ident = din("c_ident", [128, 128], BF16)
    c_ones = din("c_ones", [128, 128], BF16)
    c_rot = din("c_rot", [128, 128], BF16)
    c_cos = din("c_cos", [128, TAB])
    c_sin = din("c_sin", [128, TAB])
    c_cc = {S: din("c_cc%d" % S, [256, 256], BF16) for S in svals}
    c_sc = {S: din("c_sc%d" % S, [256, 256], BF16) for S in svals}
    c_CS = din("c_CS", [TAB, TAB], BF16)
    c_NS = din("c_NS", [TAB, TAB], BF16)
    y = nc.dram_tensor("y", [NT, D], F32, kind="ExternalOutput").ap()

    wqkv_b = dscr("wqkv_b", [128, 8, 3 * D], BF16)
    wo_b = dscr("wo_b", [128, 8, D], BF16)
    fin_b = dscr("fin_b", [128, 8, D], BF16)
    fout_b = dscr("fout_b", [128, 8, D], BF16)
    xq_b = [dscr("xq_b%d" % i, [128, 8, D], BF16) for i in range(2)]
    xo_b = [dscr("xo_b%d" % i, [128, 8, D], BF16) for i in range(2)]
    xkv_b = [dscr("xkv_b%d" % i, [128, 8, 2 * D], BF16) for i in range(2)]
    wup_b = [dscr("wup_b%d" % i, [22, 128, 8, 256], BF16) for i in range(2)]
    wdn_b = [dscr("wdn_b%d" % i, [128, NFC, D], BF16) for i in range(2)]
    xa = dscr("xa", [Smax, D], F32)
    xb = dscr("xb", [Smax, D], F32)
    hTa = dscr("hTa", [D, Smax + 2], BF16)
    hTb = dscr("hTb", [D, Smax + 2], BF16)
    qT = dscr("qT", [D, Smax], BF16)
    kT = dscr("kT", [D, Smax], BF16)
    v2 = dscr("v2", [NH, 128, Smax // 128, 128], BF16)
    Ad = dscr("Ad", [Smax, D], BF16)
    Bd = dscr("Bd", [Smax, D], BF16)
    fTd = dscr("fTd", [D, Smax], BF16)
    memK = dscr("memK", [2 * nseq, 128, 8, MEMT], BF16)
    memV = dscr("memV", [2 * nseq, 128, 2, D], BF16)

    P = Prog(nc)
    for k in ("xa", "xb", "hTa", "hTb", "qT", "kT", "v2", "Ad", "Bd", "fTd", "memK", "memV", "y", "wscr"):
        P.multi.add(k)
    for k in ["wqkv", "wo", "xkv0", "xkv1", "xq0", "xo0", "xq1", "xo1", "wup0", "wup1", "wdn0", "wdn1", "fin", "fout"]:
        P.multi.add(("wb", k))
    bank = [b[:] for b in P.banks]

    def bk(i):
        return ("b", i)

    ident = P.alloc("ident", [128], BF16)
    ones = P.alloc("ones", [128], BF16)
    rot = P.alloc("rot", [128], BF16)
    epsD = P.alloc("epsD", [1], F32)
    epsS = P.alloc("epsS", [1], F32)
    nlam = P.alloc("nlam", [1], F32)
    sg = P.alloc("sg", [1], F32)
    zcol = P.alloc("zcol", [8, 1], BF16)
    ones32 = P.alloc("ones32", [128], F32)
    cq = P.alloc("cq", [1], F32)
    ck = P.alloc("ck", [1], F32)
    P.dma("sp", ident, c_ident, "c0", w=["ident"])
    P.dma("sp", ones, c_ones, "c1", w=["ones"])
    P.dma("sp", rot, c_rot, "c2", w=["rot"])
    P.op("pool", lambda e: e.memset(epsD, EPS), w=["epsD"])
    P.op("pool", lambda e: e.memset(epsS, SUBEPS), w=["epsS"])
    P.op("pool", lambda e: e.memset(zcol, 0.0), w=["zcol"])
    P.op("pool", lambda e: e.memset(cq, 0.125), w=["cq"])
    P.op("pool", lambda e: e.memset(ones32, 1.0), w=["ones32"])
    P.op("pool", lambda e: e.memset(ck, 1.0), w=["ck"])
    P.set_mark()

    def phase_lambda():
        P.reset()
        lt = [P.alloc("lt%d" % i, [64], F32) for i in range(4)]
        pr = [P.alloc("pr%d" % i, [64], F32) for i in range(2)]
        sm = [P.alloc("sm%d" % i, [1], F32) for i in range(2)]
        names = ["attn_lambda_q1", "attn_lambda_k1", "attn_lambda_q2", "attn_lambda_k2"]
        for i in range(4):
            P.dma("sp", lt[i], W[names[i]][0:1, :].partition_broadcast(128), "lam%d" % i, w=["lt%d" % i])
        for i in range(2):
            P.op("dve", lambda e, i=i: e.tensor_tensor(out=pr[i], in0=lt[2 * i], in1=lt[2 * i + 1], op=ALU.mult),
                 r=["lt%d" % (2 * i), "lt%d" % (2 * i + 1)], w=["pr%d" % i])
            P.op("dve", lambda e, i=i: e.reduce_sum(sm[i], pr[i], axis=AX.X), r=["pr%d" % i], w=["sm%d" % i])
            P.op("act", lambda e, i=i: e.activation(out=sm[i], in_=sm[i], func=AF.Exp), r=["sm%d" % i], w=["sm%d" % i])
        P.op("dve", lambda e: e.tensor_tensor(out=nlam, in0=sm[1], in1=sm[0], op=ALU.subtract), r=["sm0", "sm1"], w=["nlam"])
        P.op("dve", lambda e: e.tensor_scalar(nlam, nlam, -LAMBDA_INIT, None, op0=ALU.add), r=["nlam"], w=["nlam"])
        P.dma("sp", sg, W["attn_subln_g"][0, :].rearrange("(p k) -> p k", k=1), "lam4", w=["sg"], slow=True)
        P.op("dve", lambda e: e.tensor_scalar(sg, sg, 1.0 - LAMBDA_INIT, None, op0=ALU.mult), r=["sg"], w=["sg"])
        P.barrier()

    def phase_weights():
        def cv(key, dst, src):
            P.bg_sems.add("cv_" + key)
            P.dma("pool", dst, src, "cv_" + key, w=[("wb", key)])

        def std(key, src2d, dst, C):
            nsplit = 2 if C == 8 else 2
            per = C // nsplit
            for i in range(nsplit):
                cv(key, dst[:, i * per:(i + 1) * per, :], src2d[i * per * 128:(i + 1) * per * 128, :].rearrange("(c p) f -> p c f", p=128))

        std("wqkv", W["attn_w_qkv"][0], wqkv_b, 8)
        std("wo", W["attn_w_o"][0], wo_b, 8)
        for i in range(2):
            std("xkv%d" % i, W["xattn_w_kv"][i], xkv_b[i], 8)
        for i in range(2):
            std("xq%d" % i, W["xattn_w_q"][i], xq_b[i], 8)
            std("xo%d" % i, W["xattn_w_o"][i], xo_b[i], 8)
            for c in range(8):
                for half in range(2):
                    src = W["ffn_w_up"][i][c * 128:(c + 1) * 128, half * FF:(half + 1) * FF].rearrange("p (g k) -> p g k", k=256)
                    dst = wup_b[i][half * 11:(half + 1) * 11, :, c, :].rearrange("g p k -> p g k")
                    cv("wup%d" % i, dst, src)
            std("wdn%d" % i, W["ffn_w_down"][i], wdn_b[i], NFC)
            if i == 0:
                std("fin", W["fnet_w_in"][0], fin_b, 8)
                std("fout", W["fnet_w_out"][0], fout_b, 8)

    def load_gB(gB, g_ap_row, sem):
        P.dma("sp", gB, g_ap_row.partition_broadcast(128), sem, w=["gB"])

    def rstd_ops(ss, rstd, n, kss, krs, dim, eps_ap, keps):
        P.op("act", lambda e: e.activation(out=rstd[:, 0:n], in_=ss[:, 0:n], func=AF.Ln, bias=eps_ap, scale=1.0 / dim),
             r=[kss, keps], w=[krs])
        P.op("act", lambda e: e.activation(out=rstd[:, 0:n], in_=rstd[:, 0:n], func=AF.Exp, scale=-0.5), r=[krs], w=[krs])

    def norm_transpose(xs, kxs, rs, krs, gB, h0r, psT_i, dst, kdst, mul_eng="dve"):
        h0, kh0 = h0r.next()
        P.op(mul_eng, lambda e: e.scalar_tensor_tensor(out=h0, in0=xs, scalar=rs, in1=gB, op0=ALU.mult, op1=ALU.mult),
             r=[kxs, krs, "gB"], w=[kh0])
        psT = bank[psT_i].bitcast(BF16)
        for c in range(8):
            P.op("pe", lambda e, c=c: e.transpose(psT[:, c * 128:(c + 1) * 128], h0[:, c * 128:(c + 1) * 128], ident),
                 r=[kh0, "ident"], w=[bk(psT_i)])
        P.op("act", lambda e: e.activation(out=dst, in_=psT.rearrange("p (c t) -> p c t", t=128), func=AF.Copy),
             r=[bk(psT_i)], w=[kdst])

    class EpiBufs:
        def __init__(self, nx=2, nh=2):
            self.xr = Ring([(P.alloc("xt%d" % i, [4, D], F32), "xt%d" % i) for i in range(nx)])
            self.hr = Ring([(P.alloc("hTs%d" % i, [8, 512], BF16), "hTs%d" % i) for i in range(nh)])
            self.h0r = Ring([(P.alloc("h0_%d" % i, [D], BF16), "h0_%d" % i) for i in range(2)])
            self.ss = P.alloc("ss", [4], F32)
            self.rstd = P.alloc("rstd", [4], F32)
            self.ssr = Ring([(P.alloc("ssr%d" % i, [4], F32), "ssr%d" % i) for i in range(2)])
            self.rsr = Ring([(P.alloc("rsr%d" % i, [4], F32), "rsr%d" % i) for i in range(2)])
            self.junk = P.alloc("junk", [D], BF16)
            self.gB = P.alloc("gB", [D], F32)

    def epilogue(E, t0, ps_fn, x_in, x_out, kxo, hT_out, kho, y_out, pairs, psT_i):
        psT_list = list(psT_i) if isinstance(psT_i, (list, tuple)) else [psT_i]
        xt, kx = E.xr.next()
        kxs = [(kx, s) for s in range(4)]
        P.dma("sp", xt, x_in[t0:t0 + 512, :].rearrange("(s p) d -> p s d", p=128), kx, w=kxs)
        ss, kss = E.ssr.next()
        rstd, krs = E.rsr.next()
        P.op("pool", lambda e: e.memset(ss, 0.0), w=[kss])
        for st in range(4):
            bA, bB = ps_fn(st, pairs[st % 2])
            P.op("dve", lambda e, st=st, bA=bA: e.tensor_tensor(out=xt[:, st, 0:512], in0=bank[bA], in1=xt[:, st, 0:512], op=ALU.add),
                 r=[bk(bA), kxs[st]], w=[kxs[st]])
            P.op("dve", lambda e, st=st, bB=bB: e.tensor_tensor(out=xt[:, st, 512:1024], in0=bank[bB], in1=xt[:, st, 512:1024], op=ALU.add),
                 r=[bk(bB), kxs[st]], w=[kxs[st]])
            P.op("act", lambda e, st=st: e.activation(out=E.junk, in_=xt[:, st, :], func=AF.Square, accum_out=ss[:, st:st + 1]),
                 r=[kxs[st]], w=["junk", kss])
        rstd_ops(ss, rstd, 4, kss, krs, D, epsD, "epsD")
        if x_out is not None:
            P.dma("pool", x_out[t0:t0 + 512, :].rearrange("(s p) d -> p s d", p=128), xt, kx + "s", r=kxs, w=[kxo])

        def part2():
            if hT_out is not None:
                hTs, khs = E.hr.next()
                for st in range(4):
                    norm_transpose(xt[:, st, :], kxs[st], rstd[:, st:st + 1], krs, E.gB, E.h0r, psT_list[st % len(psT_list)],
                                   hTs[:, :, st * 128:(st + 1) * 128], khs)
                P.dma("pool", hT_out.rearrange("(c p) s -> p c s", p=128)[:, :, 1 + t0:1 + t0 + 512], hTs, khs + "s",
                      r=[khs], w=[kho])
            else:
                for st in range(4):
                    P.op("dve", lambda e, st=st: e.scalar_tensor_tensor(out=xt[:, st, :], in0=xt[:, st, :], scalar=rstd[:, st:st + 1],
                                                                        in1=E.gB, op0=ALU.mult, op1=ALU.mult),
                         r=[kxs[st], krs, "gB"], w=[kxs[st]])
                P.dma("pool", y_out[t0:t0 + 512, :].rearrange("(s p) d -> p s d", p=128), xt, kx + "s", r=kxs, w=["y"])
        return part2

    def proj_psfn(actT, kact, wres, kw):
        def f(st, pair):
            for half in range(2):
                b = pair[half]
                for c in range(8):
                    P.mm(bank[b], actT[:, c, st * 128:(st + 1) * 128], wres[:, c, half * 512:(half + 1) * 512],
                         c == 0, c == 7, [kact, kw], bk(b))
            return pair
        return f

    def hT_window(hT_src, t0):
        return hT_src.rearrange("(c p) s -> p c s", p=128)[:, :, 1 + t0:1 + t0 + 512]

    def phase_mem():
        P.reset()
        wkv = P.alloc("wkv", [8, 2 * D], BF16)
        gB = P.alloc("gB", [D], F32)
        mt = Ring([(P.alloc("mt%d" % i, [2, D], F32), "mt%d" % i) for i in range(2)])
        memT = Ring([(P.alloc("memT%d" % i, [8, MEMT], BF16), "memT%d" % i) for i in range(2)])
        kxs = Ring([(P.alloc("kxs%d" % i, [8, MEMT], BF16), "kxs%d" % i) for i in range(2)])
        vxs = Ring([(P.alloc("vxs%d" % i, [2, D], BF16), "vxs%d" % i) for i in range(2)])
        h0r = Ring([(P.alloc("h0_%d" % i, [D], BF16), "h0_%d" % i) for i in range(2)])
        ss = P.alloc("ss", [4], F32)
        rstd = P.alloc("rstd", [4], F32)
        junk = P.alloc("junk", [D], BF16)
        bi = [0]
        for li in range(2):
            P.dma("sp", wkv, xkv_b[li], "wkv", r=[("wb", "xkv%d" % li)], w=["wkv"])
            load_gB(gB, W["norm_mem_g"][li:li + 1, :], "gB")
            for j in range(nseq):
                m, km = mt.next()
                P.dma("sp", m, mem[j * MEMT:(j + 1) * MEMT, :].rearrange("(s p) d -> p s d", p=128), km, w=[km])
                P.op("pool", lambda e: e.memset(ss, 0.0), w=["ss"])
                for st in range(2):
                    P.op("act", lambda e, st=st, m=m: e.activation(out=junk, in_=m[:, st, :], func=AF.Square, accum_out=ss[:, st:st + 1]),
                         r=[km], w=["junk", "ss"])
                rstd_ops(ss, rstd, 2, "ss", "rstd", D, epsD, "epsD")
                mT, kmT = memT.next()
                for st in range(2):
                    norm_transpose(m[:, st, :], km, rstd[:, st:st + 1], "rstd", gB, h0r, 7, mT[:, :, st * 128:(st + 1) * 128], kmT)
                kx, kkx = kxs.next()
                for fo in range(8):
                    b = bi[0] % 4
                    bi[0] += 1
                    for c in range(8):
                        P.mm(bank[b][:, 0:MEMT], wkv[:, c, fo * 128:(fo + 1) * 128], mT[:, c, :], c == 0, c == 7, ["wkv", kmT], bk(b))
                    P.op("act" if fo % 2 == 0 else "dve",
                         (lambda e, b=b, fo=fo, kx=kx: e.activation(out=kx[:, fo, :], in_=bank[b][:, 0:MEMT], func=AF.Copy)) if fo % 2 == 0 else
                         (lambda e, b=b, fo=fo, kx=kx: e.tensor_copy(kx[:, fo, :], bank[b][:, 0:MEMT])),
                         r=[bk(b)], w=[kkx])
                P.dma("pool", memK[li * nseq + j], kx, kkx + "s", r=[kkx], w=["memK"])
                vx, kvx = vxs.next()
                for st in range(2):
                    for half in range(2):
                        b = bi[0] % 4
                        bi[0] += 1
                        for c in range(8):
                            P.mm(bank[b], mT[:, c, st * 128:(st + 1) * 128], wkv[:, c, D + half * 512:D + (half + 1) * 512],
                                 c == 0, c == 7, ["wkv", kmT], bk(b))
                        if half == 0:
                            P.op("act", lambda e, b=b, st=st, vx=vx: e.activation(out=vx[:, st, 0:512], in_=bank[b], func=AF.Copy),
                                 r=[bk(b)], w=[kvx])
                        else:
                            P.op("dve", lambda e, b=b, st=st, vx=vx: e.tensor_copy(vx[:, st, 512:1024], bank[b]), r=[bk(b)], w=[kvx])
                P.dma("pool", memV[li * nseq + j], vx, kvx + "s", r=[kvx], w=["memV"])
        P.barrier()

    def phase_A(xin, S, hT_out, kho):
        P.reset()
        E = EpiBufs()
        load_gB(E.gB, W["norm_mix_g"][0:1, :], "gB")
        for t0 in range(0, S, 512):
            xt, kx = E.xr.next()
            P.dma("sp", xt, xin[t0:t0 + 512, :].rearrange("(s p) d -> p s d", p=128), kx, w=[kx])
            P.op("pool", lambda e: e.memset(E.ss, 0.0), w=["ss"])
            for st in range(4):
                P.op("act", lambda e, st=st, xt=xt: e.activation(out=E.junk, in_=xt[:, st, :], func=AF.Square, accum_out=E.ss[:, st:st + 1]),
                     r=[kx], w=["junk", "ss"])
            rstd_ops(E.ss, E.rstd, 4, "ss", "rstd", D, epsD, "epsD")
            hTs, khs = E.hr.next()
            for st in range(4):
                norm_transpose(xt[:, st, :], kx, E.rstd[:, st:st + 1], "rstd", E.gB, E.h0r, 4 + (st % 2),
                               hTs[:, :, st * 128:(st + 1) * 128], khs)
            P.dma("pool", hT_window(hT_out, t0), hTs, khs + "s", r=[khs], w=[kho])
        P.barrier()

    def phase_B(S, hT_in, khi):
        P.reset()
        wqkv = P.alloc("wqkv", [8, 3 * D], BF16)
        cosT = P.alloc("cosT", [S], F32)
        sinT = P.alloc("sinT", [S], F32)
        hw = Ring([(P.alloc("hTw%d" % i, [8, 512], BF16), "hTw%d" % i) for i in range(2)])
        qs = Ring([(P.alloc("qTs%d" % i, [8, 512], BF16), "qTs%d" % i) for i in range(2)])
        ks = Ring([(P.alloc("kTs%d" % i, [8, 512], BF16), "kTs%d" % i) for i in range(2)])
        vs = Ring([(P.alloc("vs%d" % i, [4, D], BF16), "vs%d" % i) for i in range(2)])
        qb = Ring([(P.alloc("qb%d" % i, [512], BF16), "qb%d" % i) for i in range(3)])
        t1r = Ring([(P.alloc("t1_%d" % i, [512], F32), "t1_%d" % i) for i in range(3)])
        t2r = Ring([(P.alloc("t2_%d" % i, [512], F32), "t2_%d" % i) for i in range(3)])
        P.dma("sp", wqkv, wqkv_b, "wqkv", r=[("wb", "wqkv")], w=["wqkv"])
        P.dma("sp", cosT, c_cos[:, 0:S], "cosT", w=["cosT"])
        P.dma("sp", sinT, c_sin[:, 0:S], "sinT", w=["sinT"])
        bi = [0]
        for t0 in range(0, S, 512):
            hTw, khw = hw.next()
            P.dma("sp", hTw, hT_window(hT_in, t0), khw, r=[khi], w=[khw])
            qTs, kqs = qs.next()
            kTs, kks = ks.next()
            for fo in range(16):
                if "qk" in SKIP:
                    break
                isq = fo < 8
                b = bi[0] % 3
                br = 3 + bi[0] % 3
                bi[0] += 1
                for c in range(8):
                    P.mm(bank[b], wqkv[:, c, fo * 128:(fo + 1) * 128], hTw[:, c, :], c == 0, c == 7, ["wqkv", khw], bk(b))
                if "qk_act" in SKIP:
                    continue
                q16, kq16 = qb.next()
                P.op("act", lambda e, b=b, q16=q16: e.activation(out=q16, in_=bank[b], func=AF.Copy), r=[bk(b)], w=[kq16])
                if "rot" not in SKIP:
                    P.mm(bank[br], rot, q16, True, True, ["rot", kq16], bk(br))
                if "qk_dve" in SKIP:
                    continue
                t1, kt1 = t1r.next()
                t2, kt2 = t2r.next()
                sc = cq if isq else ck
                P.op("dve", lambda e, b=b, t1=t1, t0=t0: e.tensor_tensor(out=t1, in0=bank[b], in1=cosT[:, t0:t0 + 512], op=ALU.mult),
                     r=[bk(b), "cosT"], w=[kt1])
                if "qk_t2" in SKIP:
                    continue
                P.op("dve", lambda e, br=br, t2=t2, t0=t0: e.tensor_tensor(out=t2, in0=bank[br], in1=sinT[:, t0:t0 + 512], op=ALU.mult),
                     r=[bk(br), "sinT"], w=[kt2])
                if "qk_add" in SKIP:
                    continue
                dst, kd = (qTs[:, fo, :], kqs) if isq else (kTs[:, fo - 8, :], kks)
                P.op("dve", lambda e, t1=t1, t2=t2, dst=dst: e.tensor_tensor(out=dst, in0=t1, in1=t2, op=ALU.add), r=[kt1, kt2], w=[kd])
            if "qkstore" not in SKIP:
                P.dma("pool", qT.rearrange("(h p) s -> p h s", p=128)[:, :, t0:t0 + 512], qTs, kqs + "s", r=[kqs], w=["qT"])
                P.dma("pool", kT.rearrange("(h p) s -> p h s", p=128)[:, :, t0:t0 + 512], kTs, kks + "s", r=[kks], w=["kT"])
            vt, kv = vs.next()
            for st in range(4):
                if "v" in SKIP:
                    break
                for half in range(2):
                    b = 6 + half
                    for c in range(8):
                        P.mm(bank[b], hTw[:, c, st * 128:(st + 1) * 128], wqkv[:, c, 2 * D + half * 512:2 * D + (half + 1) * 512],
                             c == 0, c == 7, ["wqkv", khw], bk(b))
                    if half == 0:
                        P.op("act", lambda e, b=b, st=st, vt=vt: e.activation(out=vt[:, st, 0:512], in_=bank[b], func=AF.Copy), r=[bk(b)], w=[kv])
                    else:
                        P.op("dve", lambda e, b=b, st=st, vt=vt: e.tensor_copy(vt[:, st, 512:1024], bank[b]), r=[bk(b)], w=[kv])
            kt0 = t0 // 128
            for h in range(NH):
                if "vstore" in SKIP:
                    break
                P.dma("pool", v2[h, :, kt0:kt0 + 4, :], vt[:, :, h * 128:(h + 1) * 128], kv + "s", r=[kv], w=["v2"])
        P.barrier()

    def phase_C(S, xin, x_out, kxo, hT_out, kho):
        P.reset()
        NKT = S // 128
        E = EpiBufs(nx=1, nh=1)
        load_gB(E.gB, W["norm_xattn_g"][0:1, :], "gB")
        wo = P.alloc("wo", [8, D], BF16)
        P.dma("sp", wo, wo_b, "wo", r=[("wb", "wo")], w=["wo"])
        qw = Ring([(P.alloc("qTw%d" % i, [8, 512], BF16), "qTw%d" % i) for i in range(2)])
        kvr = Ring([((P.alloc("KT%d" % i, [S], BF16), P.alloc("Vh%d" % i, [NKT, 128], BF16)), "KV%d" % i) for i in range(2)])
        pr = [Ring([(P.alloc("pT%d_%d" % (c, i), [512], BF16), "pT%d_%d" % (c, i)) for i in range(4)]) for c in range(2)]
        oallr = Ring([(P.alloc("oall%d" % i, [8, 512], F32), "oall%d" % i) for i in range(2)])
        oT = P.alloc("oT", [8, 512], BF16)
        oc0 = P.alloc("oc0", [512], F32)
        oc1 = P.alloc("oc1", [512], F32)
        sc0 = P.alloc("sc0", [512], F32)
        sc1 = P.alloc("sc1", [512], F32)
        acc0 = P.alloc("acc0", [512], F32)
        acc1 = P.alloc("acc1", [512], F32)
        acc1p = P.alloc("acc1p", [512], F32)
        pend1 = [None]
        pend2 = [None]
        r0 = P.alloc("r0", [512], F32)
        r1 = P.alloc("r1", [512], F32)
        ta = P.alloc("ta", [512], F32)
        tb = P.alloc("tb", [512], F32)
        sqr = Ring([(P.alloc("sq%d" % i, [512], BF16), "sq%d" % i) for i in range(2)])
        rsr = Ring([(P.alloc("rsn%d" % i, [512], F32), "rsn%d" % i) for i in range(2)])
        SB = [(0, 1), (2, 3)]
        O0, O1, S0, S1 = 4, 5, 6, 7
        for t0 in range(0, S, 512):
            qTw, kqw = qw.next()
            P.dma("sp", qTw, qT.rearrange("(h p) s -> p h s", p=128)[:, :, t0:t0 + 512], kqw, r=["qT"], w=[kqw])
            oall, koall = oallr.next()
            for h in range(NH):
                (KT, Vh), kkv = kvr.next()
                P.dma("sp", KT, kT[h * 128:(h + 1) * 128, 0:S], kkv, r=["kT"], w=[kkv])
                P.dma("sp", Vh, v2[h, :, 0:NKT, :], kkv, r=["v2"], w=[kkv])

                def scores(kt, par, KT=KT, qTw=qTw, kkv=kkv, kqw=kqw, h=h):
                    b0, b1 = SB[par]
                    P.mm(bank[b0], KT[0:64, kt * 128:(kt + 1) * 128], qTw[0:64, h, :], True, True, [kkv, kqw], bk(b0))
                    P.mm(bank[b1], KT[64:128, kt * 128:(kt + 1) * 128], qTw[64:128, h, :], True, True, [kkv, kqw], bk(b1))

                scores(0, 0)
                for kt in range(NKT):
                    par = kt % 2
                    if kt + 1 < NKT:
                        scores(kt + 1, 1 - par)
                    b0, b1 = SB[par]
                    p0, kp0 = pr[0].next()
                    p1, kp1 = pr[1].next()
                    P.op("act", lambda e, b0=b0, p0=p0: e.activation(out=p0, in_=bank[b0], func=AF.Exp, scale=0.125), r=[bk(b0)], w=[kp0])
                    P.op("act", lambda e, b1=b1, p1=p1: e.activation(out=p1, in_=bank[b1], func=AF.Exp, scale=0.125), r=[bk(b1)], w=[kp1])
                    st_, sp_ = (kt == 0), (kt == NKT - 1)
                    P.mm(bank[O0], Vh[:, kt, :], p0, st_, sp_, [kkv, kp0], bk(O0))
                    P.mm(bank[O1], Vh[:, kt, :], p1, st_, sp_, [kkv, kp1], bk(O1))
                    if kt % 2 == 0:
                        P.mm(bank[S0], ones, p0, kt == 0, False, ["ones", kp0], bk(S0))
                    elif kt == 1:
                        P.op("dve", lambda e, p0=p0: e.tensor_copy(acc0, p0), r=[kp0], w=["acc0"])
                    else:
                        P.op("dve", lambda e, p0=p0: e.tensor_tensor(out=acc0, in0=acc0, in1=p0, op=ALU.add), r=[kp0, "acc0"], w=["acc0"])
                    if kt == 0:
                        P.op("dve", lambda e, p1=p1: e.tensor_copy(acc1, p1), r=[kp1], w=["acc1"])
                    else:
                        P.op("dve", lambda e, p1=p1: e.tensor_tensor(out=acc1, in0=acc1, in1=p1, op=ALU.add), r=[kp1, "acc1"], w=["acc1"])
                P.mm(bank[S0], ones32, acc0, False, True, ["ones32", "acc0"], bk(S0))
                P.mm(bank[S1], ones32, acc1, True, True, ["ones32", "acc1"], bk(S1))
                P.op("dve", lambda e: e.tensor_copy(oc0, bank[O0]), r=[bk(O0)], w=["oc0"])
                P.op("act", lambda e: e.activation(out=sc0, in_=bank[S0], func=AF.Ln), r=[bk(S0)], w=["sc0"])
                P.op("dve", lambda e: e.tensor_copy(oc1, bank[O1]), r=[bk(O1)], w=["oc1"])
                P.op("act", lambda e: e.activation(out=sc1, in_=bank[S1], func=AF.Ln), r=[bk(S1)], w=["sc1"])
                P.op("act", lambda e: e.activation(out=r0, in_=sc0, func=AF.Exp, scale=-1.0), r=["sc0"], w=["r0"])
                P.op("act", lambda e: e.activation(out=r1, in_=sc1, func=AF.Exp, scale=-1.0), r=["sc1"], w=["r1"])
                P.op("dve", lambda e: e.tensor_tensor(out=ta, in0=oc0, in1=r0, op=ALU.mult), r=["oc0", "r0"], w=["ta"])
                P.op("dve", lambda e: e.tensor_tensor(out=tb, in0=oc1, in1=r1, op=ALU.mult), r=["oc1", "r1"], w=["tb"])
                P.op("dve", lambda e, h=h, oall=oall: e.scalar_tensor_tensor(out=oall[:, h, :], in0=tb, scalar=nlam, in1=ta, op0=ALU.mult, op1=ALU.add),
                     r=["ta", "tb", "nlam"], w=[(koall, h)])
                if h == 0 and pend1[0] is not None:
                    pend2[0] = pend1[0]()
                    pend1[0] = None
                elif h == 1 and pend2[0] is not None:
                    pend2[0]()
                    pend2[0] = None

            def post_heads(t0=t0, oall=oall, koall=koall):
                for h in range(NH):
                    sq, ksq = sqr.next()
                    rsn, krsn = rsr.next()
                    b = h % 4
                    P.op("pool", lambda e, h=h, sq=sq: e.tensor_tensor(out=sq, in0=oall[:, h, :], in1=oall[:, h, :], op=ALU.mult),
                         r=[(koall, h)], w=[ksq])
                    P.mm(bank[b], ones, sq, True, True, ["ones", ksq], bk(b))
                    P.op("act", lambda e, b=b, rsn=rsn: e.activation(out=rsn, in_=bank[b], func=AF.Ln, bias=epsS, scale=1.0 / 128),
                         r=[bk(b), "epsS"], w=[krsn])
                    P.op("act", lambda e, rsn=rsn: e.activation(out=rsn, in_=rsn, func=AF.Exp, scale=-0.5), r=[krsn], w=[krsn])
                    P.op("dve", lambda e, h=h, rsn=rsn: e.scalar_tensor_tensor(out=oT[:, h, :], in0=oall[:, h, :], scalar=sg, in1=rsn,
                                                                              op0=ALU.mult, op1=ALU.mult),
                         r=[(koall, h), "sg", krsn], w=["oT"])
                return epilogue(E, t0, proj_psfn(oT, "oT", wo, "wo"), xin, x_out, kxo, hT_out, kho, None, [(0, 1), (2, 3)], [0, 1, 2, 3])

            pend1[0] = post_heads
        if pend1[0] is not None:
            pend2[0] = pend1[0]()
        if pend2[0] is not None:
            pend2[0]()
        P.barrier()

    def phase_D(S, li, sj, hT_in, khi, xin, kxi, x_out, kxo, hT_out, kho):
        P.reset()
        E = EpiBufs()
        load_gB(E.gB, W["norm_ffn_g"][li:li + 1, :], "gB")
        wq = P.alloc("wq", [8, D], BF16)
        wo = P.alloc("wo", [8, D], BF16)
        Kx = P.alloc("Kx", [8, MEMT], BF16)
        Vx = P.alloc("Vx", [2, D], BF16)
        P.dma("sp", wq, xq_b[li], "wq", r=[("wb", "xq%d" % li)], w=["wq"])
        P.dma("sp", wo, xo_b[li], "wo", r=[("wb", "xo%d" % li)], w=["wo"])
        P.dma("sp", Kx, memK[li * nseq + sj], "Kx", r=["memK"], w=["Kx"])
        P.dma("sp", Vx, memV[li * nseq + sj], "Vx", r=["memV"], w=["Vx"])
        hw = Ring([(P.alloc("hTw%d" % i, [8, 512], BF16), "hTw%d" % i) for i in range(2)])
        qxr = Ring([(P.alloc("qx%d" % i, [8, 512], BF16), "qx%d" % i) for i in range(2)])
        oXr = Ring([(P.alloc("oX%d" % i, [8, 512], BF16), "oX%d" % i) for i in range(2)])
        pmr = Ring([(P.alloc("pm%d" % i, [512], BF16), "pm%d" % i) for i in range(4)])
        rr = Ring([(P.alloc("rx%d" % i, [512], F32), "rx%d" % i) for i in range(2)])
        smr = Ring([(P.alloc("smc%d" % i, [512], F32), "smc%d" % i) for i in range(2)])
        pend = [None]
        for t0 in range(0, S, 512):
            hTw, khw = hw.next()
            P.dma("sp", hTw, hT_window(hT_in, t0), khw, r=[khi], w=[khw])
            qx, kqx = qxr.next()
            for fo in range(8):
                b = fo % 2
                for c in range(8):
                    P.mm(bank[b], wq[:, c, fo * 128:(fo + 1) * 128], hTw[:, c, :], c == 0, c == 7, ["wq", khw], bk(b))
                if fo % 2 == 0:
                    P.op("act", lambda e, b=b, fo=fo, qx=qx: e.activation(out=qx[:, fo, :], in_=bank[b], func=AF.Copy, scale=1.0 / 16),
                         r=[bk(b)], w=[(kqx, fo)])
                else:
                    P.op("dve", lambda e, b=b, fo=fo, qx=qx: e.tensor_scalar(qx[:, fo, :], bank[b], 1.0 / 16, None, op0=ALU.mult),
                         r=[bk(b)], w=[(kqx, fo)])
            if pend[0] is not None:
                pend[0]()
                pend[0] = None
            oX, koX = oXr.next()

            def xscores(h, par, qx=qx, kqx=kqx):
                for mt_ in range(2):
                    b = 2 * par + mt_
                    for j in range(2):
                        P.mm(bank[b], Kx[:, 2 * h + j, mt_ * 128:(mt_ + 1) * 128], qx[:, 2 * h + j, :], j == 0, j == 1,
                             ["Kx", (kqx, 2 * h + j)], bk(b))

            xscores(0, 0)
            for h in range(4):
                par = h % 2
                if h + 1 < 4:
                    xscores(h + 1, 1 - par)
                pms = []
                for mt_ in range(2):
                    b = 2 * par + mt_
                    pm, kpm = pmr.next()
                    P.op("act", lambda e, b=b, pm=pm: e.activation(out=pm, in_=bank[b], func=AF.Exp), r=[bk(b)], w=[kpm])
                    pms.append((pm, kpm))
                for j in range(2):
                    b = 4 + j
                    for mt_ in range(2):
                        P.mm(bank[b], Vx[:, mt_, (2 * h + j) * 128:(2 * h + j + 1) * 128], pms[mt_][0], mt_ == 0, mt_ == 1,
                             ["Vx", pms[mt_][1]], bk(b))
                for mt_ in range(2):
                    P.mm(bank[6], ones, pms[mt_][0], mt_ == 0, mt_ == 1, ["ones", pms[mt_][1]], bk(6))
                rx, krx = rr.next()
                smc, ksmc = smr.next()
                P.op("act", lambda e, smc=smc: e.activation(out=smc, in_=bank[6], func=AF.Ln), r=[bk(6)], w=[ksmc])
                P.op("act", lambda e, rx=rx, smc=smc: e.activation(out=rx, in_=smc, func=AF.Exp, scale=-1.0), r=[ksmc], w=[krx])
                for j in range(2):
                    P.op("dve", lambda e, j=j, h=h, rx=rx, oX=oX: e.tensor_tensor(out=oX[:, 2 * h + j, :], in0=bank[4 + j], in1=rx, op=ALU.mult),
                         r=[bk(4 + j), krx], w=[koX])
            pend[0] = epilogue(E, t0, proj_psfn(oX, koX, wo, "wo"), xin, x_out, kxo, hT_out, kho, None, [(0, 1), (2, 3)], 7)
        if pend[0] is not None:
            pend[0]()
        P.barrier()

    def phase_E(S, li, hT_in, khi, xin, kxi, x_out, kxo, hT_out, kho, y_out, g_row):
        P.reset()
        E = EpiBufs(nx=1, nh=1)
        load_gB(E.gB, g_row, "gB")
        wdn = P.alloc("wdn", [NFC, D], BF16)
        P.dma("sp", wdn, wdn_b[li], "wdn", r=[("wb", "wdn%d" % li)], w=["wdn"])
        cw = P.alloc("cw", [3, NFC], F32)
        cb = P.alloc("cb", [NFC], F32)
        for j in range(3):
            P.dma("sp", cw[:, j, :], W["ffn_conv_w"][li, j, :].rearrange("(f p) -> p f", p=128), "cw", w=["cw"], slow=True)
        P.dma("sp", cb, W["ffn_conv_b"][li, :].rearrange("(f p) -> p f", p=128), "cw", w=["cw"], slow=True)
        wur = Ring([(P.alloc("wu%d" % i, [2, 8, 256], BF16), "wu%d" % i) for i in range(3)])
        hw = Ring([(P.alloc("hTh%d" % i, [8, 514], BF16), "hTh%d" % i) for i in range(2)])
        u = P.alloc("u", [NFC, 512], BF16)
        gbr = Ring([(P.alloc("gb%d" % i, [514], F32), "gb%d" % i) for i in range(2)])
        c1r = Ring([(P.alloc("c1_%d" % i, [512], F32), "c1_%d" % i) for i in range(2)])
        c2r = Ring([(P.alloc("c2_%d" % i, [512], F32), "c2_%d" % i) for i in range(2)])
        c3r = Ring([(P.alloc("c3_%d" % i, [512], F32), "c3_%d" % i) for i in range(2)])
        ger = Ring([(P.alloc("ge%d" % i, [512], F32), "ge%d" % i) for i in range(2)])
        bi = [0]
        pend = [None]
        for t0 in range(0, S, 512):
            hTh, khh = hw.next()
            P.dma("sp", hTh, hT_in.rearrange("(c p) s -> p c s", p=128)[:, :, t0:t0 + 514], khh, r=[khi], w=[khh])
            for g in range(11):
                if g == 2 and pend[0] is not None:
                    pend[0]()
                    pend[0] = None
                wu, kwu = wur.next()
                P.dma("sp", wu[:, 0], wup_b[li][g], kwu, r=[("wb", "wup%d" % li)], w=[kwu])
                P.dma("sp", wu[:, 1], wup_b[li][11 + g], kwu, r=[("wb", "wup%d" % li)], w=[kwu])
                for j in range(2):
                    fc = 2 * g + j
                    par = bi[0] % 2
                    bi[0] += 1
                    bG, bV, bH = 2 * par, 2 * par + 1, 4 + par
                    for c in range(8):
                        P.mm(bank[bG], wu[:, 0, c, j * 128:(j + 1) * 128], hTh[:, c, 1:513], c == 0, c == 7, [kwu, khh], bk(bG))
                    for c in range(8):
                        P.mm(bank[bH][:, 0:2], wu[:, 0, c, j * 128:(j + 1) * 128], hTh[:, c, 0:514:513], c == 0, c == 7, [kwu, khh], bk(bH))
                    for c in range(8):
                        P.mm(bank[bV], wu[:, 1, c, j * 128:(j + 1) * 128], hTh[:, c, 1:513], c == 0, c == 7, [kwu, khh], bk(bV))
                    gb, kgb = gbr.next()
                    P.op("act", lambda e, gb=gb, bG=bG: e.activation(out=gb[:, 1:513], in_=bank[bG], func=AF.Copy), r=[bk(bG)], w=[(kgb, 0)])
                    P.op("dve", lambda e, gb=gb, bH=bH: e.tensor_copy(gb[:, 0:514:513], bank[bH][:, 0:2]), r=[bk(bH)], w=[(kgb, 1)])
                    c1, kc1 = c1r.next()
                    c2, kc2 = c2r.next()
                    c3, kc3 = c3r.next()
                    ge, kge = ger.next()
                    P.op("dve", lambda e, gb=gb, c1=c1, fc=fc: e.tensor_scalar(c1, gb[:, 0:512], cw[:, 0, fc:fc + 1], cb[:, fc:fc + 1],
                                                                               op0=ALU.mult, op1=ALU.add),
                         r=[(kgb, 0), (kgb, 1), "cw"], w=[kc1])
                    P.op("dve", lambda e, gb=gb, c1=c1, c2=c2, fc=fc: e.scalar_tensor_tensor(out=c2, in0=gb[:, 1:513], scalar=cw[:, 1, fc:fc + 1], in1=c1,
                                                                                              op0=ALU.mult, op1=ALU.add),
                         r=[(kgb, 0), (kgb, 1), "cw", kc1], w=[kc2])
                    P.op("dve", lambda e, gb=gb, c2=c2, c3=c3, fc=fc: e.scalar_tensor_tensor(out=c3, in0=gb[:, 2:514], scalar=cw[:, 2, fc:fc + 1], in1=c2,
                                                                                              op0=ALU.mult, op1=ALU.add),
                         r=[(kgb, 0), (kgb, 1), "cw", kc2], w=[kc3])
                    P.op("act", lambda e, c3=c3, ge=ge: e.activation(out=ge, in_=c3, func=AF.Gelu), r=[kc3], w=[kge])
                    P.op("dve", lambda e, ge=ge, bV=bV, fc=fc: e.tensor_tensor(out=u[:, fc, :], in0=bank[bV], in1=ge, op=ALU.mult),
                         r=[bk(bV), kge], w=[("u", fc)])

            def ps_fn(st, pair):
                for half in range(2):
                    b = pair[half]
                    for kc in range(NFC):
                        P.mm(bank[b], u[:, kc, st * 128:(st + 1) * 128], wdn[:, kc, half * 512:(half + 1) * 512],
                             kc == 0, kc == NFC - 1, [("u", kc), "wdn"], bk(b))
                return pair

            pend[0] = epilogue(E, t0, ps_fn, xin, x_out, kxo, hT_out, kho, y_out, [(0, 1), (2, 3)], [6, 7])
        if pend[0] is not None:
            pend[0]()
        P.barrier()

    def phase_F1(S, hT_in, khi):
        P.reset()
        win = P.alloc("win", [8, D], BF16)
        cc = P.alloc("cc", [2, 256], BF16)
        sc = P.alloc("sc", [2, 256], BF16)
        P.dma("sp", win, fin_b, "win", r=[("wb", "fin")], w=["win"])
        P.dma("sp", cc, c_cc[S].rearrange("(j p) n -> p j n", p=128), "cc", w=["cc"])
        P.dma("sp", sc, c_sc[S].rearrange("(j p) n -> p j n", p=128), "sc", w=["sc"])
        hw = Ring([(P.alloc("hTw%d" % i, [8, 512], BF16), "hTw%d" % i) for i in range(2)])
        uTr = Ring([(P.alloc("uT%d" % i, [8, 512], BF16), "uT%d" % i) for i in range(2)])
        Asr = Ring([(P.alloc("As%d" % i, [4, D], BF16), "As%d" % i) for i in range(2)])
        Bsr = Ring([(P.alloc("Bs%d" % i, [4, D], BF16), "Bs%d" % i) for i in range(2)])
        for t0 in range(0, S, 512):
            hTw, khw = hw.next()
            P.dma("sp", hTw, hT_window(hT_in, t0), khw, r=[khi], w=[khw])
            uT, kuT = uTr.next()
            for fo in range(8):
                b = fo % 2
                for c in range(8):
                    P.mm(bank[b], win[:, c, fo * 128:(fo + 1) * 128], hTw[:, c, :], c == 0, c == 7, ["win", khw], bk(b))
                if fo % 2 == 0:
                    P.op("act", lambda e, b=b, fo=fo, uT=uT: e.activation(out=uT[:, fo, :], in_=bank[b], func=AF.Copy), r=[bk(b)], w=[(kuT, fo)])
                else:
                    P.op("dve", lambda e, b=b, fo=fo, uT=uT: e.tensor_copy(uT[:, fo, :], bank[b]), r=[bk(b)], w=[(kuT, fo)])
            As, kAs = Asr.next()
            Bs, kBs = Bsr.next()
            for st in range(4):
                for (tab, ktab, dst, kd, b0) in ((cc, "cc", As, kAs, 2), (sc, "sc", Bs, kBs, 4)):
                    for g in range(4):
                        b = b0 + g // 2
                        o = (g % 2) * 256
                        for j in range(2):
                            P.mm(bank[b][:, o:o + 256], uT[:, 2 * g + j, st * 128:(st + 1) * 128], tab[:, j, :], j == 0, j == 1,
                                 [(kuT, 2 * g + j), ktab], bk(b))
                    P.op("act", lambda e, b0=b0, dst=dst, st=st: e.activation(out=dst[:, st, 0:512], in_=bank[b0], func=AF.Copy), r=[bk(b0)], w=[kd])
                    P.op("dve", lambda e, b0=b0, dst=dst, st=st: e.tensor_copy(dst[:, st, 512:1024], bank[b0 + 1]), r=[bk(b0 + 1)], w=[kd])
            P.dma("pool", Ad[t0:t0 + 512, :].rearrange("(s p) d -> p s d", p=128), As, kAs + "s", r=[kAs], w=["Ad"])
            P.dma("pool", Bd[t0:t0 + 512, :].rearrange("(s p) d -> p s d", p=128), Bs, kBs + "s", r=[kBs], w=["Bd"])
        P.barrier()

    def phase_F2(S):
        P.reset()
        NKT = S // 128
        rs_ = TAB // S
        G = 4
        Ah = P.alloc("Ah", [NKT, 512], BF16)
        Bh = P.alloc("Bh", [NKT, 512], BF16)
        tr = Ring([((P.alloc("tC%d" % i, [G, 512], BF16), P.alloc("tN%d" % i, [G, 512], BF16)), "tab%d" % i) for i in range(4)])
        fr = Ring([(P.alloc("fs%d" % i, [4, 512], BF16), "fs%d" % i) for i in range(2)])
        CSv = c_CS.rearrange("(k p r) n -> p k r n", p=128, r=rs_)
        NSv = c_NS.rearrange("(k p r) n -> p k r n", p=128, r=rs_)
        pb = [0]
        for half in range(2):
            P.dma("sp", Ah, Ad[0:S, half * 512:(half + 1) * 512].rearrange("(k p) d -> p k d", p=128), "Ah", r=["Ad"], w=["Ah"])
            P.dma("sp", Bh, Bd[0:S, half * 512:(half + 1) * 512].rearrange("(k p) d -> p k d", p=128), "Bh", r=["Bd"], w=["Bh"])
            for s0 in range(0, S, 512):
                base = 4 * (pb[0] % 2)
                pb[0] += 1
                for kg in range(0, NKT, G):
                    (tC, tN), ktab = tr.next()
                    P.dma("sp", tC, CSv[:, kg:kg + G, 0, s0:s0 + 512], ktab, w=[ktab])
                    P.dma("sp", tN, NSv[:, kg:kg + G, 0, s0:s0 + 512], ktab, w=[ktab])
                    for cc_ in range(4):
                        b = base + cc_
                        for k in range(G):
                            kt = kg + k
                            P.mm(bank[b], Ah[:, kt, cc_ * 128:(cc_ + 1) * 128], tC[:, k, :], kt == 0, False, ["Ah", ktab], bk(b))
                            P.mm(bank[b], Bh[:, kt, cc_ * 128:(cc_ + 1) * 128], tN[:, k, :], False, kt == NKT - 1, ["Bh", ktab], bk(b))
                fs, kfs = fr.next()
                for cc_ in range(4):
                    b = base + cc_
                    if cc_ % 2 == 0:
                        P.op("act", lambda e, b=b, cc_=cc_, fs=fs: e.activation(out=fs[:, cc_, :], in_=bank[b], func=AF.Copy), r=[bk(b)], w=[kfs])
                    else:
                        P.op("dve", lambda e, b=b, cc_=cc_, fs=fs: e.tensor_copy(fs[:, cc_, :], bank[b]), r=[bk(b)], w=[kfs])
                P.dma("pool", fTd.rearrange("(c p) s -> p c s", p=128)[:, half * 4:(half + 1) * 4, s0:s0 + 512], fs, kfs + "s",
                      r=[kfs], w=["fTd"])
        P.barrier()

    def phase_F3(S, xin, kxi, x_out, kxo, hT_out, kho):
        P.reset()
        E = EpiBufs()
        load_gB(E.gB, W["norm_xattn_g"][1:2, :], "gB")
        wout = P.alloc("wout", [8, D], BF16)
        P.dma("sp", wout, fout_b, "wout", r=[("wb", "fout")], w=["wout"])
        fw = Ring([(P.alloc("fTw%d" % i, [8, 512], BF16), "fTw%d" % i) for i in range(2)])
        pend = [None]
        for t0 in range(0, S, 512):
            fTw, kfw = fw.next()
            P.dma("sp", fTw, fTd.rearrange("(c p) s -> p c s", p=128)[:, :, t0:t0 + 512], kfw, r=["fTd"], w=[kfw])
            p2 = epilogue(E, t0, proj_psfn(fTw, kfw, wout, "wout"), xin, x_out, kxo, hT_out, kho, None, [(0, 1), (2, 3)], [4, 5, 6, 7])
            if pend[0] is not None:
                pend[0]()
            pend[0] = p2
        if pend[0] is not None:
            pend[0]()
        P.barrier()

    phase_lambda()
    if stop_after != "L":
        phase_weights()
    for sj, S in enumerate(seqs):
        if stop_after in ("L", "W", "M"):
            break
        xin = x[offs[sj]:offs[sj] + S, :]
        yout = y[offs[sj]:offs[sj] + S, :]
        for hTx, kh in ((hTa, "hTa"), (hTb, "hTb")):
            v = hTx.rearrange("(c p) s -> p c s", p=128)
            P.dma("pool", v[:, :, 0:1], zcol, "zc", r=["zcol"], w=[kh], slow=True)
            P.dma("pool", v[:, :, S + 1:S + 2], zcol, "zc", r=["zcol"], w=[kh], slow=True)
        P.barrier()
        phase_A(xin, S, hTa, "hTa")
        if stop_after == "A":
            break
        phase_B(S, hTa, "hTa")
        if stop_after == "B":
            break
        phase_C(S, xin, xa, "xa", hTb, "hTb")
        if sj == 0:
            phase_mem()
        if stop_after == "C":
            break
        phase_D(S, 0, sj, hTb, "hTb", xa, "xa", xb, "xb", hTa, "hTa")
        if stop_after == "D0":
            break
        phase_E(S, 0, hTa, "hTa", xb, "xb", xa, "xa", hTb, "hTb", None, W["norm_mix_g"][1:2, :])
        if stop_after == "E0":
            break
        phase_F1(S, hTb, "hTb")
        phase_F2(S)
        if stop_after == "F2":
            break
        phase_F3(S, xa, "xa", xb, "xb", hTa, "hTa")
        if stop_after == "F3":
            break
        phase_D(S, 1, sj, hTa, "hTa", xb, "xb", xa, "xa", hTb, "hTb")
        if stop_after == "D1":
            break
        phase_E(S, 1, hTb, "hTb", xa, "xa", None, None, None, None, yout, W["final_norm_g"][0:1, :])
    P.final_wait("pool")
    P.final_wait("sp")
    P.emit()
    stats = P.stats
    P.es.close()
    return nc, stats


def make_consts(seqs, TAB):
    c = {}
    c["c_ident"] = np.eye(128, dtype=np.float32).astype(NPBF)
    c["c_ones"] = np.ones((128, 128), dtype=np.float32).astype(NPBF)
    rot = np.zeros((128, 128), dtype=np.float32)
    for p in range(128):
        blk = (p % 64) // 32
        if blk == 0:
            rot[p + 32, p] = -1.0
        else:
            rot[p - 32, p] = 1.0
    c["c_rot"] = rot.astype(NPBF)
    half = 32
    inv_freq = (10000.0 ** (-(np.arange(0, half, dtype=np.float32)) * 2.0 / 64)).astype(np.float32)
    ang = np.arange(TAB, dtype=np.float32)[:, None] * inv_freq[None, :]
    cos = np.cos(ang).astype(np.float32).T
    sin = np.sin(ang).astype(np.float32).T
    c["c_cos"] = np.ascontiguousarray(np.tile(cos, (4, 1)))
    c["c_sin"] = np.ascontiguousarray(np.tile(sin, (4, 1)))
    k = np.arange(256, dtype=np.int64)
    a256 = 2.0 * np.pi * ((k[:, None] * k[None, :]) % 256) / 256.0
    for S in sorted(set(seqs)):
        scl = 1.0 / math.sqrt(256.0 * S)
        c["c_cc%d" % S] = (np.cos(a256) * scl).astype(np.float32).astype(NPBF)
        c["c_sc%d" % S] = (np.sin(a256) * scl).astype(np.float32).astype(NPBF)
    s = np.arange(TAB, dtype=np.int64)
    aS = (2.0 * np.pi / TAB) * ((s[:, None] * s[None, :]) % TAB).astype(np.float64)
    c["c_CS"] = np.cos(aS).astype(np.float32).astype(NPBF)
    c["c_NS"] = (-np.sin(aS)).astype(np.float32).astype(NPBF)
    return c


_CACHE = {}


def run_cores(xs, mems, weights, seqs, TAB, dbg=False, stop_after=None, ncores=None):
    key = (tuple(seqs), TAB, dbg, stop_after)
    if key not in _CACHE:
        _CACHE[key] = (build(seqs, TAB, dbg=dbg, stop_after=stop_after), make_consts(seqs, TAB))
    (nc, stats), consts = _CACHE[key]
    n = len(xs)
    wd = {}
    for nme, shp in WNAMES:
        wd[nme] = np.ascontiguousarray(np.asarray(weights[nme], dtype=np.float32).reshape(shp))
    in_maps = []
    for i in range(n):
        m = {"x": np.ascontiguousarray(xs[i], dtype=np.float32), "mem": np.ascontiguousarray(mems[i], dtype=np.float32)}
        m.update(wd)
        m.update(consts)
        in_maps.append(m)
    res = run_bass_kernel_spmd(nc, in_maps, core_ids=list(range(n)))
    return res.results


def kernel(x_prompt, x_sample, mem_prompt, mem_sample, **weights):
    x_prompt = np.asarray(x_prompt, dtype=np.float32)
    x_sample = np.asarray(x_sample, dtype=np.float32)
    mem_prompt = np.asarray(mem_prompt, dtype=np.float32)
    mem_sample = np.asarray(mem_sample, dtype=np.float32)
    n = 8
    SP, SS = x_prompt.shape[1], x_sample.shape[1]
    seqs = [SP, SP, SS]
    xs, mems = [], []
    for i in range(n):
        xs.append(np.concatenate([x_prompt[2 * i], x_prompt[2 * i + 1], x_sample[i]], axis=0))
        mems.append(np.concatenate([mem_prompt[2 * i], mem_prompt[2 * i + 1], mem_sample[i]], axis=0))
    res = run_cores(xs, mems, weights, seqs, max(seqs))
    y_prompt = np.empty_like(x_prompt)
    y_sample = np.empty_like(x_sample)
    for i in range(n):
        yy = res[i]["y"]
        y_prompt[2 * i] = yy[0:SP]
        y_prompt[2 * i + 1] = yy[SP:2 * SP]
        y_sample[i] = yy[2 * SP:2 * SP + SS]
    return (y_prompt, y_sample)
```
